# Optimizing a Trainium2 kernel written in Bass

```python
import math
import jax
import jax.numpy as jnp
from jax import lax
import numpy as np

D_MODEL = 1024
BATCH = 8
SEQ = 2048
DEPTH = 4
DEC_BATCH = 32
DEC_SEQ = 32
PAST_LEN = 2048

CHUNK = 64
W_A = D_MODEL // 4
W_B = D_MODEL // 4
W_C = D_MODEL // 4
W_D = D_MODEL // 4
MIX_WIDTH = W_A + W_B + W_C + W_D
POOL_WINDOWS = (2, 4, 8, 16)
N_POOL = 4
POOL_GW = W_A // N_POOL
POOL_HIST = 15
H_B = 4
DH_B = W_B // H_B
BAND_CHUNKS = 8
ATTN_WINDOW = BAND_CHUNKS * CHUNK
REL_CLIP = 256
H_C = 4
DK_C = W_C // H_C
DV_C = W_C // H_C
CONV_W = 4
QKV_C = 3 * W_C
H_D = 4
DK_D = W_D // (2 * H_D)
DV_D = W_D // H_D
WK_D = H_D * DK_D
GLA_RANK = 16
GLA_GATE_NORM = 16.0
D_FF = 4 * D_MODEL
EPS = 1e-6
NEG_INF = -1e30
PROJ_SIZES = (W_A, 3 * W_B, QKV_C, W_C, H_C, H_C, WK_D, WK_D, W_D, W_D, GLA_RANK)
IN_COLS = W_A + 3 * W_B + QKV_C + W_C + 2 * H_C + 2 * WK_D + 2 * W_D + GLA_RANK

kernel_name = 'hybrid_streaming_encoder_step'


def _rmsnorm(x, g):
    xf = x.astype(jnp.float32)
    y = xf * lax.rsqrt(jnp.mean(xf * xf, axis=-1, keepdims=True) + EPS)
    return (y * g.astype(jnp.float32)).astype(x.dtype)


def _l2norm(x):
    return x * lax.rsqrt(jnp.sum(x * x, axis=-1, keepdims=True) + EPS)


def _split_cols(p):
    parts, off = [], 0
    for n in PROJ_SIZES:
        parts.append(p[..., off:off + n])
        off += n
    return parts


def _rel_bias(table, rel):
    idx = jnp.clip(rel, -REL_CLIP, REL_CLIP) + REL_CLIP
    return table.astype(jnp.float32)[:, idx]


def _to_chunks(t, cs):
    b, n = t.shape[0], t.shape[1] // cs
    t = t.reshape((b, n, cs) + t.shape[2:])
    return jnp.transpose(t, (1, 0, 3, 2) + tuple(range(4, t.ndim)))


def _from_chunks(t):
    n, b, h, cs = t.shape[:4]
    t = jnp.transpose(t, (1, 0, 3, 2) + tuple(range(4, t.ndim)))
    return t.reshape((b, n * cs, h) + t.shape[4:])


def _pool_mixer(u, hist, pos0, w_pool, scale):
    b, t = u.shape[:2]
    ext = jnp.concatenate([hist.astype(u.dtype), u], axis=1)
    ef = ext.astype(jnp.float32)
    c0 = jnp.concatenate([jnp.zeros((b, 1, W_A), jnp.float32), jnp.cumsum(ef, axis=1)], axis=1)
    pos = pos0 + jnp.arange(t)
    groups = []
    for gi, win in enumerate(POOL_WINDOWS):
        lo, hi = gi * POOL_GW, (gi + 1) * POOL_GW
        wsum = (c0[:, POOL_HIST + 1:POOL_HIST + 1 + t, lo:hi]
                - c0[:, POOL_HIST + 1 - win:POOL_HIST + 1 - win + t, lo:hi])
        cnt = jnp.minimum(win, pos + 1).astype(jnp.float32)[None, :, None]
        groups.append(wsum / cnt - ef[:, POOL_HIST:, lo:hi])
    pooled = jnp.stack(groups, axis=2)
    y = jnp.einsum('btgc,gcd->btgd', pooled, w_pool.astype(jnp.float32)).reshape(b, t, W_A)
    return (y * scale.astype(jnp.float32)).astype(u.dtype), ext[:, -POOL_HIST:]


def _band_attention_prompt(q, k, v, table):
    b, t = q.shape[:2]
    nc = t // CHUNK
    nb = BAND_CHUNKS + 1
    qc, kc, vc = (a.astype(jnp.float32).reshape(b, nc, CHUNK, H_B, DH_B) for a in (q, k, v))
    pad = ((0, 0), (BAND_CHUNKS, 0), (0, 0), (0, 0), (0, 0))
    kp, vp = jnp.pad(kc, pad), jnp.pad(vc, pad)
    kband = jnp.concatenate([kp[:, j:j + nc] for j in range(nb)], axis=2)
    vband = jnp.concatenate([vp[:, j:j + nc] for j in range(nb)], axis=2)
    qi = jnp.arange(CHUNK)
    kj = jnp.arange(nb * CHUNK)
    bias = _rel_bias(table, kj[None, :] - BAND_CHUNKS * CHUNK - qi[:, None])
    kpos = (jnp.arange(nc)[:, None] - BAND_CHUNKS) * CHUNK + kj[None, :]
    s = jnp.einsum('bnqhd,bnkhd->bnhqk', qc, kband) * (DH_B ** -0.5) + bias[None, None]
    s = jnp.where((kpos >= 0)[None, :, None, None, :], s, NEG_INF)
    p = jax.nn.softmax(s, axis=-1)
    o = jnp.einsum('bnhqk,bnkhd->bnqhd', p, vband)
    return o.reshape(b, t, W_B).astype(q.dtype)


def _attention_sample(q, k, v, k_cache, v_cache, table):
    b, t = q.shape[:2]
    n_cache = k_cache.shape[2]
    kall = jnp.concatenate([k_cache.astype(jnp.float32), jnp.transpose(k, (0, 2, 1, 3)).astype(jnp.float32)], axis=2)
    vall = jnp.concatenate([v_cache.astype(jnp.float32), jnp.transpose(v, (0, 2, 1, 3)).astype(jnp.float32)], axis=2)
    bias = _rel_bias(table, jnp.arange(n_cache + t)[None, :] - n_cache - jnp.arange(t)[:, None])
    s = jnp.einsum('bqhd,bhkd->bhqk', q.astype(jnp.float32), kall) * (DH_B ** -0.5) + bias[None]
    p = jax.nn.softmax(s, axis=-1)
    o = jnp.einsum('bhqk,bhkd->bqhd', p, vall)
    return o.reshape(b, t, W_B).astype(q.dtype)


def _gated_delta_chunked(q, k, v, g, beta, s0):
    t = q.shape[1]
    cs = min(CHUNK, t)
    xs = tuple(_to_chunks(a, cs) for a in (q, k, v, g, beta))
    incl = jnp.tril(jnp.ones((cs, cs), dtype=bool))
    strict = jnp.tril(jnp.ones((cs, cs), dtype=bool), -1)
    eye = jnp.eye(cs, dtype=jnp.float32)

    def step(S, inp):
        qc, kc, vc, gc, bc = inp
        dv = vc.shape[-1]
        G = jnp.cumsum(gc, axis=-1)
        diff = G[..., :, None] - G[..., None, :]
        dec = jnp.where(incl, jnp.exp(jnp.where(incl, diff, 0.0)), 0.0)
        A = jnp.where(strict, bc[..., :, None] * jnp.einsum('bhid,bhjd->bhij', kc, kc) * dec, 0.0)
        rhs = jnp.concatenate([bc[..., None] * vc, (bc * jnp.exp(G))[..., None] * kc], axis=-1)
        sol = lax.linalg.triangular_solve(A + eye, rhs, left_side=True, lower=True, unit_diagonal=True)
        u, w = sol[..., :dv], sol[..., dv:]
        un = u - jnp.einsum('bhik,bhkv->bhiv', w, S)
        qk = jnp.einsum('bhik,bhjk->bhij', qc, kc) * dec
        o = (jnp.einsum('bhik,bhkv->bhiv', qc * jnp.exp(G)[..., None], S)
             + jnp.einsum('bhij,bhjv->bhiv', qk, un))
        gl = G[..., -1:]
        S = (jnp.exp(gl)[..., None] * S
             + jnp.einsum('bhjk,bhjv->bhkv', kc * jnp.exp(gl - G)[..., None], un))
        return S, o

    s_fin, o = lax.scan(step, s0, xs)
    return _from_chunks(o), s_fin


def _gla_chunked(q, k, v, gk, s0):
    t = q.shape[1]
    cs = min(CHUNK, t)
    xs = tuple(_to_chunks(a, cs) for a in (q, k, v, gk))
    incl = jnp.tril(jnp.ones((cs, cs), dtype=bool))[:, :, None]

    def step(S, inp):
        qc, kc, vc, gc = inp
        G = jnp.cumsum(gc, axis=2)
        diff = G[:, :, :, None, :] - G[:, :, None, :, :]
        dec = jnp.where(incl, jnp.exp(jnp.where(incl, diff, 0.0)), 0.0)
        att = jnp.einsum('bhik,bhjk,bhijk->bhij', qc, kc, dec)
        o = (jnp.einsum('bhik,bhkv->bhiv', qc * jnp.exp(G), S)
             + jnp.einsum('bhij,bhjv->bhiv', att, vc))
        gl = G[:, :, -1:, :]
        S = (jnp.exp(gl[:, :, 0, :])[..., None] * S
             + jnp.einsum('bhjk,bhjv->bhkv', kc * jnp.exp(gl - G), vc))
        return S, o

    s_fin, o = lax.scan(step, s0, xs)
    return _from_chunks(o), s_fin


def _delta_mixer(qkv, z, a, bb, conv_hist, s0, conv_w, a_log, dt_bias, norm_g):
    b, t = qkv.shape[:2]
    ext = jnp.concatenate([conv_hist.astype(qkv.dtype), qkv], axis=1)
    acc = ext[:, 0:t] * conv_w[0]
    for j in range(1, CONV_W):
        acc = acc + ext[:, j:j + t] * conv_w[j]
    c = jax.nn.silu(acc.astype(jnp.float32))
    q = _l2norm(c[..., :W_C].reshape(b, t, H_C, DK_C)) * (DK_C ** -0.5)
    k = _l2norm(c[..., W_C:2 * W_C].reshape(b, t, H_C, DK_C))
    v = c[..., 2 * W_C:].reshape(b, t, H_C, DV_C)
    beta = jax.nn.sigmoid(bb.astype(jnp.float32))
    g = -jnp.exp(a_log.astype(jnp.float32)) * jax.nn.softplus(a.astype(jnp.float32) + dt_bias.astype(jnp.float32))
    o, s_new = _gated_delta_chunked(q, k, v, g, beta, s0.astype(jnp.float32))
    o = _rmsnorm(o, norm_g) * jax.nn.silu(z.astype(jnp.float32).reshape(b, t, H_C, DV_C))
    return o.reshape(b, t, W_C).astype(qkv.dtype), ext[:, -(CONV_W - 1):], s_new


def _gla_mixer(q, k, v, gate, gk_low, s0, w_gk, b_gk, norm_g):
    b, t = q.shape[:2]
    gk = jax.nn.log_sigmoid(jnp.einsum('btr,rk->btk', gk_low.astype(jnp.float32), w_gk.astype(jnp.float32))
                            + b_gk.astype(jnp.float32)) / GLA_GATE_NORM
    qh = q.astype(jnp.float32).reshape(b, t, H_D, DK_D) * (DK_D ** -0.5)
    kh = k.astype(jnp.float32).reshape(b, t, H_D, DK_D)
    vh = v.astype(jnp.float32).reshape(b, t, H_D, DV_D)
    o, s_new = _gla_chunked(qh, kh, vh, gk.reshape(b, t, H_D, DK_D), s0.astype(jnp.float32))
    o = _rmsnorm(o, norm_g) * jax.nn.silu(gate.astype(jnp.float32).reshape(b, t, H_D, DV_D))
    return o.reshape(b, t, W_D).astype(q.dtype), s_new


def _trunk(x, pos0, pool_h, kv_h, conv_h, sd_h, sg_h, w):
    (attn_norm_g, w_in, pool_w, pool_scale, rel_bias, conv_w, a_log, dt_bias, delta_norm_g,
     gla_w_gk, gla_b_gk, gla_norm_g, w_out, mlp_norm_g, w_up, w_down, final_norm_g) = w
    b, t = x.shape[:2]
    new_pool, new_k, new_v, new_conv, new_sd, new_sg = [], [], [], [], [], []
    for l in range(DEPTH):
        h = _rmsnorm(x, attn_norm_g[l])
        (u_a, qkv_b, qkv_c, z_c, a_c, b_c, q_d, k_d, v_d, g_d, gk_d) = _split_cols(
            jnp.einsum('btd,dc->btc', h, w_in[l]))
        y_a, st_pool = _pool_mixer(u_a, pool_h[l], pos0, pool_w[l], pool_scale[l])
        q_b, k_b, v_b = (qkv_b[..., i * W_B:(i + 1) * W_B].reshape(b, t, H_B, DH_B) for i in range(3))
        if kv_h is None:
            y_b = _band_attention_prompt(q_b, k_b, v_b, rel_bias[l])
            keep = min(ATTN_WINDOW, t)
            k_rows, v_rows = k_b[:, t - keep:], v_b[:, t - keep:]
        else:
            y_b = _attention_sample(q_b, k_b, v_b, kv_h[0][l], kv_h[1][l], rel_bias[l])
            k_rows, v_rows = k_b, v_b
        y_c, st_conv, st_d = _delta_mixer(qkv_c, z_c, a_c, b_c, conv_h[l], sd_h[l], conv_w[l],
                                          a_log[l], dt_bias[l], delta_norm_g[l])
        y_d, st_g = _gla_mixer(q_d, k_d, v_d, g_d, gk_d, sg_h[l], gla_w_gk[l], gla_b_gk[l], gla_norm_g[l])
        mix = jnp.concatenate([y_a, y_b, y_c, y_d], axis=-1)
        x = x + jnp.einsum('btc,cd->btd', mix, w_out[l])
        h2 = _rmsnorm(x, mlp_norm_g[l])
        up = jnp.square(jax.nn.relu(jnp.einsum('btd,df->btf', h2, w_up[l])))
        x = x + jnp.einsum('btf,fd->btd', up, w_down[l])
        new_pool.append(st_pool.astype(x.dtype))
        new_k.append(jnp.transpose(k_rows, (0, 2, 1, 3)))
        new_v.append(jnp.transpose(v_rows, (0, 2, 1, 3)))
        new_conv.append(st_conv.astype(x.dtype))
        new_sd.append(st_d.astype(x.dtype))
        new_sg.append(st_g.astype(x.dtype))
    y = _rmsnorm(x, final_norm_g)
    return (y, jnp.stack(new_pool), jnp.stack(new_k), jnp.stack(new_v),
            jnp.stack(new_conv), jnp.stack(new_sd), jnp.stack(new_sg))


def setup_inputs(seed: int = 0) -> dict:
    key = jax.random.key(seed)
    ks = jax.random.split(key, 32)
    f32 = jnp.float32

    def nrm(i, shape, scale):
        return scale * jax.random.normal(ks[i], shape, f32)

    l_b = min(ATTN_WINDOW, PAST_LEN)
    dt = jnp.exp(jax.random.uniform(ks[20], (DEPTH, H_C), f32, math.log(1e-3), math.log(1e-1)))
    return {
        'x_prompt': nrm(0, (BATCH, SEQ, D_MODEL), 1.0),
        'x_sample': nrm(1, (DEC_BATCH, DEC_SEQ, D_MODEL), 1.0),
        'cache_pool': nrm(2, (DEPTH, DEC_BATCH, POOL_HIST, W_A), 1.0),
        'cache_attn_k': nrm(3, (DEPTH, DEC_BATCH, H_B, l_b, DH_B), 1.0),
        'cache_attn_v': nrm(4, (DEPTH, DEC_BATCH, H_B, l_b, DH_B), 1.0),
        'state_conv': nrm(5, (DEPTH, DEC_BATCH, CONV_W - 1, QKV_C), 1.0),
        'state_delta': nrm(6, (DEPTH, DEC_BATCH, H_C, DK_C, DV_C), DK_C ** -0.5),
        'state_gla': nrm(7, (DEPTH, DEC_BATCH, H_D, DK_D, DV_D), 1.0),
        'attn_norm_g': 1.0 + nrm(8, (DEPTH, D_MODEL), 0.05),
        'w_in': nrm(9, (DEPTH, D_MODEL, IN_COLS), D_MODEL ** -0.5),
        'pool_w': nrm(10, (DEPTH, N_POOL, POOL_GW, POOL_GW), POOL_GW ** -0.5),
        'pool_scale': 1.0 + nrm(11, (DEPTH, W_A), 0.1),
        'rel_bias': nrm(12, (DEPTH, H_B, 2 * REL_CLIP + 1), 0.5),
        'conv_w': nrm(13, (DEPTH, CONV_W, QKV_C), CONV_W ** -0.5),
        'a_log': jnp.log(jax.random.uniform(ks[14], (DEPTH, H_C), f32, 1.0, 16.0)),
        'dt_bias': dt + jnp.log(-jnp.expm1(-dt)),
        'delta_norm_g': 1.0 + nrm(15, (DEPTH, DV_C), 0.05),
        'gla_w_gk': nrm(16, (DEPTH, GLA_RANK, WK_D), GLA_RANK ** -0.5),
        'gla_b_gk': nrm(17, (DEPTH, WK_D), 0.1),
        'gla_norm_g': 1.0 + nrm(18, (DEPTH, DV_D), 0.05),
        'w_out': nrm(19, (DEPTH, MIX_WIDTH, D_MODEL), MIX_WIDTH ** -0.5),
        'mlp_norm_g': 1.0 + nrm(21, (DEPTH, D_MODEL), 0.05),
        'w_up': nrm(22, (DEPTH, D_MODEL, D_FF), D_MODEL ** -0.5),
        'w_down': nrm(23, (DEPTH, D_FF, D_MODEL), 0.5 * D_FF ** -0.5),
        'final_norm_g': 1.0 + nrm(24, (D_MODEL,), 0.05),
    }


def reference(x_prompt, x_sample, cache_pool, cache_attn_k, cache_attn_v, state_conv, state_delta,
              state_gla, attn_norm_g, w_in, pool_w, pool_scale, rel_bias, conv_w, a_log, dt_bias,
              delta_norm_g, gla_w_gk, gla_b_gk, gla_norm_g, w_out, mlp_norm_g, w_up, w_down,
              final_norm_g):
    w = (attn_norm_g, w_in, pool_w, pool_scale, rel_bias, conv_w, a_log, dt_bias, delta_norm_g,
         gla_w_gk, gla_b_gk, gla_norm_g, w_out, mlp_norm_g, w_up, w_down, final_norm_g)
    bp = x_prompt.shape[0]
    dtp = x_prompt.dtype
    y_prompt, pool_p, k_p, v_p, conv_p, delta_p, gla_p = _trunk(
        x_prompt, 0,
        jnp.zeros((DEPTH, bp, POOL_HIST, W_A), dtp), None,
        jnp.zeros((DEPTH, bp, CONV_W - 1, QKV_C), dtp),
        jnp.zeros((DEPTH, bp, H_C, DK_C, DV_C), dtp),
        jnp.zeros((DEPTH, bp, H_D, DK_D, DV_D), dtp), w)
    y_sample, pool_s, k_s, v_s, conv_s, delta_s, gla_s = _trunk(
        x_sample, PAST_LEN, cache_pool, (cache_attn_k, cache_attn_v), state_conv,
        state_delta, state_gla, w)
    return (y_prompt, y_sample, pool_p, k_p, v_p, conv_p, delta_p, gla_p,
            pool_s, k_s, v_s, conv_s, delta_s, gla_s)
```

```python
import contextlib
import numpy as np
import concourse.bass as bass
import concourse.mybir as mybir
from concourse.bass_utils import run_bass_kernel_spmd

F32 = mybir.dt.float32
BF16 = mybir.dt.bfloat16
AF = mybir.ActivationFunctionType
ALU = mybir.AluOpType

NCORES = 8
DEPTH = 4
D = 1024
NT = 2176
NBLK = 17
INC = 2840
EPS = 1e-6
ENGS = ["pe", "act", "dve", "pool", "sp"]
RING = 8
OVW_ = 15400
DBGB = 12


class Op:
    __slots__ = ("eng", "fn", "deps", "sig", "sigval", "dma", "k")


class Prog:
    def __init__(self):
        self.ops = {e: [] for e in ENGS}
        self.last_w = {}
        self.readers = {}
        self.ndma = {e: 0 for e in ENGS}
        self.alias = {}
        self.ov_res = set()
        self.ov_recent = {e: [] for e in ENGS}
        self.fence = []
        self.last_acc = {}
        self.capture = None

    def phase_switch(self):
        f = []
        for e in ENGS:
            f.extend(self.ov_recent[e])
        self.fence = f
        self.ov_recent = {e: [] for e in ENGS}

    def add(self, eng, fn, rd=(), wr=(), dma=False):
        if self.capture is not None:
            self.capture.append((eng, fn, tuple(rd), tuple(wr), dma))
            return None
        op = Op()
        op.eng, op.fn, op.dma, op.sig, op.sigval, op.k = eng, fn, dma, False, 0, 0
        wr = list(wr)
        for r in list(wr):
            wr.extend(self.alias.get(r, ()))
        deps = set()
        for r in rd:
            w = self.last_w.get(r)
            if w is not None:
                deps.add(w)
        for r in wr:
            w = self.last_w.get(r)
            if w is not None:
                deps.add(w)
            for x in self.readers.get(r, ()):
                deps.add(x)
        for r in list(rd) + list(wr):
            if r.startswith("pb"):
                la = self.last_acc.get(r)
                if la is not None and la.eng != eng:
                    deps.add(la)
                self.last_acc[r] = op
        touches_ov = any(r in self.ov_res for r in rd) or any(r in self.ov_res for r in wr)
        if touches_ov:
            deps.update(self.fence)
        for r in rd:
            self.readers.setdefault(r, []).append(op)
        for r in wr:
            self.last_w[r] = op
            self.readers[r] = []
        deps.discard(op)
        op.deps = [d for d in deps if d.dma or d.eng != eng or eng != "pe"]
        if dma:
            op.k = self.ndma[eng]
            self.ndma[eng] += 1
        self.ops[eng].append(op)
        if touches_ov:
            lst = self.ov_recent[eng]
            lst.append(op)
            keep = RING + 1
            if len(lst) > keep:
                nd = [o for o in lst if not o.dma][-1:]
                dd = [o for o in lst if o.dma][-RING:]
                self.ov_recent[eng] = dd + nd
        return op

    def emit(self, nc, stack):
        for e in ENGS:
            for op in self.ops[e]:
                for d in op.deps:
                    d.sig = True
        esem = {e: stack.enter_context(nc.semaphore("es_" + e)) for e in ENGS}
        dsem = {e: [stack.enter_context(nc.semaphore("ds_%s_%d" % (e, i))) for i in range(RING)]
                for e in ENGS if self.ndma[e] > 0}
        for e in ENGS:
            c = 0
            for op in self.ops[e]:
                if op.sig and not op.dma:
                    c += 1
                    op.sigval = c
        block = stack.enter_context(nc.Block())

        def run(e, eng):
            waited = {}

            def wait(key, sem, val):
                if waited.get(key, 0) < val:
                    eng.wait_ge(sem, val)
                    waited[key] = val

            for op in self.ops[e]:
                for d in op.deps:
                    if d.dma:
                        wait(("d", d.eng, d.k % RING), dsem[d.eng][d.k % RING], 16 * (d.k // RING + 1))
                    else:
                        wait(("e", d.eng), esem[d.eng], d.sigval)
                if op.dma:
                    if op.k >= RING:
                        wait(("d", e, op.k % RING), dsem[e][op.k % RING], 16 * (op.k // RING))
                    op.fn(eng).then_inc(dsem[e][op.k % RING], 16)
                else:
                    ins = op.fn(eng)
                    if op.sig:
                        ins.then_inc(esem[e], 1)
            n = self.ndma[e]
            for s in range(min(n, RING)):
                last = ((n - 1 - s) // RING) * RING + s
                wait(("d", e, s), dsem[e][s], 16 * (last // RING + 1))

        @block.tensor
        def _(t):
            run("pe", t)

        @block.scalar
        def _(t):
            run("act", t)

        @block.vector
        def _(t):
            run("dve", t)

        @block.gpsimd
        def _(t):
            run("pool", t)

        @block.sync
        def _(t):
            run("sp", t)


def bc(ap, shape):
    return ap.broadcast_to(list(shape))


class _Stop(Exception):
    pass


def build(NL=DEPTH, DBG=False, STOP=None):
    nc = bass.Bass("TRN2", target_bir_lowering=False)
    P = Prog()
    stack = contextlib.ExitStack()

    def din(name, shape):
        return nc.dram_tensor(name, list(shape), F32, kind="ExternalInput").ap()

    def dout(name, shape):
        return nc.dram_tensor(name, list(shape), F32, kind="ExternalOutput").ap()

    xT_d = din("xT", [D, NT])
    w_in_d = din("w_in", [DEPTH, D, INC])
    w_out_d = din("w_out", [DEPTH, D, D])
    w_up_d = din("w_up", [DEPTH, D, 4096])
    w_dn_d = din("w_down", [DEPTH, 4096, D])
    gvec_d = din("gvec", [128, 72])
    pcol_d = din("pcol", [128, DEPTH, 8])
    convw_d = din("convw", [128, DEPTH, 6, 4])
    hrow_d = din("hrow", [128, DEPTH, 8])
    poolw_d = din("poolw", [DEPTH, 2, 128, 128])
    wgk_d = din("wgk", [16, DEPTH, 128])
    bp_d = din("bp", [DEPTH, 128, 5, 512])
    bsc_d = din("bsc", [DEPTH, 128, 4, 128])
    bsn_d = din("bsn", [DEPTH, 128, 4, 128])
    cst_d = din("cst", [128, 13, 128])
    cm_d = din("cm", [128, 2, 16])
    invc_d = din("invc", [128, 2, 128])
    cpool_d = din("cpool", [DEPTH, 4, 256, 15])
    ck_d = din("ck", [DEPTH, 4, 256, 512])
    cv_d = din("cv", [DEPTH, 4, 4, 512, 64])
    sconv_d = din("sconv", [DEPTH, 4, 768, 3])
    sdel_d = din("sdel", [DEPTH, 4, 64, 4, 64])
    sgla_d = din("sgla", [DEPTH, 4, 4, 32, 64])

    yT_o = dout("yT", [D, NT])
    pool_o = dout("pool_o", [DEPTH, 5, 256, 15])
    k_o = dout("k_o", [DEPTH, 256, 640])
    v_o = dout("v_o", [DEPTH, 640, 256])
    conv_o = dout("conv_o", [DEPTH, 5, 768, 3])
    del_o = dout("del_o", [DEPTH, 5, 64, 4, 64])
    gla_o = dout("gla_o", [DEPTH, 5, 4, 32, 64])
    dbg_o = dout("dbg_o", [2, 128, 8, 128]) if DBG else None
    dbg2_o = dout("dbg2_o", [16, 128, 1024]) if DBG else None

    def sb(name, shape, dt=F32):
        return stack.enter_context(nc.sbuf_tensor("s_" + name, list(shape), dt))

    xT = sb("xT", [128, 8, NT])
    w_in = sb("w_in_sb", [128, 8, INC], BF16)
    w_out = sb("w_out_sb", [128, 8, D], BF16)
    h2T = w_in[:].rearrange("p k c -> p (k c)")[:, 0:8 * NT].rearrange("p (k t) -> p k t", k=8)
    gvec = sb("gvec", [128, 72])
    pcol = sb("pcol", [128, DEPTH, 8])
    convw = sb("convw", [128, DEPTH, 6, 4])
    hrow = sb("hrow", [128, DEPTH, 8])
    negA = sb("negA", [128, 4])
    poolw = sb("poolw", [128, 2, 128], BF16)
    wgk = sb("wgk", [16, DEPTH, 128], BF16)
    bp = sb("bp", [128, 5, 512], BF16)
    bsc = sb("bsc", [128, 4, 128], BF16)
    bsn = sb("bsn", [128, 4, 128], BF16)
    cstf = sb("cstf", [128, 9, 128])
    cstb = sb("cstb", [128, 4, 128], BF16)
    cm = sb("cm", [128, 2, 16])
    invc = sb("invc", [128, 2, 128])
    sq = [sb("sq%d" % i, [128, 128], BF16) for i in range(2)]
    lnt = sb("lnt", [128, 512])
    rstd = sb("rstd", [128, 128])

    OVW = OVW_
    OV = sb("ov", [128, OVW])
    ov_off = [0]
    ov_offs = {}
    P.alias = {}

    def ovview(off, words, shape, dt, parts):
        v = OV[0:parts, off:off + words]
        if dt != F32:
            v = v.bitcast(dt)
        if len(shape) == 3:
            v = v.rearrange("p (a b) -> p a b", a=shape[1])
        return v

    def ova(name, shape, dt=F32, parts=128):
        n = 1
        for d_ in shape[1:]:
            n *= d_
        words = n if dt == F32 else (n + 1) // 2
        off = ov_off[0]
        assert off + words <= OVW, ("overlay overflow", name, off, words)
        ov_off[0] = off + words
        ov_offs[name] = (off, words)
        P.ov_res.add(name)
        return ovview(off, words, shape, dt, parts)

    def ovat(name, off, shape, dt=F32, parts=128):
        n = 1
        for d_ in shape[1:]:
            n *= d_
        words = n if dt == F32 else (n + 1) // 2
        ov_offs[name] = (off, words)
        P.ov_res.add(name)
        return ovview(off, words, shape, dt, parts)

    def union(host, members):
        P.alias[host] = list(members)
        for m_ in members:
            P.alias[m_] = [host]

    hT = ova("hT", [128, 8, 128], BF16)
    mixT = ova("mixT", [128, 8, 128], BF16)
    uext = ova("uext", [128, 2, 192])
    pwa = ova("pwa", [128, 2, 192])
    pwb = ova("pwb", [128, 2, 192])
    sqo = ovat("sqo", ov_offs["pwa"][0], [128, 256], BF16)
    tno = ovat("tno", ov_offs["pwa"][0] + 128, [128, 256])
    union("pwa", ["sqo", "tno"])
    Sdec = ovat("Sdec", ov_offs["pwb"][0], [64, 4, 64], parts=64)
    union("pwb", ["Sdec"])
    pooled = ova("pooled", [128, 2, 128], BF16)
    qb = ova("qb", [128, 2, 128], BF16)
    kbT = ova("kbT", [128, 2, 640], BF16)
    for i in range(5):
        P.ov_res.add("kbT%d" % i)
    vb1 = ova("vb1", [128, 5, 384], BF16)
    tS = ova("tS", [128, 512])
    rc = tS
    QKm = ovat("QKm", ov_offs["tS"][0], [128, 4, 128], BF16)
    wT = ovat("wT", ov_offs["tS"][0] + 256, [64, 4, 128], BF16, parts=64)
    union("tS", ["QKm", "wT"])
    cext = ova("cext", [128, 6, 144])
    cacc = ova("cacc", [128, 6, 128])
    sqc = ova("sqc", [128, 4, 128], BF16)
    qnT = ova("qnT", [128, 2, 128], BF16)
    knT = ova("knT", [128, 2, 128], BF16)
    vcT = ova("vcT", [128, 2, 128], BF16)
    zs = ova("zs", [128, 2, 128], BF16)
    gs = ova("gs", [128, 2, 128], BF16)
    ab = ova("ab", [128, 8])
    sm = ova("sm", [128, 64])
    gm = ova("gm", [128, 16])
    glr = ova("glr", [64, 16], parts=64)
    eglr = ova("eglr", [64, 16], parts=64)
    qgT = ova("qgT", [64, 4, 128], BF16, parts=64)
    Ee = ova("Ee", [128, 4, 128])
    qm = ova("qm", [128, 4, 128], BF16)
    ktc = ova("ktc", [128, 4, 128], BF16)
    Em = ova("Em", [128, 4, 128])
    sqoD = ovat("sqoD", ov_offs["Ee"][0], [128, 256], BF16)
    tnoD = ovat("tnoD", ov_offs["Ee"][0] + 128, [128, 256])
    lntD = ovat("lntD", ov_offs["Em"][0], [128, 256])
    union("Ee", ["sqoD", "tnoD"])
    union("Em", ["lntD"])
    attm = ova("attm", [128, 4, 128], BF16)
    Gp32 = ova("Gp32", [128, 256])
    Lp = [ova("Lp%d" % i, [128, 4, 128], BF16) for i in range(2)]
    Bpw = [ova("Bpw%d" % i, [128, 4, 128], BF16) for i in range(2)]
    Yb = [ova("Yb%d" % i, [128, 4, 128], BF16) for i in range(2)]
    kcs = ovat("kcs", ov_offs["Lp0"][0], [128, 2, 512], BF16)
    vcs = ovat("vcs", ov_offs["Bpw0"][0], [128, 4, 384], BF16)
    union("kcs", ["Lp0", "Lp1"])
    union("vcs", ["Bpw0", "Bpw1", "Yb0"])
    pT1 = ovat("pT", ov_offs["Yb1"][0], [128, 512], BF16)
    union("Yb1", ["pT"])
    ktil = ova("ktil", [128, 4, 64], BF16)
    Y32 = ova("Y32", [128, 4, 128])
    kf32 = ovat("kf32", ov_offs["Y32"][0], [128, 2, 128])
    vf32 = ovat("vf32", ov_offs["Y32"][0] + 256, [128, 256])
    union("Y32", ["kf32", "vf32"])
    tmpu = ova("tmpu", [128, 256])
    spd = ova("spd", [128, 128])
    Gpbf = ova("Gpbf", [128, 256], BF16)
    unc = ova("unc", [128, 4, 256], BF16)
    Gs = ova("Gs", [128, 128])
    Dm = ova("Dm", [128, 128])
    e1 = ova("e1", [128, 128])
    e2 = ova("e2", [128, 128])
    S32 = [ova("S32_%d" % i, [64, 4, 64], parts=64) for i in range(2)]
    Sbf = [ova("Sbf_%d" % i, [64, 4, 64], BF16, parts=64) for i in range(2)]
    qd = ova("qd", [128, 128])
    kd = ova("kd", [128, 128])
    gkl = ova("gkl", [16, 128], BF16, parts=16)
    egl = ova("egl", [128, 4])
    qt = ova("qt", [128, 128], BF16)
    kt = ova("kt", [128, 128], BF16)
    vD = ova("vD", [128, 256], BF16)
    G32 = [ova("G32_%d" % i, [128, 256]) for i in range(2)]
    dbgt = ova("dbgt", [128, 8, 128]) if DBG else None
    lntP = ova("lntP", [128, 128])
    mixer_words = ov_off[0]
    for nm_, k_ in (("hT", 8), ("mixT", 8), ("cacc", 6), ("cext", 6), ("unc", 4)):
        for i_ in range(k_):
            P.ov_res.add("%s%d" % (nm_, i_))
    for i_ in range(5):
        P.ov_res.add("vb1_%d" % i_)
    ov_off[0] = 0
    wup = [ova("wup%d" % i, [128, 8, 512], BF16) for i in range(2)]
    wdn = [ova("wdn%d" % i, [128, 4, D], BF16) for i in range(2)]
    actb = [ova("actb%d" % i, [128, 4, 512], BF16) for i in range(2)]
    relu_t = [ova("relu%d" % i, [128, 512], BF16) for i in range(2)]
    yout = [ova("yout%d" % i, [128, 128]) for i in range(2)]
    mlp_words = ov_off[0]
    print("overlay words: mixer", mixer_words, "mlp", mlp_words, "of", OVW)

    banks = [stack.enter_context(nc.psum_tensor("pb%d" % i, [128, 512], F32)) for i in range(8)]
    bk_i = [0]

    bank_sess = {}

    held = set()
    cur_pool = [None]
    pool_i = {}

    def bank(hold=False):
        pool = cur_pool[0]
        while True:
            if pool is None:
                i = bk_i[0] % 8
                bk_i[0] += 1
            else:
                k_ = pool_i.get(pool, 0)
                pool_i[pool] = k_ + 1
                i = pool[k_ % len(pool)]
            if ("pb%d" % i) not in held:
                break
        if hold:
            held.add("pb%d" % i)
        bank_sess["pb%d" % i] = set()
        return banks[i], "pb%d" % i

    def mm(out, lhsT, rhs, rd, wr, start=True, stop=True):
        st = bank_sess[wr[0]]
        base = out.base_partition()
        quads = set(range(base // 32, (base + out.shape[0] + 31) // 32))
        newq = quads - st
        if newq:
            assert newq == quads, ("mixed psum quadrants", wr, base, out.shape)
            s_ = True
            st |= quads
        else:
            s_ = False
        P.add("pe", lambda e: e.matmul(out, lhsT, rhs, start=s_, stop=True, skip_group_check=True), rd, wr)

    def act(out, in_, func, rd, wr, bias=None, scale=None):
        kw = {}
        if bias is not None:
            kw["bias"] = bias
        if scale is not None:
            kw["scale"] = scale
        P.add("act", lambda e: e.activation(out, in_, func, **kw), rd, wr)

    def tt(out, in0, in1, op, rd, wr, eng="dve"):
        P.add(eng, lambda e: e.tensor_tensor(out=out, in0=in0, in1=in1, op=op), rd, wr)

    def ts(out, in0, s1, op0, rd, wr, s2=None, op1=None, eng="dve"):
        if op1 is None:
            P.add(eng, lambda e: e.tensor_scalar(out=out, in0=in0, scalar1=s1, scalar2=None, op0=op0), rd, wr)
        else:
            P.add(eng, lambda e: e.tensor_scalar(out=out, in0=in0, scalar1=s1, scalar2=s2, op0=op0, op1=op1), rd, wr)

    def stt(out, in0, scalar, in1, op0, op1, rd, wr):
        P.add("dve", lambda e: e.scalar_tensor_tensor(out=out, in0=in0, scalar=scalar, in1=in1, op0=op0, op1=op1), rd, wr)

    def cpy(out, in_, rd, wr, eng="dve"):
        P.add(eng, lambda e: e.tensor_copy(out, in_), rd, wr)

    def mset(ap, val, wr, eng="dve"):
        P.add(eng, lambda e: e.memset(ap, val), (), wr)

    def dma(out, in_, rd, wr, eng="sp"):
        P.add(eng, lambda e: e.dma_start(out=out, in_=in_), rd, wr, dma=True)

    def scan(out, d0, d1, rd, wr):
        P.add("dve", lambda e: e.tensor_tensor_scan(out=out, data0=d0, data1=d1, initial=0.0, op0=ALU.mult, op1=ALU.add), rd, wr)

    def recip(out, in_, rd, wr):
        P.add("dve", lambda e: e.reciprocal(out, in_), rd, wr)

    IDENT, ONESN, BLK1, BLK64, ONES = 0, 1, 2, 3, 8
    VOFF = [0, 64, 192, 256]
    VCOL = [0, 128, 192, 320]

    def ck(n):
        if STOP is not None and STOP == n:
            raise _Stop()

    try:
        for kc in range(8):
            dma(xT[:, kc, :], xT_d[kc * 128:(kc + 1) * 128, :], (), ["x%d" % kc])
        XR = ["x%d" % k for k in range(8)]
        dma(gvec[:], gvec_d, (), ["gvec"])
        dma(pcol[:], pcol_d, (), ["pcol"])
        ts(pcol[:, :, 5], pcol[:, :, 4], -1.0, ALU.mult, ["pcol"], ["pcol"])
        dma(convw[:], convw_d, (), ["convw"])
        dma(hrow[:], hrow_d, (), ["hrow"])
        dma(cstf[:], cst_d[:, 4:13, :], (), ["cstf"])
        dma(cm[:], cm_d, (), ["cm"])
        dma(invc[:], invc_d, (), ["invc"])
        dma(cstb[:], cst_d[:, 0:4, :], (), ["cstb"], eng="pool")
        dma(wgk[:], wgk_d, (), ["wgk"], eng="pool")

        def load_w_in(l):
            for kc in range(8):
                dma(w_in[:, kc, :], w_in_d[l, kc * 128:(kc + 1) * 128, :], (), ["w_in"], eng="pool")

        def load_w_out(l):
            for kc in range(8):
                dma(w_out[:, kc, :], w_out_d[l, kc * 128:(kc + 1) * 128, :], (), ["w_out"], eng="pool")

        load_w_in(0)
        load_w_out(0)

        def rms_stats(cols, n, xres, lbuf=None, lres="lnt"):
            if lbuf is None:
                lbuf = lnt
            bkt, bkr = bank()
            for kc in range(8):
                s = sq[kc % 2]
                act(s[:, :n], xT[:, kc, cols], AF.Square, [xres[kc]], ["sq%d" % (kc % 2)])
                mm(bkt[:, :n], cstb[:, ONESN, :], s[:, :n], ["sq%d" % (kc % 2), "cstb"], [bkr], start=(kc == 0), stop=(kc == 7))
            act(lbuf[:, :n], bkt[:, :n], AF.Ln, [bkr], [lres], bias=EPS)
            act(rstd[:, :n], lbuf[:, :n], AF.Exp, [lres], ["rstd"], scale=-0.5)

        TIL = [(0, 512), (512, 512), (1024, 512), (1536, 512), (2048, 128)]

        for l in range(NL):
            dma(bp[:], bp_d[l], (), ["bp"], eng="pool")
            dma(bsc[:], bsc_d[l], (), ["bsc"], eng="pool")
            dma(bsn[:], bsn_d[l], (), ["bsn"], eng="pool")
            dma(poolw[:], poolw_d[l].rearrange("k p c -> p k c"), (), ["poolw"], eng="pool")
            act(negA[:], hrow[:, l, 0:4], AF.Exp, ["hrow"], ["negA"])
            ts(negA[:], negA[:], -1.0, ALU.mult, ["negA"], ["negA"])
            mset(vb1[:], 1.0, ["vb1_%d" % s_ for s_ in range(5)])
            mset(S32[0][:], 0.0, ["S32_0"])
            mset(Sbf[0][:], 0.0, ["Sbf_0"])
            mset(G32[0][:], 0.0, ["G32_0"])

            def h2_block(bb2):
                cols2 = slice(bb2 * 128, bb2 * 128 + 128)
                rms_stats(cols2, 128, XR, lntP, "lntP")
                for kc in range(8):
                    stt(h2T[:, kc, cols2], xT[:, kc, cols2], gvec[:, 32 + l * 8 + kc:32 + l * 8 + kc + 1], rstd[:, :128],
                        ALU.mult, ALU.mult, [XR[kc], "rstd", "gvec"], ["w_in"])

            for b in range(NBLK):
                smp = (b == 16)
                ty = 1 if smp else 0
                nch = 4 if smp else 2
                cs = 32 if smp else 64
                cols = slice(b * 128, (b + 1) * 128)
                TRI, SAME, MSLN, RST = 0 + ty, 2 + ty, 4 + ty, 6 + ty
                xres = XR
                CX = ["cext%d" % g_ for g_ in range(6)]

                if smp:
                    uxs = uext[:, :, 0:188].rearrange("p g (s t) -> p g s t", s=4)
                    cxs = cext[:, :, 0:140].rearrange("p g (s t) -> p g s t", s=4)
                slot = b % 5
                need_kv = smp or b >= 12

                def proj(c0, M):
                    bkt, bkr = bank()
                    for kc in range(8):
                        mm(bkt[0:M, 0:128], w_in[:, kc, c0:c0 + M], hT[:, kc, :], ["w_in", "hT%d" % kc], [bkr],
                           start=(kc == 0), stop=(kc == 7))
                    return bkt, bkr

                def front_early(bq):
                    smq = (bq == 16)
                    colq = slice(bq * 128, (bq + 1) * 128)
                    slq = bq % 5
                    nkv = smq or bq >= 12
                    rms_stats(colq, 128, XR, lntP, "lntP")
                    for kc in range(8):
                        stt(hT[:, kc, :], xT[:, kc, colq], gvec[:, l * 8 + kc:l * 8 + kc + 1], rstd[:, :128],
                            ALU.mult, ALU.mult, [XR[kc], "rstd", "gvec"], ["hT%d" % kc])
                    if smq:
                        for g in range(2):
                            dma(uxs[:, g, :, 0:15], cpool_d[l, :, g * 128:(g + 1) * 128, :].rearrange("s p t -> p s t"), (), ["uext"])
                        for g in range(6):
                            dma(cxs[:, g, :, 0:3], sconv_d[l, :, g * 128:(g + 1) * 128, :].rearrange("s p t -> p s t"), (), ["cext%d" % g])
                    elif bq == 0:
                        mset(uext[:, :, 0:15], 0.0, ["uext"])
                        mset(cext[:, :, 0:3], 0.0, CX)
                    else:
                        cpy(uext[:, :, 0:15], uext[:, :, 128:143], ["uext"], ["uext"])
                        cpy(cext[:, :, 0:3], cext[:, :, 128:131], CX, CX)
                    for g in range(2):
                        bkt, bkr = proj(g * 128, 128)
                        if smq:
                            act(uxs[:, g, :, 15:47], bkt[:, 0:128].rearrange("p (s t) -> p s t", s=4), AF.Copy, [bkr], ["uext"])
                        else:
                            act(uext[:, g, 15:143], bkt[:, 0:128], AF.Copy, [bkr], ["uext"])
                    for g in range(2):
                        bkt, bkr = proj(256 + g * 128, 128)
                        act(qb[:, g, :], bkt[:, 0:128], AF.Copy, [bkr], ["qb"], scale=0.125)
                    for g in range(2):
                        bkt, bkr = proj(512 + g * 128, 128)
                        act(kbT[:, g, slq * 128:(slq + 1) * 128], bkt[:, 0:128], AF.Copy, [bkr], ["kbT%d" % slq])
                    for g in range(6):
                        bkt, bkr = proj(1024 + g * 128, 128)
                        if smq:
                            cpy(cxs[:, g, :, 3:35], bkt[:, 0:128].rearrange("p (s t) -> p s t", s=4), [bkr], ["cext%d" % g])
                        else:
                            cpy(cext[:, g, 3:131], bkt[:, 0:128], [bkr], ["cext%d" % g])
                    bkt, bkr = proj(2056, 128)
                    act(qd[:], bkt[:, 0:128], AF.Copy, [bkr], ["qd"], scale=float(32 ** -0.5))
                    bkt, bkr = proj(2184, 128)
                    cpy(kd[:], bkt[:, 0:128], [bkr], ["kd"])
                    bkt, bkr = proj(2824, 16)
                    cpy(gkl[:, :], bkt[0:16, 0:128], [bkr], ["gkl"])
                    for g in range(6):
                        for j in range(4):
                            if smq:
                                src = cxs[:, g, :, j:j + 32]
                                dst = cacc[:, g, :].rearrange("p (s t) -> p s t", s=4)
                            else:
                                src = cext[:, g, j:j + 128]
                                dst = cacc[:, g, :]
                            if j == 0:
                                ts(dst, src, convw[:, l, g, 0:1], ALU.mult, ["cext%d" % g, "convw"], ["cacc%d" % g])
                            else:
                                stt(dst, src, convw[:, l, g, j:j + 1], dst, ALU.mult, ALU.add, ["cext%d" % g, "convw", "cacc%d" % g], ["cacc%d" % g])

                def front_late(bq):
                    slq = bq % 5
                    nkv = (bq == 16) or bq >= 12
                    if nkv:
                        for g in range(2):
                            bkt, bkr = proj(512 + g * 128, 128)
                            act(kf32[:, g, :], bkt[:, 0:128], AF.Copy, [bkr], ["kf32"])
                    for g in range(2):
                        bkt, bkr = proj(1792 + g * 128, 128)
                        act(zs[:, g, :], bkt[:, 0:128], AF.Silu, [bkr], ["zs"])
                    for g in range(2):
                        bkt, bkr = proj(2568 + g * 128, 128)
                        act(gs[:, g, :], bkt[:, 0:128], AF.Silu, [bkr], ["gs"])
                    bkt, bkr = bank()
                    bk2, bk2r = bank()
                    for kc in range(8):
                        mm(bkt[:, 0:256], hT[:, kc, :], w_in[:, kc, 768:1024], ["w_in", "hT%d" % kc], [bkr], start=(kc == 0), stop=(kc == 7))
                        mm(bkt[:, 256:512], hT[:, kc, :], w_in[:, kc, 2312:2568], ["w_in", "hT%d" % kc], [bkr], start=(kc == 0), stop=(kc == 7))
                        mm(bk2[:, 0:8], hT[:, kc, :], w_in[:, kc, 2048:2056], ["w_in", "hT%d" % kc], [bk2r], start=(kc == 0), stop=(kc == 7))
                    for h in range(4):
                        cpy(vb1[:, slq, VCOL[h]:VCOL[h] + 64], bkt[:, 64 * h:64 * h + 64], [bkr], ["vb1_%d" % slq])
                    if nkv:
                        act(vf32[:], bkt[:, 0:256], AF.Copy, [bkr], ["vf32"])
                    act(vD[:], bkt[:, 256:512], AF.Copy, [bkr], ["vD"])
                    cpy(ab[:], bk2[:, 0:8], [bk2r], ["ab"])

                def w_out_part(k0, k1):
                    for m in range(8):
                        bkt, bkr = bank()
                        for kc in range(k0, k1):
                            mm(bkt[:, 0:128], w_out[:, kc, m * 128:(m + 1) * 128], mixT[:, kc, :], ["w_out", "mixT%d" % kc], [bkr],
                               start=(kc == k0), stop=(kc == k1 - 1))
                        tt(xT[:, m, cols], xT[:, m, cols], bkt[:, 0:128], ALU.add, ["x%d" % m, bkr], ["x%d" % m])

                prefetched = (1 <= b <= 15)
                if not prefetched:
                    front_early(b)
                do_fab = (1 <= b <= 15)
                listF, listO, listA, listB = [], [], [], []
                P.capture = listF
                cur_pool[0] = (0, 1, 2) if do_fab else None
                front_late(b)
                P.capture = listO

                ck(100 * b + 2)
                if need_kv:
                    c0 = 512 if smp else (b - 12) * 128
                    for g in range(2):
                        dma(k_o[l, g * 128:(g + 1) * 128, c0:c0 + 128], kf32[:, g, :], ["kf32"], ())
                    dma(v_o[l, c0:c0 + 128, :], vf32[:], ["vf32"], ())
                if smp:
                    for g in range(2):
                        dma(pool_o[l, 1:5, g * 128:(g + 1) * 128, :].rearrange("s p t -> p s t"), uxs[:, g, :, 32:47], ["uext"], ())
                    for g in range(6):
                        dma(conv_o[l, 1:5, g * 128:(g + 1) * 128, :].rearrange("s p t -> p s t"), cxs[:, g, :, 32:35], ["cext%d" % g], ())
                elif b == 15:
                    for g in range(2):
                        dma(pool_o[l, 0, g * 128:(g + 1) * 128, :], uext[:, g, 128:143], ["uext"], ())
                    for g in range(6):
                        dma(conv_o[l, 0, g * 128:(g + 1) * 128, :], cext[:, g, 128:131], ["cext%d" % g], ())

                P.capture = listA
                cur_pool[0] = (3,) if do_fab else None
                if smp:
                    def V(t, a, bb):
                        return t[:, :, 0:188].rearrange("p g (s t) -> p g s t", s=4)[:, :, :, a:bb]
                    L_ = 47
                else:
                    def V(t, a, bb):
                        return t[:, :, a:bb]
                    L_ = 143
                wins = [(uext, pwa, 1), (pwa, pwb, 2), (pwb, pwa, 4), (pwa, pwb, 8)]
                for gi, (src, dst, sh) in enumerate(wins):
                    lo = 2 * sh - 1
                    dres = "pwa" if dst is pwa else "pwb"
                    tt(V(dst, lo, L_), V(src, lo, L_), V(src, lo - sh, L_ - sh), ALU.add,
                       ["uext", "pwa", "pwb"], [dres])
                    kc, p0 = gi // 2, 64 * (gi % 2)
                    win = 2 * sh
                    if smp:
                        o_ = pooled[p0:p0 + 64, kc, :].rearrange("p (s t) -> p s t", s=4)
                        i0 = V(dst, 15, 47)[p0:p0 + 64, kc]
                        i1 = V(uext, 15, 47)[p0:p0 + 64, kc]
                    else:
                        o_ = pooled[p0:p0 + 64, kc, :]
                        i0 = dst[p0:p0 + 64, kc, 15:143]
                        i1 = uext[p0:p0 + 64, kc, 15:143]
                    if b == 0:
                        tt(tS[p0:p0 + 64, 0:128], i0, invc[p0:p0 + 64, kc, :], ALU.mult, ["invc", dres], ["tS"])
                        tt(o_, tS[p0:p0 + 64, 0:128], i1, ALU.subtract, ["uext", "tS"], ["pooled"])
                    else:
                        stt(o_, i0, 1.0 / win, i1, ALU.mult, ALU.subtract, ["uext", dres], ["pooled"])
                for kc in range(2):
                    bkt, bkr = bank()
                    mm(bkt[:, 0:128], poolw[:, kc, :], pooled[:, kc, :], ["poolw", "pooled"], [bkr])
                    ts(mixT[:, kc, :], bkt[:, 0:128], pcol[:, l, kc:kc + 1], ALU.mult, [bkr, "pcol"], ["mixT%d" % kc])

                P.capture = listB
                cur_pool[0] = (4, 5, 6, 7) if do_fab else None
                ob, obr = bank(hold=True)
                if not smp:
                    rlist = [r for r in range(5) if b - 4 + r >= 0]
                    for ri, r in enumerate(rlist):
                        ks = (b - 4 + r) % 5
                        scp = [bank(), bank()]
                        for h in range(4):
                            p0, g = 64 * (h % 2), h // 2
                            sc, scr = scp[h % 2]
                            mm(sc[:, 128 * g:128 * g + 128], kbT[p0:p0 + 64, g, ks * 128:(ks + 1) * 128], qb[p0:p0 + 64, g, :],
                               ["kbT%d" % ks, "qb"], [scr])
                        i2 = ri % 2
                        ck(100 * b + 41)
                        for par in range(2):
                            sc, scr = scp[par]
                            tt(tS[:].rearrange("p (h q) -> p h q", h=4)[:, par::2, :], sc[:, 0:256].rearrange("p (h q) -> p h q", h=2),
                               bp[:, r, :].rearrange("p (h q) -> p h q", h=4)[:, par::2, :], ALU.add, [scr, "bp"], ["tS"])
                        ck(100 * b + 42)
                        act(pT1[:], tS[:], AF.Exp, ["tS"], ["pT"])
                        ck(100 * b + 43)
                        for h in range(4):
                            mm(ob[:, 128 * h:128 * h + 128], vb1[:, ks, VOFF[h]:VOFF[h] + 128], pT1[:, 128 * h:128 * h + 128],
                               ["vb1_%d" % ks, "pT"], [obr], start=(ri == 0), stop=(ri == len(rlist) - 1))
                    ck(100 * b + 44)
                    for h in range(4):
                        po, ps_ = (0, 64) if h % 2 == 0 else (64, 0)
                        recip(rc[po:po + 64, 128 * h:128 * h + 128], ob[ps_:ps_ + 64, 128 * h:128 * h + 128], [obr], ["tS"])
                        ck(100 * b + 45)
                        tt(mixT[po:po + 64, 2 + h // 2, :], ob[po:po + 64, 128 * h:128 * h + 128], rc[po:po + 64, 128 * h:128 * h + 128],
                           ALU.mult, [obr, "tS"], ["mixT%d" % (2 + h // 2)])
                else:
                    for s in range(4):
                        dma(kcs[:], ck_d[l, s].rearrange("(g p) t -> p g t", p=128), (), ["kcs"], eng="pool")
                        mset(vcs[:], 1.0, ["vcs"])
                        for h in range(4):
                            dma(vcs[:, :, VCOL[h]:VCOL[h] + 64], cv_d[l, s, h].rearrange("(r p) d -> p r d", p=128), (), ["vcs"], eng="pool")
                        for r in range(5):
                            scp = [bank(), bank()]
                            for h in range(4):
                                p0, g = 64 * (h % 2), h // 2
                                sc, scr = scp[h % 2]
                                if r < 4:
                                    lhsT = kcs[p0:p0 + 64, g, r * 128:(r + 1) * 128]
                                    rdk = "kcs"
                                else:
                                    lhsT = kbT[p0:p0 + 64, g, slot * 128:(slot + 1) * 128]
                                    rdk = "kbT%d" % slot
                                mm(sc[:, 32 * g:32 * g + 32], lhsT, qb[p0:p0 + 64, g, 32 * s:32 * s + 32], [rdk, "qb"], [scr])
                            i2 = r % 2
                            tab = bsc[:, r, :] if r < 4 else bsn[:, s, :]
                            for par in range(2):
                                sc, scr = scp[par]
                                tt(tS[:, 0:128].rearrange("p (h q) -> p h q", h=4)[:, par::2, :], sc[:, 0:64].rearrange("p (h q) -> p h q", h=2),
                                   tab.rearrange("p (h q) -> p h q", h=4)[:, par::2, :], ALU.add, [scr, "bsc", "bsn"], ["tS"])
                            act(pT1[:, 0:128], tS[:, 0:128], AF.Exp, ["tS"], ["pT"])
                            for h in range(4):
                                lhsT = vcs[:, r, VOFF[h]:VOFF[h] + 128] if r < 4 else vb1[:, slot, VOFF[h]:VOFF[h] + 128]
                                mm(ob[:, 128 * s + 32 * h:128 * s + 32 * h + 32], lhsT, pT1[:, 32 * h:32 * h + 32],
                                   ["vcs", "vb1_%d" % slot, "pT"], [obr], start=(r == 0), stop=(r == 4))
                    obv = ob[:].rearrange("p (s h q) -> p s h q", s=4, h=4)
                    rcv = rc[:].rearrange("p (s h q) -> p s h q", s=4, h=4)
                    for h in range(4):
                        po, ps_ = (0, 64) if h % 2 == 0 else (64, 0)
                        recip(rcv[po:po + 64, :, h, :], obv[ps_:ps_ + 64, :, h, :], [obr], ["tS"])
                        tt(mixT[po:po + 64, 2 + h // 2, :].rearrange("p (s q) -> p s q", s=4), obv[po:po + 64, :, h, :],
                           rcv[po:po + 64, :, h, :], ALU.mult, [obr, "tS"], ["mixT%d" % (2 + h // 2)])

                held.discard(obr)
                P.capture = None
                cur_pool[0] = None
                if do_fab:
                    keyed = []
                    for lst, off_, sc_ in ((listF, 0.0, 0.5), (listA, 0.0, 1.0), (listB, 0.0, 1.0)):
                        n_ = len(lst)
                        for i_, op_ in enumerate(lst):
                            keyed.append((off_ + sc_ * (i_ + 0.5) / n_, len(keyed), op_))
                    keyed.sort(key=lambda t_: (t_[0], t_[1]))
                    for _, _, op_ in keyed:
                        P.add(*op_)
                    for op_ in listO:
                        P.add(*op_)
                else:
                    for lst in (listF, listO, listA, listB):
                        for op_ in lst:
                            P.add(*op_)
                listC = []
                P.capture = listC
                cur_pool[0] = (0, 1, 2, 3)
                tyC, nchC, csC = 1, 4, 32
                TRIc, SAMEc, MSLNc = 0 + tyC, 2 + tyC, 4 + tyC
                act(vcT[:], cacc[:, 4:6, :], AF.Silu, ["cacc4", "cacc5"], ["vcT"])
                csl = cacc[:, 0:4, :]
                CQ = ["cacc0", "cacc1", "cacc2", "cacc3"]
                act(csl, csl, AF.Silu, CQ, CQ)
                act(sqc[:], csl, AF.Square, CQ, ["sqc"])
                bkt, bkr = bank()
                for g in range(4):
                    mm(bkt[:, 128 * g:128 * g + 128], cstb[:, BLK1, :], sqc[:, g, :], ["cstb", "sqc"], [bkr])
                act(lnt[:], bkt[:], AF.Ln, [bkr], ["lnt"], bias=EPS)
                act(lnt[:], lnt[:], AF.Exp, ["lnt"], ["lnt"], scale=-0.5)
                rn = lnt[:].rearrange("p (g t) -> p g t", g=4)
                stt(qnT[:], csl[:, 0:2, :], 0.125, rn[:, 0:2, :], ALU.mult, ALU.mult, ["cacc0", "cacc1", "lnt"], ["qnT"])
                tt(knT[:], csl[:, 2:4, :], rn[:, 2:4, :], ALU.mult, ["cacc2", "cacc3", "lnt"], ["knT"])
                btr, btrr = bank()
                for g in range(2):
                    mm(btr[:, 128 * g:128 * g + 128], vcT[:, g, :], cstb[:, IDENT, :], ["vcT", "cstb"], [btrr])
                    mm(btr[:, 256 + 128 * g:256 + 128 * g + 128], knT[:, g, :], cstb[:, IDENT, :], ["knT", "cstb"], [btrr])
                tt(sm[:, 0:4], ab[:, 0:4], hrow[:, l, 4:8], ALU.add, ["ab", "hrow"], ["sm"])
                act(sm[:, 4:8], sm[:, 0:4], AF.Exp, ["sm"], ["sm"])
                act(sm[:, 8:12], sm[:, 4:8], AF.Ln, ["sm"], ["sm"], bias=1.0)
                tt(sm[:, 12:16], sm[:, 8:12], negA[:], ALU.mult, ["sm", "negA"], ["sm"])
                act(sm[:, 16:20], ab[:, 4:8], AF.Exp, ["ab"], ["sm"], scale=-1.0)
                ts(sm[:, 16:20], sm[:, 16:20], 1.0, ALU.add, ["sm"], ["sm"])
                recip(sm[:, 20:24], sm[:, 16:20], ["sm"], ["sm"])
                tt(gm[:, 0:4 * nchC].rearrange("p (c h) -> p c h", c=nchC), bc(sm[:, 12:16].unsqueeze(1), [128, nchC, 4]),
                   bc(cm[:, tyC, 0:nchC].unsqueeze(2), [128, nchC, 4]), ALU.mult, ["sm", "cm"], ["gm"])
                bsm, bsmr = bank()
                mm(bsm[:, 0:4], cstf[:, TRIc, :], sm[:, 12:16], ["cstf", "sm"], [bsmr])
                mm(bsm[:, 4:8], cstf[:, SAMEc, :], sm[:, 12:16], ["cstf", "sm"], [bsmr])
                mm(bsm[0:64, 8:8 + 4 * nchC], cstf[:, ONES, 0:64], gm[:, 0:4 * nchC], ["cstf", "gm"], [bsmr])
                bgr, bgrr = bank()
                for h in range(4):
                    mm(bgr[:, 128 * h:128 * h + 128], bc(sm[:, 12 + h:13 + h], [128, 128]), cstf[:, TRIc, :], ["cstf", "sm"], [bgrr])
                cpy(sm[:, 24:32], bsm[:, 0:8], [bsmr], ["sm"])
                cpy(glr[:, 0:4 * nchC], bsm[0:64, 8:8 + 4 * nchC], [bsmr], ["glr"])
                act(eglr[:, 0:4 * nchC], glr[:, 0:4 * nchC], AF.Exp, ["glr"], ["eglr"])
                act(sm[:, 32:36], sm[:, 24:28], AF.Exp, ["sm"], ["sm"])
                tt(sm[:, 44:48], sm[:, 28:32], sm[:, 24:28], ALU.subtract, ["sm"], ["sm"])
                act(sm[:, 36:40], sm[:, 44:48], AF.Exp, ["sm"], ["sm"])
                tt(sm[:, 40:44], sm[:, 20:24], sm[:, 32:36], ALU.mult, ["sm"], ["sm"])
                act(Em[:].rearrange("p h t -> p (h t)"), bgr[:], AF.Exp, [bgrr], ["Em"])
                for h in range(4):
                    p0, g = 64 * (h % 2), h // 2
                    tt(qgT[:, h, :], qnT[p0:p0 + 64, g, :], Em[p0:p0 + 64, h, :], ALU.mult, ["qnT", "Em"], ["qgT"])
                tt(Y32[:, :, 0:64], btr[:, 0:256].rearrange("p (h d) -> p h d", h=4), bc(sm[:, 20:24].unsqueeze(2), [128, 4, 64]),
                   ALU.mult, [btrr, "sm"], ["Y32"])
                tt(Y32[:, :, 64:128], btr[:, 256:512].rearrange("p (h d) -> p h d", h=4), bc(sm[:, 40:44].unsqueeze(2), [128, 4, 64]),
                   ALU.mult, [btrr, "sm"], ["Y32"])
                act(Yb[0][:], Y32[:], AF.Copy, ["Y32"], ["Yb0"])
                tt(ktil[:], btr[:, 256:512].rearrange("p (h d) -> p h d", h=4), bc(sm[:, 36:40].unsqueeze(2), [128, 4, 64]),
                   ALU.mult, [btrr, "sm"], ["ktil"])
                gram = [bank(), bank()]
                for h in range(4):
                    p0, g = 64 * (h % 2), h // 2
                    gb, gbr = gram[h % 2]
                    mm(gb[:, 128 * g:128 * g + 128], knT[p0:p0 + 64, g, :], knT[p0:p0 + 64, g, :], ["knT"], [gbr])
                    mm(gb[:, 256 + 128 * g:256 + 128 * g + 128], knT[p0:p0 + 64, g, :], qnT[p0:p0 + 64, g, :], ["knT", "qnT"], [gbr])
                for h in range(4):
                    ts(Ee[:, h, :], bgr[:, 128 * h:128 * h + 128], sm[:, 24 + h:25 + h], ALU.subtract, [bgrr, "sm"], ["Ee"])
                stt(Ee[:], Ee[:], -1.0, Ee[:], ALU.mult, ALU.max, ["Ee"], ["Ee"])
                act(Ee[:], Ee[:], AF.Exp, ["Ee"], ["Ee"], scale=-1.0)
                tt(Em[:], Ee[:], bc(cstf[:, MSLNc, :].unsqueeze(1), [128, 4, 128]), ALU.mult, ["Ee", "cstf", "qgT"], ["Em"])
                for h in range(4):
                    stt(Lp[0][:, h, :], gram[h % 2][0][:, 128 * (h // 2):128 * (h // 2) + 128], sm[:, 20 + h:21 + h], Em[:, h, :],
                        ALU.mult, ALU.mult, [gram[h % 2][1], "sm", "Em"], ["Lp0"])
                tt(Em[:], Ee[:], bc(cstf[:, TRIc, :].unsqueeze(1), [128, 4, 128]), ALU.mult, ["Ee", "cstf"], ["Em"])
                for par in range(2):
                    tt(QKm[:, par::2, :], gram[par][0][:, 256:512].rearrange("p (h t) -> p h t", h=2), Em[:, par::2, :], ALU.mult,
                       [gram[par][1], "Em"], ["QKm"])
                bb_, bbr = bank()
                for h in range(4):
                    mm(bb_[:, 128 * h:128 * h + 128], Lp[0][:, h, :], cstb[:, IDENT, :], ["Lp0", "cstb"], [bbr])
                act(Bpw[0][:].rearrange("p h t -> p (h t)"), bb_[:], AF.Copy, [bbr], ["Bpw0"])
                NLV = 5
                for k in range(NLV):
                    ci, ni = k % 2, (k + 1) % 2
                    by, byr = bank()
                    for h in range(4):
                        mm(by[:, 128 * h:128 * h + 128], Bpw[ci][:, h, :], Yb[ci][:, h, :], ["Bpw%d" % ci, "Yb%d" % ci], [byr])
                    tt(Y32[:].rearrange("p h t -> p (h t)"), Y32[:].rearrange("p h t -> p (h t)"), by[:], ALU.add, ["Y32", byr], ["Y32"])
                    act(Yb[ni][:], Y32[:], AF.Copy, ["Y32"], ["Yb%d" % ni])
                    if k < NLV - 1:
                        b2, b2r = bank()
                        for h in range(4):
                            mm(b2[:, 128 * h:128 * h + 128], Lp[ci][:, h, :], Bpw[ci][:, h, :], ["Lp%d" % ci, "Bpw%d" % ci], [b2r])
                        act(Bpw[ni][:].rearrange("p h t -> p (h t)"), b2[:], AF.Copy, [b2r], ["Bpw%d" % ni])
                        if k < NLV - 2:
                            l2, l2r = bank()
                            for h in range(4):
                                mm(l2[:, 128 * h:128 * h + 128], Bpw[ci][:, h, :], Lp[ci][:, h, :], ["Lp%d" % ci, "Bpw%d" % ci], [l2r])
                            act(Lp[ni][:].rearrange("p h t -> p (h t)"), l2[:], AF.Copy, [l2r], ["Lp%d" % ni])
                yfin = NLV % 2
                Yf, Yfr = Yb[yfin], "Yb%d" % yfin
                bw, bwr = bank()
                for h in range(4):
                    mm(bw[0:64, 128 * h:128 * h + 128], Yf[:, h, 64:128], cstb[:, IDENT, :], [Yfr, "cstb"], [bwr])
                act(wT[:].rearrange("p h t -> p (h t)"), bw[0:64, :], AF.Copy, [bwr], ["wT"])
                bo, bor = bank(hold=True)
                for c in range(nchC):
                    si = (c % 2) if smp else 0
                    sr, sbr = "S32_%d" % si, "Sbf_%d" % si
                    if smp:
                        dma(S32[si][:], sdel_d[l, c], (), [sr])
                        cpy(Sbf[si][:], S32[si][:], [sr], [sbr])
                    bws, bwsr = bank()
                    for h in range(4):
                        mm(bws[:, 64 * h:64 * h + 64], wT[:, h, :], Sbf[si][:, h, :], ["wT", sbr], [bwsr])
                    tt(tmpu[:].rearrange("p (h d) -> p h d", h=4), Y32[:, :, 0:64], bws[:, 0:256].rearrange("p (h d) -> p h d", h=4),
                       ALU.subtract, ["Y32", bwsr], ["tmpu"])
                    ts(unc[:, c, :], tmpu[:], cm[:, tyC, c:c + 1], ALU.mult, ["tmpu", "cm"], ["unc%d" % c])
                    for h in range(4):
                        p0, g = 64 * (h % 2), h // 2
                        mm(bo[p0:p0 + 64, 128 * g + c * csC:128 * g + (c + 1) * csC], Sbf[si][:, h, :], qgT[:, h, c * csC:(c + 1) * csC],
                           [sbr, "qgT"], [bor], start=True, stop=False)
                    bds, bdsr = bank()
                    for h in range(4):
                        mm(bds[0:64, 64 * h:64 * h + 64], ktil[:, h, :], unc[:, c, 64 * h:64 * h + 64], ["ktil", "unc%d" % c], [bdsr])
                    tt(Sdec[:], S32[si][:], bc(eglr[:, 4 * c:4 * c + 4].unsqueeze(2), [64, 4, 64]), ALU.mult, [sr, "eglr"], ["Sdec"])
                    tt(Sbf[si][:], Sdec[:], bds[0:64, 0:256].rearrange("p (h d) -> p h d", h=4), ALU.add, ["Sdec", bdsr], [sbr])
                    tt(S32[si][:], Sdec[:], bds[0:64, 0:256].rearrange("p (h d) -> p h d", h=4), ALU.add, ["Sdec", bdsr], [sr])
                    for h in range(4):
                        p0, g = 64 * (h % 2), h // 2
                        mm(bo[p0:p0 + 64, 128 * g + c * csC:128 * g + (c + 1) * csC], unc[:, c, 64 * h:64 * h + 64],
                           QKm[:, h, c * csC:(c + 1) * csC], ["unc%d" % c, "QKm"], [bor], start=False, stop=True)
                    if smp:
                        dma(del_o[l, 1 + c], S32[si][:], [sr], ())
                if b == 15:
                    dma(del_o[l, 0], S32[0][:], ["S32_0"], ())

                def out_norm(bo_, bor_, gcol, gate, gres, kbase, sq_=None, sqr="sqo", tn_=None, tnr="tno", ln_=None, lnr="lnt"):
                    sq_ = sqo if sq_ is None else sq_
                    tn_ = tno if tn_ is None else tn_
                    ln_ = lnt[:, 0:256] if ln_ is None else ln_[:]
                    act(sq_[:], bo_[:, 0:256], AF.Square, [bor_], [sqr])
                    bn, bnr = bank()
                    for g in range(2):
                        mm(bn[:, 128 * g:128 * g + 128], cstb[:, BLK64, :], sq_[:, 128 * g:128 * g + 128], ["cstb", sqr], [bnr])
                    act(ln_, bn[:, 0:256], AF.Ln, [bnr], [lnr], bias=EPS)
                    act(ln_, ln_, AF.Exp, [lnr], [lnr], scale=-0.5)
                    stt(tn_[:], bo_[:, 0:256], pcol[:, l, gcol:gcol + 1], ln_, ALU.mult, ALU.mult, [bor_, "pcol", lnr], [tnr])
                    tt(mixT[:, kbase:kbase + 2, :], tn_[:].rearrange("p (g t) -> p g t", g=2), gate[:], ALU.mult, [tnr, gres], ["mixT%d" % kbase, "mixT%d" % (kbase + 1)])

                out_norm(bo, bor, 2, zs, "zs", 4)
                held.discard(bor)
                listD = []
                P.capture = listD
                cur_pool[0] = (4, 5)

                ck(100 * b + 6)
                bd, bdr = bank()
                mm(bd[:, 0:128], wgk[:, l, :], gkl[:, :], ["wgk", "gkl"], [bdr])
                act(e1[:], bd[:, 0:128], AF.Exp, [bdr, "pcol"], ["e1"], bias=pcol[:, l, 5:6], scale=-1.0)
                act(spd[:], e1[:], AF.Ln, ["e1"], ["spd"], bias=1.0)
                scan(Gs[:], cstf[:, RST, :], spd[:], ["cstf", "spd"], ["Gs"])
                Gv = Gs[:].rearrange("p (c t) -> p c t", c=nch)
                tt(Dm[:].rearrange("p (c t) -> p c t", c=nch), Gv, bc(Gv[:, :, cs - 1:cs], [128, nch, cs]), ALU.subtract, ["Gs"], ["Dm"])
                act(e1[:], Dm[:], AF.Exp, ["Dm"], ["e1"], scale=-1.0 / 16)
                act(e2[:], Dm[:], AF.Exp, ["Dm"], ["e2"], scale=1.0 / 16)
                act(egl[:, 0:nch], Gv[:, :, cs - 1], AF.Exp, ["Gs"], ["egl"], scale=-1.0 / 16)
                tt(qt[:], qd[:], e1[:], ALU.mult, ["qd", "e1"], ["qt"])
                tt(kt[:], kd[:], e2[:], ALU.mult, ["kd", "e2"], ["kt"])
                for h in range(4):
                    ts(qm[:, h, :], qt[:], cm[:, ty, 8 + h:9 + h], ALU.mult, ["qt", "cm"], ["qm"])
                bt2, bt2r = bank()
                mm(bt2[:, 0:128], kt[:], cstb[:, IDENT, :], ["kt", "cstb"], [bt2r])
                for c in range(nch):
                    ts(ktc[:, c, :], bt2[:, 0:128], cm[:, ty, c:c + 1], ALU.mult, [bt2r, "cm"], ["ktc"])
                bat, batr = bank()
                for h in range(4):
                    mm(bat[:, 128 * h:128 * h + 128], kt[:], qm[:, h, :], ["kt", "qm"], [batr])
                tt(attm[:], bat[:].rearrange("p (h t) -> p h t", h=4), bc(cstf[:, TRI, :].unsqueeze(1), [128, 4, 128]), ALU.mult,
                   [batr, "cstf"], ["attm"])
                bod, bodr = bank(hold=True)
                for c in range(nch):
                    si = (c % 2) if smp else 0
                    gr = "G32_%d" % si
                    if smp:
                        mset(G32[si][:], 0.0, [gr])
                        for h in range(4):
                            dma(G32[si][32 * h:32 * h + 32, 64 * h:64 * h + 64], sgla_d[l, c, h], (), [gr])
                    bx, bxr = bank()
                    mm(bx[:, 0:256], ktc[:, c, :], vD[:], ["ktc", "vD"], [bxr])
                    ts(Gp32[:], G32[si][:], egl[:, c:c + 1], ALU.mult, [gr, "egl"], ["Gp32"])
                    act(Gpbf[:], Gp32[:], AF.Copy, ["Gp32"], ["Gpbf"])
                    for h in range(4):
                        p0, g = 64 * (h % 2), h // 2
                        osl = bod[p0:p0 + 64, 128 * g + c * cs:128 * g + (c + 1) * cs]
                        mm(osl, Gpbf[:, 64 * h:64 * h + 64], qm[:, h, c * cs:(c + 1) * cs], ["Gpbf", "qm"], [bodr], start=True, stop=False)
                        mm(osl, vD[:, 64 * h:64 * h + 64], attm[:, h, c * cs:(c + 1) * cs], ["vD", "attm"], [bodr], start=False, stop=True)
                    tt(G32[si][:], Gp32[:], bx[:, 0:256], ALU.add, ["Gp32", bxr], [gr])
                    if smp:
                        for h in range(4):
                            dma(gla_o[l, 1 + c, h], G32[si][32 * h:32 * h + 32, 64 * h:64 * h + 64], [gr], ())
                if b == 15:
                    for h in range(4):
                        dma(gla_o[l, 0, h], G32[0][32 * h:32 * h + 32, 64 * h:64 * h + 64], ["G32_0"], ())
                out_norm(bod, bodr, 3, gs, "gs", 6, sqoD, "sqoD", tnoD, "tnoD", lntD, "lntD")
                held.discard(bodr)
                P.capture = None
                cur_pool[0] = None
                listP = []
                if 1 <= b + 1 <= 15:
                    P.capture = listP
                    cur_pool[0] = (6, 7)
                    front_early(b + 1)
                    P.capture = None
                    cur_pool[0] = None
                elif b == 16:
                    P.capture = listP
                    cur_pool[0] = (6, 7)
                    for bb2 in range(16):
                        h2_block(bb2)
                    P.capture = None
                    cur_pool[0] = None
                P.capture = listP
                cur_pool[0] = (6, 7)
                w_out_part(0, 4)
                P.capture = None
                cur_pool[0] = None
                keyed = []
                for lst, off_, sc_ in ((listC, 0.0, 1.0), (listD, 0.0, 1.0), (listP, 0.0, 1.0)):
                    n_ = len(lst)
                    for i_, op_ in enumerate(lst):
                        keyed.append((off_ + sc_ * (i_ + 0.5) / n_, len(keyed), op_))
                keyed.sort(key=lambda t_: (t_[0], t_[1]))
                for _, _, op_ in keyed:
                    P.add(*op_)

                if DBG and l == 0 and b == DBGB:
                    dl = [("vD", vD[:], 256), ("qt", qt[:], 128), ("kt", kt[:], 128), ("qm", qm[:].rearrange("p h t -> p (h t)"), 512),
                          ("ktc", ktc[:, 0:2, :].rearrange("p h t -> p (h t)"), 256), ("attm", attm[:].rearrange("p h t -> p (h t)"), 512),
                          ("Gs", Gs[:], 128), ("e1", e1[:], 128), ("e2", e2[:], 128), ("egl", egl[:, 0:2], 2), ("G32", G32[0][:], 256),
                          ("Gp32", Gp32[:], 256), ("Gpbf", Gpbf[:], 256), ("qd", qd[:], 128), ("kd", kd[:], 128), ("gs", gs[:].rearrange("p h t -> p (h t)"), 256)]
                    dflat = dbgt[:].rearrange("p a b -> p (a b)")
                    for di_, (nm_, ap_, n_) in enumerate(dl):
                        cpy(dflat[:, 0:n_], ap_, [nm_], ["dbgt"])
                        dma(dbg2_o[di_, :, 0:n_], dflat[:, 0:n_], ["dbgt"], ())
                if DBG and l == 0 and b in (DBGB, 16):
                    cpy(dbgt[:], mixT[:], ["mixT%d" % k_ for k_ in range(8)], ["dbgt"])
                    dma(dbg_o[0 if b == DBGB else 1], dbgt[:], ["dbgt"], ())
                ck(100 * b + 7)
                w_out_part(4, 8)

            ck(5000)
            P.phase_switch()
            if l + 1 < NL:
                load_w_out(l + 1)
            h2_block(16)
            ai = 0
            for j in range(8):
                wb = j % 2
                dma(wup[wb][:], w_up_d[l, :, j * 512:(j + 1) * 512].rearrange("(kc p) f -> p kc f", p=128), (), ["wup%d" % wb], eng="pool")
                dma(wdn[wb][:], w_dn_d[l, j * 512:(j + 1) * 512, :].rearrange("(fc p) d -> p fc d", p=128), (), ["wdn%d" % wb], eng="pool")
                for (t0, n) in TIL:
                    cols = slice(t0, t0 + n)
                    a_ = ai % 2
                    ai += 1
                    for fm in range(4):
                        bkt, bkr = bank()
                        for kc in range(8):
                            mm(bkt[:, :n], wup[wb][:, kc, fm * 128:(fm + 1) * 128], h2T[:, kc, cols], ["wup%d" % wb, "w_in"], [bkr],
                               start=(kc == 0), stop=(kc == 7))
                        r_ = fm % 2
                        act(relu_t[r_][:, :n], bkt[:, :n], AF.Relu, [bkr], ["relu%d" % r_])
                        tt(actb[a_][:, fm, :n], relu_t[r_][:, :n], relu_t[r_][:, :n], ALU.mult, ["relu%d" % r_], ["actb%d" % a_])
                    for m in range(8):
                        bkt, bkr = bank()
                        for fc in range(4):
                            mm(bkt[:, :n], wdn[wb][:, fc, m * 128:(m + 1) * 128], actb[a_][:, fc, :n], ["wdn%d" % wb, "actb%d" % a_], [bkr],
                               start=(fc == 0), stop=(fc == 3))
                        tt(xT[:, m, cols], xT[:, m, cols], bkt[:, :n], ALU.add, ["x%d" % m, bkr], ["x%d" % m])
            P.phase_switch()
            if l + 1 < NL:
                load_w_in(l + 1)

        ck(6000)
        for bb2 in range(NBLK):
            cols = slice(bb2 * 128, bb2 * 128 + 128)
            rms_stats(cols, 128, XR)
            for kc in range(8):
                yo = yout[kc % 2]
                stt(yo[:], xT[:, kc, cols], gvec[:, 64 + kc:65 + kc], rstd[:, :128], ALU.mult, ALU.mult,
                    [XR[kc], "rstd", "gvec"], ["yout%d" % (kc % 2)])
                dma(yT_o[kc * 128:(kc + 1) * 128, cols], yo[:], ["yout%d" % (kc % 2)], ())

    except _Stop:
        pass

    P.emit(nc, stack)
    stack.close()
    return nc


def _consts():
    cst = np.zeros((128, 13, 128), np.float32)
    cst[:, 12] = 1.0
    i = np.arange(128)
    cst[:, 0] = np.eye(128)
    cst[:, 1] = 1.0 / 1024
    blk = (i[:, None] // 64 == i[None, :] // 64).astype(np.float32)
    cst[:, 2] = blk
    cst[:, 3] = blk / 64
    for ty, cs in ((0, 64), (1, 32)):
        same = (i[:, None] // cs == i[None, :] // cs)
        cst[:, 4 + ty] = (same & (i[:, None] <= i[None, :]))
        cst[:, 6 + ty] = same
        cst[:, 8 + ty] = -(same & (i[:, None] > i[None, :])).astype(np.float32)
        cst[:, 10 + ty] = np.broadcast_to((i % cs != 0).astype(np.float32)[None, :], (128, 128))
    cm = np.zeros((128, 2, 16), np.float32)
    for ty, cs in ((0, 64), (1, 32)):
        for c in range(128 // cs):
            cm[:, ty, c] = (i // cs == c)
            cm[:, ty, 4 + c] = -cm[:, ty, c]
        for h in range(4):
            cm[:, ty, 8 + h] = (i // 32 == h)
    invc = np.zeros((128, 2, 128), np.float32)
    t = np.arange(128)
    for g, win in enumerate((2, 4, 8, 16)):
        kc, p0 = g // 2, 64 * (g % 2)
        invc[p0:p0 + 64, kc, :] = 1.0 / np.minimum(win, t + 1)[None, :]
    return cst, cm, invc


def _colT(v):
    return v.reshape(v.shape[:-1] + (8, 128))


def kernel(x_prompt, x_sample, cache_pool, cache_attn_k, cache_attn_v, state_conv, state_delta,
           state_gla, attn_norm_g, w_in, pool_w, pool_scale, rel_bias, conv_w, a_log, dt_bias,
           delta_norm_g, gla_w_gk, gla_b_gk, gla_norm_g, w_out, mlp_norm_g, w_up, w_down,
           final_norm_g, _NL=DEPTH, _DBG=False, _CORES=NCORES, _STOP=None):
    f = np.float32
    cst, cm, invc = _consts()
    gvec = np.zeros((128, 72), f)
    gvec[:, 0:32] = attn_norm_g.reshape(4, 8, 128).transpose(2, 0, 1).reshape(128, 32)
    gvec[:, 32:64] = mlp_norm_g.reshape(4, 8, 128).transpose(2, 0, 1).reshape(128, 32)
    gvec[:, 64:72] = final_norm_g.reshape(8, 128).T
    pcol = np.zeros((128, 4, 8), f)
    pcol[:, :, 0:2] = pool_scale.reshape(4, 2, 128).transpose(2, 0, 1)
    pcol[:, :, 2] = np.concatenate([delta_norm_g, delta_norm_g], 1).T
    pcol[:, :, 3] = np.concatenate([gla_norm_g, gla_norm_g], 1).T
    pcol[:, :, 4] = gla_b_gk.T
    convw = np.ascontiguousarray(conv_w.reshape(4, 4, 6, 128).transpose(3, 0, 2, 1)).astype(f)
    hrow = np.zeros((128, 4, 8), f)
    hrow[:, :, 0:4] = a_log[None]
    hrow[:, :, 4:8] = dt_bias[None]
    poolw = np.zeros((4, 2, 128, 128), f)
    for g in range(4):
        kc, p0 = g // 2, 64 * (g % 2)
        poolw[:, kc, p0:p0 + 64, p0:p0 + 64] = pool_w[:, g]
    wgk = np.ascontiguousarray(gla_w_gk.transpose(1, 0, 2)).astype(f)
    NEG = f(-100.0)
    k = np.arange(128)[:, None]
    q = np.arange(128)[None, :]
    bp = np.zeros((4, 128, 5, 4, 128), f)
    for r in range(5):
        idx = np.clip((r - 4) * 128 + k - q, -256, 256) + 256
        tab = rel_bias[:, :, idx]
        tab = tab.transpose(0, 2, 1, 3).copy()
        if r == 0:
            tab[:, 0:64, :, 64:128] = NEG
        if r == 4:
            tab[:, 64:128, :, 0:64] = NEG
        bp[:, :, r] = tab
    bp = bp.reshape(4, 128, 5, 512)
    q32 = np.arange(32)[None, :]
    bsc = np.zeros((4, 128, 4, 4, 32), f)
    for r in range(4):
        idx = np.clip(r * 128 + k - 512 - q32, -256, 256) + 256
        bsc[:, :, r] = rel_bias[:, :, idx].transpose(0, 2, 1, 3)
    bsc = bsc.reshape(4, 128, 4, 128)
    bsn = np.full((4, 128, 4, 4, 32), NEG, f)
    kk = np.arange(32)[:, None]
    idx = np.clip(kk - q32, -256, 256) + 256
    tabn = rel_bias[:, :, idx].transpose(0, 2, 1, 3)
    for s in range(4):
        bsn[:, 32 * s:32 * s + 32, s] = tabn
    bsn = bsn.reshape(4, 128, 4, 128)

    shared = dict(w_in=np.ascontiguousarray(w_in, f), w_out=np.ascontiguousarray(w_out, f),
                  w_up=np.ascontiguousarray(w_up, f), w_down=np.ascontiguousarray(w_down, f),
                  gvec=gvec, pcol=pcol, convw=convw, hrow=hrow, poolw=poolw, wgk=wgk, bp=bp, bsc=bsc, bsn=bsn,
                  cst=cst, cm=cm, invc=invc)
    in_maps = []
    for c in range(NCORES):
        sl = slice(4 * c, 4 * c + 4)
        xs = np.concatenate([x_prompt[c], x_sample[sl].reshape(128, D)], 0)
        m = dict(shared)
        m["xT"] = np.ascontiguousarray(xs.T, f)
        m["cpool"] = np.ascontiguousarray(cache_pool[:, sl].transpose(0, 1, 3, 2), f)
        m["ck"] = np.ascontiguousarray(cache_attn_k[:, sl].transpose(0, 1, 2, 4, 3).reshape(4, 4, 256, 512), f)
        m["cv"] = np.ascontiguousarray(cache_attn_v[:, sl], f)
        m["sconv"] = np.ascontiguousarray(state_conv[:, sl].transpose(0, 1, 3, 2), f)
        m["sdel"] = np.ascontiguousarray(state_delta[:, sl].transpose(0, 1, 3, 2, 4), f)
        m["sgla"] = np.ascontiguousarray(state_gla[:, sl], f)
        in_maps.append(m)

    nc = build(_NL, _DBG, _STOP)
    res = run_bass_kernel_spmd(nc, in_maps[:_CORES], core_ids=list(range(_CORES))).results
    res = list(res) + [res[0]] * (NCORES - _CORES)
    global _last_res
    _last_res = res

    y_prompt = np.zeros((8, 2048, D), f)
    y_sample = np.zeros((32, 32, D), f)
    pool_p = np.zeros((4, 8, 15, 256), f); pool_s = np.zeros((4, 32, 15, 256), f)
    k_p = np.zeros((4, 8, 4, 512, 64), f); v_p = np.zeros((4, 8, 4, 512, 64), f)
    k_s = np.zeros((4, 32, 4, 32, 64), f); v_s = np.zeros((4, 32, 4, 32, 64), f)
    conv_p = np.zeros((4, 8, 3, 768), f); conv_s = np.zeros((4, 32, 3, 768), f)
    delta_p = np.zeros((4, 8, 4, 64, 64), f); delta_s = np.zeros((4, 32, 4, 64, 64), f)
    gla_p = np.zeros((4, 8, 4, 32, 64), f); gla_s = np.zeros((4, 32, 4, 32, 64), f)
    for c in range(NCORES):
        r = res[c]
        sl = slice(4 * c, 4 * c + 4)
        yt = r["yT"].T
        y_prompt[c] = yt[:2048]
        y_sample[sl] = yt[2048:].reshape(4, 32, D)
        po = r["pool_o"].transpose(0, 1, 3, 2)
        pool_p[:, c] = po[:, 0]; pool_s[:, sl] = po[:, 1:]
        ko = r["k_o"]
        k_p[:, c] = ko[:, :, :512].reshape(4, 4, 64, 512).transpose(0, 1, 3, 2)
        k_s[:, sl] = ko[:, :, 512:].reshape(4, 4, 64, 4, 32).transpose(0, 3, 1, 4, 2)
        vo = r["v_o"]
        v_p[:, c] = vo[:, :512].reshape(4, 512, 4, 64).transpose(0, 2, 1, 3)
        v_s[:, sl] = vo[:, 512:].reshape(4, 4, 32, 4, 64).transpose(0, 1, 3, 2, 4)
        co = r["conv_o"].transpose(0, 1, 3, 2)
        conv_p[:, c] = co[:, 0]; conv_s[:, sl] = co[:, 1:]
        do = r["del_o"].transpose(0, 1, 3, 2, 4)
        delta_p[:, c] = do[:, 0]; delta_s[:, sl] = do[:, 1:]
        go = r["gla_o"]
        gla_p[:, c] = go[:, 0]; gla_s[:, sl] = go[:, 1:]
    return (y_prompt, y_sample, pool_p, k_p, v_p, conv_p, delta_p, gla_p,
            pool_s, k_s, v_s, conv_s, delta_s, gla_s)
```

```python
import contextlib
import numpy as np
import concourse.bass as bass
import concourse.mybir as mybir
from concourse.bass_utils import run_bass_kernel_spmd

F32 = mybir.dt.float32
BF16 = mybir.dt.bfloat16
AF = mybir.ActivationFunctionType
ALU = mybir.AluOpType

NCORES = 8
DEPTH = 4
D = 1024
NT = 2176
NBLK = 17
INC = 2840
EPS = 1e-6
ENGS = ["pe", "act", "dve", "pool", "sp"]
RING = 8
OVW_ = 15400
DBGB = 12


class Op:
    __slots__ = ("eng", "fn", "deps", "sig", "sigval", "dma", "k")


class Prog:
    def __init__(self):
        self.ops = {e: [] for e in ENGS}
        self.last_w = {}
        self.readers = {}
        self.ndma = {e: 0 for e in ENGS}
        self.alias = {}
        self.ov_res = set()
        self.ov_recent = {e: [] for e in ENGS}
        self.fence = []
        self.last_acc = {}
        self.capture = None

    def phase_switch(self):
        f = []
        for e in ENGS:
            f.extend(self.ov_recent[e])
        self.fence = f
        self.ov_recent = {e: [] for e in ENGS}

    def add(self, eng, fn, rd=(), wr=(), dma=False):
        if self.capture is not None:
            self.capture.append((eng, fn, tuple(rd), tuple(wr), dma))
            return None
        op = Op()
        op.eng, op.fn, op.dma, op.sig, op.sigval, op.k = eng, fn, dma, False, 0, 0
        wr = list(wr)
        for r in list(wr):
            wr.extend(self.alias.get(r, ()))
        deps = set()
        for r in rd:
            w = self.last_w.get(r)
            if w is not None:
                deps.add(w)
        for r in wr:
            w = self.last_w.get(r)
            if w is not None:
                deps.add(w)
            for x in self.readers.get(r, ()):
                deps.add(x)
        for r in list(rd) + list(wr):
            if r.startswith("pb"):
                la = self.last_acc.get(r)
                if la is not None and la.eng != eng:
                    deps.add(la)
                self.last_acc[r] = op
        touches_ov = any(r in self.ov_res for r in rd) or any(r in self.ov_res for r in wr)
        if touches_ov:
            deps.update(self.fence)
        for r in rd:
            self.readers.setdefault(r, []).append(op)
        for r in wr:
            self.last_w[r] = op
            self.readers[r] = []
        deps.discard(op)
        op.deps = [d for d in deps if d.dma or d.eng != eng or eng != "pe"]
        if dma:
            op.k = self.ndma[eng]
            self.ndma[eng] += 1
        self.ops[eng].append(op)
        if touches_ov:
            lst = self.ov_recent[eng]
            lst.append(op)
            keep = RING + 1
            if len(lst) > keep:
                nd = [o for o in lst if not o.dma][-1:]
                dd = [o for o in lst if o.dma][-RING:]
                self.ov_recent[eng] = dd + nd
        return op

    def emit(self, nc, stack):
        for e in ENGS:
            for op in self.ops[e]:
                for d in op.deps:
                    d.sig = True
        esem = {e: stack.enter_context(nc.semaphore("es_" + e)) for e in ENGS}
        dsem = {e: [stack.enter_context(nc.semaphore("ds_%s_%d" % (e, i))) for i in range(RING)]
                for e in ENGS if self.ndma[e] > 0}
        for e in ENGS:
            c = 0
            for op in self.ops[e]:
                if op.sig and not op.dma:
                    c += 1
                    op.sigval = c
        block = stack.enter_context(nc.Block())

        def run(e, eng):
            waited = {}

            def wait(key, sem, val):
                if waited.get(key, 0) < val:
                    eng.wait_ge(sem, val)
                    waited[key] = val

            for op in self.ops[e]:
                for d in op.deps:
                    if d.dma:
                        wait(("d", d.eng, d.k % RING), dsem[d.eng][d.k % RING], 16 * (d.k // RING + 1))
                    else:
                        wait(("e", d.eng), esem[d.eng], d.sigval)
                if op.dma:
                    if op.k >= RING:
                        wait(("d", e, op.k % RING), dsem[e][op.k % RING], 16 * (op.k // RING))
                    op.fn(eng).then_inc(dsem[e][op.k % RING], 16)
                else:
                    ins = op.fn(eng)
                    if op.sig:
                        ins.then_inc(esem[e], 1)
            n = self.ndma[e]
            for s in range(min(n, RING)):
                last = ((n - 1 - s) // RING) * RING + s
                wait(("d", e, s), dsem[e][s], 16 * (last // RING + 1))

        @block.tensor
        def _(t):
            run("pe", t)

        @block.scalar
        def _(t):
            run("act", t)

        @block.vector
        def _(t):
            run("dve", t)

        @block.gpsimd
        def _(t):
            run("pool", t)

        @block.sync
        def _(t):
            run("sp", t)


def bc(ap, shape):
    return ap.broadcast_to(list(shape))


class _Stop(Exception):
    pass


def build(NL=DEPTH, DBG=False, STOP=None):
    nc = bass.Bass("TRN2", target_bir_lowering=False)
    P = Prog()
    stack = contextlib.ExitStack()

    def din(name, shape):
        return nc.dram_tensor(name, list(shape), F32, kind="ExternalInput").ap()

    def dout(name, shape):
        return nc.dram_tensor(name, list(shape), F32, kind="ExternalOutput").ap()

    xT_d = din("xT", [D, NT])
    w_in_d = din("w_in", [DEPTH, D, INC])
    w_out_d = din("w_out", [DEPTH, D, D])
    w_up_d = din("w_up", [DEPTH, D, 4096])
    w_dn_d = din("w_down", [DEPTH, 4096, D])
    gvec_d = din("gvec", [128, 72])
    pcol_d = din("pcol", [128, DEPTH, 8])
    convw_d = din("convw", [128, DEPTH, 6, 4])
    hrow_d = din("hrow", [128, DEPTH, 8])
    poolw_d = din("poolw", [DEPTH, 2, 128, 128])
    wgk_d = din("wgk", [16, DEPTH, 128])
    bp_d = din("bp", [DEPTH, 128, 5, 512])
    bsc_d = din("bsc", [DEPTH, 128, 4, 128])
    bsn_d = din("bsn", [DEPTH, 128, 4, 128])
    cst_d = din("cst", [128, 13, 128])
    cm_d = din("cm", [128, 2, 16])
    invc_d = din("invc", [128, 2, 128])
    cpool_d = din("cpool", [DEPTH, 4, 256, 15])
    ck_d = din("ck", [DEPTH, 4, 256, 512])
    cv_d = din("cv", [DEPTH, 4, 4, 512, 64])
    sconv_d = din("sconv", [DEPTH, 4, 768, 3])
    sdel_d = din("sdel", [DEPTH, 4, 64, 4, 64])
    sgla_d = din("sgla", [DEPTH, 4, 4, 32, 64])

    yT_o = dout("yT", [D, NT])
    pool_o = dout("pool_o", [DEPTH, 5, 256, 15])
    k_o = dout("k_o", [DEPTH, 256, 640])
    v_o = dout("v_o", [DEPTH, 640, 256])
    conv_o = dout("conv_o", [DEPTH, 5, 768, 3])
    del_o = dout("del_o", [DEPTH, 5, 64, 4, 64])
    gla_o = dout("gla_o", [DEPTH, 5, 4, 32, 64])
    dbg_o = dout("dbg_o", [2, 128, 8, 128]) if DBG else None
    dbg2_o = dout("dbg2_o", [16, 128, 1024]) if DBG else None

    def sb(name, shape, dt=F32):
        return stack.enter_context(nc.sbuf_tensor("s_" + name, list(shape), dt))

    xT = sb("xT", [128, 8, NT])
    w_in = sb("w_in_sb", [128, 8, INC], BF16)
    w_out = sb("w_out_sb", [128, 8, D], BF16)
    h2T = w_in[:].rearrange("p k c -> p (k c)")[:, 0:8 * NT].rearrange("p (k t) -> p k t", k=8)
    gvec = sb("gvec", [128, 72])
    pcol = sb("pcol", [128, DEPTH, 8])
    convw = sb("convw", [128, DEPTH, 6, 4])
    hrow = sb("hrow", [128, DEPTH, 8])
    negA = sb("negA", [128, 4])
    poolw = sb("poolw", [128, 2, 128], BF16)
    wgk = sb("wgk", [16, DEPTH, 128], BF16)
    bp = sb("bp", [128, 5, 512], BF16)
    bsc = sb("bsc", [128, 4, 128], BF16)
    bsn = sb("bsn", [128, 4, 128], BF16)
    cstf = sb("cstf", [128, 9, 128])
    cstb = sb("cstb", [128, 4, 128], BF16)
    cm = sb("cm", [128, 2, 16])
    invc = sb("invc", [128, 2, 128])
    sq = [sb("sq%d" % i, [128, 128], BF16) for i in range(2)]
    lnt = sb("lnt", [128, 512])
    rstd = sb("rstd", [128, 128])

    OVW = OVW_
    OV = sb("ov", [128, OVW])
    ov_off = [0]
    ov_offs = {}
    P.alias = {}

    def ovview(off, words, shape, dt, parts):
        v = OV[0:parts, off:off + words]
        if dt != F32:
            v = v.bitcast(dt)
        if len(shape) == 3:
            v = v.rearrange("p (a b) -> p a b", a=shape[1])
        return v

    def ova(name, shape, dt=F32, parts=128):
        n = 1
        for d_ in shape[1:]:
            n *= d_
        words = n if dt == F32 else (n + 1) // 2
        off = ov_off[0]
        assert off + words <= OVW, ("overlay overflow", name, off, words)
        ov_off[0] = off + words
        ov_offs[name] = (off, words)
        P.ov_res.add(name)
        return ovview(off, words, shape, dt, parts)

    def ovat(name, off, shape, dt=F32, parts=128):
        n = 1
        for d_ in shape[1:]:
            n *= d_
        words = n if dt == F32 else (n + 1) // 2
        ov_offs[name] = (off, words)
        P.ov_res.add(name)
        return ovview(off, words, shape, dt, parts)

    def union(host, members):
        P.alias[host] = list(members)
        for m_ in members:
            P.alias[m_] = [host]

    hT = ova("hT", [128, 8, 128], BF16)
    mixT = ova("mixT", [128, 8, 128], BF16)
    uext = ova("uext", [128, 2, 192])
    pwa = ova("pwa", [128, 2, 192])
    pwb = ova("pwb", [128, 2, 192])
    sqo = ovat("sqo", ov_offs["pwa"][0], [128, 256], BF16)
    tno = ovat("tno", ov_offs["pwa"][0] + 128, [128, 256])
    union("pwa", ["sqo", "tno"])
    Sdec = ovat("Sdec", ov_offs["pwb"][0], [64, 4, 64], parts=64)
    union("pwb", ["Sdec"])
    pooled = ova("pooled", [128, 2, 128], BF16)
    qb = ova("qb", [128, 2, 128], BF16)
    kbT = ova("kbT", [128, 2, 640], BF16)
    for i in range(5):
        P.ov_res.add("kbT%d" % i)
    vb1 = ova("vb1", [128, 5, 384], BF16)
    tS = ova("tS", [128, 512])
    rc = tS
    QKm = ovat("QKm", ov_offs["tS"][0], [128, 4, 128], BF16)
    wT = ovat("wT", ov_offs["tS"][0] + 256, [64, 4, 128], BF16, parts=64)
    union("tS", ["QKm", "wT"])
    cext = ova("cext", [128, 6, 144])
    cacc = ova("cacc", [128, 6, 128])
    sqc = ova("sqc", [128, 4, 128], BF16)
    qnT = ova("qnT", [128, 2, 128], BF16)
    knT = ova("knT", [128, 2, 128], BF16)
    vcT = ova("vcT", [128, 2, 128], BF16)
    zs = ova("zs", [128, 2, 128], BF16)
    gs = ova("gs", [128, 2, 128], BF16)
    ab = ova("ab", [128, 8])
    sm = ova("sm", [128, 64])
    gm = ova("gm", [128, 16])
    glr = ova("glr", [64, 16], parts=64)
    eglr = ova("eglr", [64, 16], parts=64)
    qgT = ova("qgT", [64, 4, 128], BF16, parts=64)
    Ee = ova("Ee", [128, 4, 128])
    qm = ova("qm", [128, 4, 128], BF16)
    ktc = ova("ktc", [128, 4, 128], BF16)
    Em = ova("Em", [128, 4, 128])
    sqoD = ovat("sqoD", ov_offs["Ee"][0], [128, 256], BF16)
    tnoD = ovat("tnoD", ov_offs["Ee"][0] + 128, [128, 256])
    lntD = ovat("lntD", ov_offs["Em"][0], [128, 256])
    union("Ee", ["sqoD", "tnoD"])
    union("Em", ["lntD"])
    attm = ova("attm", [128, 4, 128], BF16)
    Gp32 = ova("Gp32", [128, 256])
    Lp = [ova("Lp%d" % i, [128, 4, 128], BF16) for i in range(2)]
    Bpw = [ova("Bpw%d" % i, [128, 4, 128], BF16) for i in range(2)]
    Yb = [ova("Yb%d" % i, [128, 4, 128], BF16) for i in range(2)]
    kcs = ovat("kcs", ov_offs["Lp0"][0], [128, 2, 512], BF16)
    vcs = ovat("vcs", ov_offs["Bpw0"][0], [128, 4, 384], BF16)
    union("kcs", ["Lp0", "Lp1"])
    union("vcs", ["Bpw0", "Bpw1", "Yb0"])
    pT1 = ovat("pT", ov_offs["Yb1"][0], [128, 512], BF16)
    union("Yb1", ["pT"])
    ktil = ova("ktil", [128, 4, 64], BF16)
    Y32 = ova("Y32", [128, 4, 128])
    kf32 = ovat("kf32", ov_offs["Y32"][0], [128, 2, 128])
    vf32 = ovat("vf32", ov_offs["Y32"][0] + 256, [128, 256])
    union("Y32", ["kf32", "vf32"])
    tmpu = ova("tmpu", [128, 256])
    spd = ova("spd", [128, 128])
    Gpbf = ova("Gpbf", [128, 256], BF16)
    unc = ova("unc", [128, 4, 256], BF16)
    Gs = ova("Gs", [128, 128])
    Dm = ova("Dm", [128, 128])
    e1 = ova("e1", [128, 128])
    e2 = ova("e2", [128, 128])
    S32 = [ova("S32_%d" % i, [64, 4, 64], parts=64) for i in range(2)]
    Sbf = [ova("Sbf_%d" % i, [64, 4, 64], BF16, parts=64) for i in range(2)]
    qd = ova("qd", [128, 128])
    kd = ova("kd", [128, 128])
    gkl = ova("gkl", [16, 128], BF16, parts=16)
    egl = ova("egl", [128, 4])
    qt = ova("qt", [128, 128], BF16)
    kt = ova("kt", [128, 128], BF16)
    vD = ova("vD", [128, 256], BF16)
    G32 = [ova("G32_%d" % i, [128, 256]) for i in range(2)]
    dbgt = ova("dbgt", [128, 8, 128]) if DBG else None
    lntP = ova("lntP", [128, 128])
    mixer_words = ov_off[0]
    for nm_, k_ in (("hT", 8), ("mixT", 8), ("cacc", 6), ("cext", 6), ("unc", 4)):
        for i_ in range(k_):
            P.ov_res.add("%s%d" % (nm_, i_))
    for i_ in range(5):
        P.ov_res.add("vb1_%d" % i_)
    ov_off[0] = 0
    wup = [ova("wup%d" % i, [128, 8, 512], BF16) for i in range(2)]
    wdn = [ova("wdn%d" % i, [128, 4, D], BF16) for i in range(2)]
    actb = [ova("actb%d" % i, [128, 4, 512], BF16) for i in range(2)]
    relu_t = [ova("relu%d" % i, [128, 512], BF16) for i in range(2)]
    yout = [ova("yout%d" % i, [128, 128]) for i in range(2)]
    mlp_words = ov_off[0]
    for i_ in range(2):
        for k_ in range(4):
            P.ov_res.add("wup%d_%d" % (i_, k_))
            P.ov_res.add("wdn%d_%d" % (i_, k_))
    print("overlay words: mixer", mixer_words, "mlp", mlp_words, "of", OVW)

    banks = [stack.enter_context(nc.psum_tensor("pb%d" % i, [128, 512], F32)) for i in range(8)]
    bk_i = [0]

    bank_sess = {}

    held = set()
    cur_pool = [None]
    pool_i = {}

    def bank(hold=False):
        pool = cur_pool[0]
        while True:
            if pool is None:
                i = bk_i[0] % 8
                bk_i[0] += 1
            else:
                k_ = pool_i.get(pool, 0)
                pool_i[pool] = k_ + 1
                i = pool[k_ % len(pool)]
            if ("pb%d" % i) not in held:
                break
        if hold:
            held.add("pb%d" % i)
        bank_sess["pb%d" % i] = set()
        return banks[i], "pb%d" % i

    def mm(out, lhsT, rhs, rd, wr, start=True, stop=True):
        st = bank_sess[wr[0]]
        base = out.base_partition()
        quads = set(range(base // 32, (base + out.shape[0] + 31) // 32))
        newq = quads - st
        if newq:
            assert newq == quads, ("mixed psum quadrants", wr, base, out.shape)
            s_ = True
            st |= quads
        else:
            s_ = False
        P.add("pe", lambda e: e.matmul(out, lhsT, rhs, start=s_, stop=True, skip_group_check=True), rd, wr)

    def act(out, in_, func, rd, wr, bias=None, scale=None):
        kw = {}
        if bias is not None:
            kw["bias"] = bias
        if scale is not None:
            kw["scale"] = scale
        P.add("act", lambda e: e.activation(out, in_, func, **kw), rd, wr)

    def tt(out, in0, in1, op, rd, wr, eng="dve"):
        P.add(eng, lambda e: e.tensor_tensor(out=out, in0=in0, in1=in1, op=op), rd, wr)

    def ts(out, in0, s1, op0, rd, wr, s2=None, op1=None, eng="dve"):
        if op1 is None:
            P.add(eng, lambda e: e.tensor_scalar(out=out, in0=in0, scalar1=s1, scalar2=None, op0=op0), rd, wr)
        else:
            P.add(eng, lambda e: e.tensor_scalar(out=out, in0=in0, scalar1=s1, scalar2=s2, op0=op0, op1=op1), rd, wr)

    def stt(out, in0, scalar, in1, op0, op1, rd, wr):
        P.add("dve", lambda e: e.scalar_tensor_tensor(out=out, in0=in0, scalar=scalar, in1=in1, op0=op0, op1=op1), rd, wr)

    def cpy(out, in_, rd, wr, eng="dve"):
        P.add(eng, lambda e: e.tensor_copy(out, in_), rd, wr)

    def mset(ap, val, wr, eng="dve"):
        P.add(eng, lambda e: e.memset(ap, val), (), wr)

    def dma(out, in_, rd, wr, eng="sp"):
        P.add(eng, lambda e: e.dma_start(out=out, in_=in_), rd, wr, dma=True)

    def scan(out, d0, d1, rd, wr):
        P.add("dve", lambda e: e.tensor_tensor_scan(out=out, data0=d0, data1=d1, initial=0.0, op0=ALU.mult, op1=ALU.add), rd, wr)

    def recip(out, in_, rd, wr):
        P.add("dve", lambda e: e.reciprocal(out, in_), rd, wr)

    IDENT, ONESN, BLK1, BLK64, ONES = 0, 1, 2, 3, 8
    VOFF = [0, 64, 192, 256]
    VCOL = [0, 128, 192, 320]

    def ck(n):
        if STOP is not None and STOP == n:
            raise _Stop()

    try:
        for kc in range(8):
            dma(xT[:, kc, :], xT_d[kc * 128:(kc + 1) * 128, :], (), ["x%d" % kc])
        XR = ["x%d" % k for k in range(8)]
        dma(gvec[:], gvec_d, (), ["gvec"])
        dma(pcol[:], pcol_d, (), ["pcol"])
        ts(pcol[:, :, 5], pcol[:, :, 4], -1.0, ALU.mult, ["pcol"], ["pcol"])
        dma(convw[:], convw_d, (), ["convw"])
        dma(hrow[:], hrow_d, (), ["hrow"])
        dma(cstf[:], cst_d[:, 4:13, :], (), ["cstf"])
        dma(cm[:], cm_d, (), ["cm"])
        dma(invc[:], invc_d, (), ["invc"])
        dma(cstb[:], cst_d[:, 0:4, :], (), ["cstb"], eng="pool")
        dma(wgk[:], wgk_d, (), ["wgk"], eng="pool")

        def load_w_in(l):
            for kc in range(8):
                dma(w_in[:, kc, :], w_in_d[l, kc * 128:(kc + 1) * 128, :], (), ["w_in"], eng="pool")

        def load_w_out(l):
            for kc in range(8):
                dma(w_out[:, kc, :], w_out_d[l, kc * 128:(kc + 1) * 128, :], (), ["w_out"], eng="pool")

        load_w_in(0)
        load_w_out(0)

        def rms_stats(cols, n, xres, lbuf=None, lres="lnt"):
            if lbuf is None:
                lbuf = lnt
            bkt, bkr = bank()
            for kc in range(8):
                s = sq[kc % 2]
                act(s[:, :n], xT[:, kc, cols], AF.Square, [xres[kc]], ["sq%d" % (kc % 2)])
                mm(bkt[:, :n], cstb[:, ONESN, :], s[:, :n], ["sq%d" % (kc % 2), "cstb"], [bkr], start=(kc == 0), stop=(kc == 7))
            act(lbuf[:, :n], bkt[:, :n], AF.Ln, [bkr], [lres], bias=EPS)
            act(rstd[:, :n], lbuf[:, :n], AF.Exp, [lres], ["rstd"], scale=-0.5)

        TIL = [(0, 512), (512, 512), (1024, 512), (1536, 512), (2048, 128)]

        for l in range(NL):
            dma(bp[:], bp_d[l], (), ["bp"], eng="pool")
            dma(bsc[:], bsc_d[l], (), ["bsc"], eng="pool")
            dma(bsn[:], bsn_d[l], (), ["bsn"], eng="pool")
            dma(poolw[:], poolw_d[l].rearrange("k p c -> p k c"), (), ["poolw"], eng="pool")
            act(negA[:], hrow[:, l, 0:4], AF.Exp, ["hrow"], ["negA"])
            ts(negA[:], negA[:], -1.0, ALU.mult, ["negA"], ["negA"])
            mset(vb1[:], 1.0, ["vb1_%d" % s_ for s_ in range(5)])
            mset(S32[0][:], 0.0, ["S32_0"])
            mset(Sbf[0][:], 0.0, ["Sbf_0"])
            mset(G32[0][:], 0.0, ["G32_0"])

            def h2_block(bb2):
                cols2 = slice(bb2 * 128, bb2 * 128 + 128)
                rms_stats(cols2, 128, XR, lntP, "lntP")
                for kc in range(8):
                    stt(h2T[:, kc, cols2], xT[:, kc, cols2], gvec[:, 32 + l * 8 + kc:32 + l * 8 + kc + 1], rstd[:, :128],
                        ALU.mult, ALU.mult, [XR[kc], "rstd", "gvec"], ["w_in"])

            for b in range(NBLK):
                smp = (b == 16)
                ty = 1 if smp else 0
                nch = 4 if smp else 2
                cs = 32 if smp else 64
                cols = slice(b * 128, (b + 1) * 128)
                TRI, SAME, MSLN, RST = 0 + ty, 2 + ty, 4 + ty, 6 + ty
                xres = XR
                CX = ["cext%d" % g_ for g_ in range(6)]

                if smp:
                    uxs = uext[:, :, 0:188].rearrange("p g (s t) -> p g s t", s=4)
                    cxs = cext[:, :, 0:140].rearrange("p g (s t) -> p g s t", s=4)
                slot = b % 5
                need_kv = smp or b >= 12

                def proj(c0, M):
                    bkt, bkr = bank()
                    for kc in range(8):
                        mm(bkt[0:M, 0:128], w_in[:, kc, c0:c0 + M], hT[:, kc, :], ["w_in", "hT%d" % kc], [bkr],
                           start=(kc == 0), stop=(kc == 7))
                    return bkt, bkr

                def front_early(bq):
                    smq = (bq == 16)
                    colq = slice(bq * 128, (bq + 1) * 128)
                    slq = bq % 5
                    nkv = smq or bq >= 12
                    rms_stats(colq, 128, XR, lntP, "lntP")
                    for kc in range(8):
                        stt(hT[:, kc, :], xT[:, kc, colq], gvec[:, l * 8 + kc:l * 8 + kc + 1], rstd[:, :128],
                            ALU.mult, ALU.mult, [XR[kc], "rstd", "gvec"], ["hT%d" % kc])
                    if smq:
                        for g in range(2):
                            dma(uxs[:, g, :, 0:15], cpool_d[l, :, g * 128:(g + 1) * 128, :].rearrange("s p t -> p s t"), (), ["uext"])
                        for g in range(6):
                            dma(cxs[:, g, :, 0:3], sconv_d[l, :, g * 128:(g + 1) * 128, :].rearrange("s p t -> p s t"), (), ["cext%d" % g])
                    elif bq == 0:
                        mset(uext[:, :, 0:15], 0.0, ["uext"])
                        mset(cext[:, :, 0:3], 0.0, CX)
                    else:
                        cpy(uext[:, :, 0:15], uext[:, :, 128:143], ["uext"], ["uext"])
                        cpy(cext[:, :, 0:3], cext[:, :, 128:131], CX, CX)
                    for g in range(2):
                        bkt, bkr = proj(g * 128, 128)
                        if smq:
                            act(uxs[:, g, :, 15:47], bkt[:, 0:128].rearrange("p (s t) -> p s t", s=4), AF.Copy, [bkr], ["uext"])
                        else:
                            act(uext[:, g, 15:143], bkt[:, 0:128], AF.Copy, [bkr], ["uext"])
                    for g in range(2):
                        bkt, bkr = proj(256 + g * 128, 128)
                        act(qb[:, g, :], bkt[:, 0:128], AF.Copy, [bkr], ["qb"], scale=0.125)
                    for g in range(2):
                        bkt, bkr = proj(512 + g * 128, 128)
                        act(kbT[:, g, slq * 128:(slq + 1) * 128], bkt[:, 0:128], AF.Copy, [bkr], ["kbT%d" % slq])
                    for g in range(6):
                        bkt, bkr = proj(1024 + g * 128, 128)
                        if smq:
                            cpy(cxs[:, g, :, 3:35], bkt[:, 0:128].rearrange("p (s t) -> p s t", s=4), [bkr], ["cext%d" % g])
                        else:
                            cpy(cext[:, g, 3:131], bkt[:, 0:128], [bkr], ["cext%d" % g])
                    bkt, bkr = proj(2056, 128)
                    act(qd[:], bkt[:, 0:128], AF.Copy, [bkr], ["qd"], scale=float(32 ** -0.5))
                    bkt, bkr = proj(2184, 128)
                    cpy(kd[:], bkt[:, 0:128], [bkr], ["kd"])
                    bkt, bkr = proj(2824, 16)
                    cpy(gkl[:, :], bkt[0:16, 0:128], [bkr], ["gkl"])
                    for g in range(6):
                        for j in range(4):
                            if smq:
                                src = cxs[:, g, :, j:j + 32]
                                dst = cacc[:, g, :].rearrange("p (s t) -> p s t", s=4)
                            else:
                                src = cext[:, g, j:j + 128]
                                dst = cacc[:, g, :]
                            if j == 0:
                                ts(dst, src, convw[:, l, g, 0:1], ALU.mult, ["cext%d" % g, "convw"], ["cacc%d" % g])
                            else:
                                stt(dst, src, convw[:, l, g, j:j + 1], dst, ALU.mult, ALU.add, ["cext%d" % g, "convw", "cacc%d" % g], ["cacc%d" % g])

                def front_late(bq):
                    slq = bq % 5
                    nkv = (bq == 16) or bq >= 12
                    if nkv:
                        for g in range(2):
                            bkt, bkr = proj(512 + g * 128, 128)
                            act(kf32[:, g, :], bkt[:, 0:128], AF.Copy, [bkr], ["kf32"])
                    for g in range(2):
                        bkt, bkr = proj(1792 + g * 128, 128)
                        act(zs[:, g, :], bkt[:, 0:128], AF.Silu, [bkr], ["zs"])
                    for g in range(2):
                        bkt, bkr = proj(2568 + g * 128, 128)
                        act(gs[:, g, :], bkt[:, 0:128], AF.Silu, [bkr], ["gs"])
                    bkt, bkr = bank()
                    bk2, bk2r = bank()
                    for kc in range(8):
                        mm(bkt[:, 0:256], hT[:, kc, :], w_in[:, kc, 768:1024], ["w_in", "hT%d" % kc], [bkr], start=(kc == 0), stop=(kc == 7))
                        mm(bkt[:, 256:512], hT[:, kc, :], w_in[:, kc, 2312:2568], ["w_in", "hT%d" % kc], [bkr], start=(kc == 0), stop=(kc == 7))
                        mm(bk2[:, 0:8], hT[:, kc, :], w_in[:, kc, 2048:2056], ["w_in", "hT%d" % kc], [bk2r], start=(kc == 0), stop=(kc == 7))
                    for h in range(4):
                        cpy(vb1[:, slq, VCOL[h]:VCOL[h] + 64], bkt[:, 64 * h:64 * h + 64], [bkr], ["vb1_%d" % slq])
                    if nkv:
                        act(vf32[:], bkt[:, 0:256], AF.Copy, [bkr], ["vf32"])
                    act(vD[:], bkt[:, 256:512], AF.Copy, [bkr], ["vD"])
                    cpy(ab[:], bk2[:, 0:8], [bk2r], ["ab"])

                def w_out_part(k0, k1):
                    for m in range(8):
                        bkt, bkr = bank()
                        for kc in range(k0, k1):
                            mm(bkt[:, 0:128], w_out[:, kc, m * 128:(m + 1) * 128], mixT[:, kc, :], ["w_out", "mixT%d" % kc], [bkr],
                               start=(kc == k0), stop=(kc == k1 - 1))
                        tt(xT[:, m, cols], xT[:, m, cols], bkt[:, 0:128], ALU.add, ["x%d" % m, bkr], ["x%d" % m])

                prefetched = (1 <= b <= 15)
                if not prefetched:
                    front_early(b)
                do_fab = (1 <= b <= 15)
                listF, listO, listA, listB = [], [], [], []
                P.capture = listF
                cur_pool[0] = (0, 1, 2) if do_fab else None
                front_late(b)
                P.capture = listO

                ck(100 * b + 2)
                if need_kv:
                    c0 = 512 if smp else (b - 12) * 128
                    for g in range(2):
                        dma(k_o[l, g * 128:(g + 1) * 128, c0:c0 + 128], kf32[:, g, :], ["kf32"], ())
                    dma(v_o[l, c0:c0 + 128, :], vf32[:], ["vf32"], ())
                if smp:
                    for g in range(2):
                        dma(pool_o[l, 1:5, g * 128:(g + 1) * 128, :].rearrange("s p t -> p s t"), uxs[:, g, :, 32:47], ["uext"], ())
                    for g in range(6):
                        dma(conv_o[l, 1:5, g * 128:(g + 1) * 128, :].rearrange("s p t -> p s t"), cxs[:, g, :, 32:35], ["cext%d" % g], ())
                elif b == 15:
                    for g in range(2):
                        dma(pool_o[l, 0, g * 128:(g + 1) * 128, :], uext[:, g, 128:143], ["uext"], ())
                    for g in range(6):
                        dma(conv_o[l, 0, g * 128:(g + 1) * 128, :], cext[:, g, 128:131], ["cext%d" % g], ())

                P.capture = listA
                cur_pool[0] = (3,) if do_fab else None
                if smp:
                    def V(t, a, bb):
                        return t[:, :, 0:188].rearrange("p g (s t) -> p g s t", s=4)[:, :, :, a:bb]
                    L_ = 47
                else:
                    def V(t, a, bb):
                        return t[:, :, a:bb]
                    L_ = 143
                wins = [(uext, pwa, 1), (pwa, pwb, 2), (pwb, pwa, 4), (pwa, pwb, 8)]
                for gi, (src, dst, sh) in enumerate(wins):
                    lo = 2 * sh - 1
                    dres = "pwa" if dst is pwa else "pwb"
                    tt(V(dst, lo, L_), V(src, lo, L_), V(src, lo - sh, L_ - sh), ALU.add,
                       ["uext", "pwa", "pwb"], [dres])
                    kc, p0 = gi // 2, 64 * (gi % 2)
                    win = 2 * sh
                    if smp:
                        o_ = pooled[p0:p0 + 64, kc, :].rearrange("p (s t) -> p s t", s=4)
                        i0 = V(dst, 15, 47)[p0:p0 + 64, kc]
                        i1 = V(uext, 15, 47)[p0:p0 + 64, kc]
                    else:
                        o_ = pooled[p0:p0 + 64, kc, :]
                        i0 = dst[p0:p0 + 64, kc, 15:143]
                        i1 = uext[p0:p0 + 64, kc, 15:143]
                    if b == 0:
                        tt(tS[p0:p0 + 64, 0:128], i0, invc[p0:p0 + 64, kc, :], ALU.mult, ["invc", dres], ["tS"])
                        tt(o_, tS[p0:p0 + 64, 0:128], i1, ALU.subtract, ["uext", "tS"], ["pooled"])
                    else:
                        stt(o_, i0, 1.0 / win, i1, ALU.mult, ALU.subtract, ["uext", dres], ["pooled"])
                for kc in range(2):
                    bkt, bkr = bank()
                    mm(bkt[:, 0:128], poolw[:, kc, :], pooled[:, kc, :], ["poolw", "pooled"], [bkr])
                    ts(mixT[:, kc, :], bkt[:, 0:128], pcol[:, l, kc:kc + 1], ALU.mult, [bkr, "pcol"], ["mixT%d" % kc])

                P.capture = listB
                cur_pool[0] = (4, 5, 6, 7) if do_fab else None
                ob, obr = bank(hold=True)
                if not smp:
                    rlist = [r for r in range(5) if b - 4 + r >= 0]
                    for ri, r in enumerate(rlist):
                        ks = (b - 4 + r) % 5
                        scp = [bank(), bank()]
                        for h in range(4):
                            p0, g = 64 * (h % 2), h // 2
                            sc, scr = scp[h % 2]
                            mm(sc[:, 128 * g:128 * g + 128], kbT[p0:p0 + 64, g, ks * 128:(ks + 1) * 128], qb[p0:p0 + 64, g, :],
                               ["kbT%d" % ks, "qb"], [scr])
                        i2 = ri % 2
                        ck(100 * b + 41)
                        for par in range(2):
                            sc, scr = scp[par]
                            tt(tS[:].rearrange("p (h q) -> p h q", h=4)[:, par::2, :], sc[:, 0:256].rearrange("p (h q) -> p h q", h=2),
                               bp[:, r, :].rearrange("p (h q) -> p h q", h=4)[:, par::2, :], ALU.add, [scr, "bp"], ["tS"])
                        ck(100 * b + 42)
                        act(pT1[:], tS[:], AF.Exp, ["tS"], ["pT"])
                        ck(100 * b + 43)
                        for h in range(4):
                            mm(ob[:, 128 * h:128 * h + 128], vb1[:, ks, VOFF[h]:VOFF[h] + 128], pT1[:, 128 * h:128 * h + 128],
                               ["vb1_%d" % ks, "pT"], [obr], start=(ri == 0), stop=(ri == len(rlist) - 1))
                    ck(100 * b + 44)
                    for h in range(4):
                        po, ps_ = (0, 64) if h % 2 == 0 else (64, 0)
                        recip(rc[po:po + 64, 128 * h:128 * h + 128], ob[ps_:ps_ + 64, 128 * h:128 * h + 128], [obr], ["tS"])
                        ck(100 * b + 45)
                        tt(mixT[po:po + 64, 2 + h // 2, :], ob[po:po + 64, 128 * h:128 * h + 128], rc[po:po + 64, 128 * h:128 * h + 128],
                           ALU.mult, [obr, "tS"], ["mixT%d" % (2 + h // 2)])
                else:
                    for s in range(4):
                        dma(kcs[:], ck_d[l, s].rearrange("(g p) t -> p g t", p=128), (), ["kcs"], eng="pool")
                        mset(vcs[:], 1.0, ["vcs"])
                        for h in range(4):
                            dma(vcs[:, :, VCOL[h]:VCOL[h] + 64], cv_d[l, s, h].rearrange("(r p) d -> p r d", p=128), (), ["vcs"], eng="pool")
                        for r in range(5):
                            scp = [bank(), bank()]
                            for h in range(4):
                                p0, g = 64 * (h % 2), h // 2
                                sc, scr = scp[h % 2]
                                if r < 4:
                                    lhsT = kcs[p0:p0 + 64, g, r * 128:(r + 1) * 128]
                                    rdk = "kcs"
                                else:
                                    lhsT = kbT[p0:p0 + 64, g, slot * 128:(slot + 1) * 128]
                                    rdk = "kbT%d" % slot
                                mm(sc[:, 32 * g:32 * g + 32], lhsT, qb[p0:p0 + 64, g, 32 * s:32 * s + 32], [rdk, "qb"], [scr])
                            i2 = r % 2
                            tab = bsc[:, r, :] if r < 4 else bsn[:, s, :]
                            for par in range(2):
                                sc, scr = scp[par]
                                tt(tS[:, 0:128].rearrange("p (h q) -> p h q", h=4)[:, par::2, :], sc[:, 0:64].rearrange("p (h q) -> p h q", h=2),
                                   tab.rearrange("p (h q) -> p h q", h=4)[:, par::2, :], ALU.add, [scr, "bsc", "bsn"], ["tS"])
                            act(pT1[:, 0:128], tS[:, 0:128], AF.Exp, ["tS"], ["pT"])
                            for h in range(4):
                                lhsT = vcs[:, r, VOFF[h]:VOFF[h] + 128] if r < 4 else vb1[:, slot, VOFF[h]:VOFF[h] + 128]
                                mm(ob[:, 128 * s + 32 * h:128 * s + 32 * h + 32], lhsT, pT1[:, 32 * h:32 * h + 32],
                                   ["vcs", "vb1_%d" % slot, "pT"], [obr], start=(r == 0), stop=(r == 4))
                    obv = ob[:].rearrange("p (s h q) -> p s h q", s=4, h=4)
                    rcv = rc[:].rearrange("p (s h q) -> p s h q", s=4, h=4)
                    for h in range(4):
                        po, ps_ = (0, 64) if h % 2 == 0 else (64, 0)
                        recip(rcv[po:po + 64, :, h, :], obv[ps_:ps_ + 64, :, h, :], [obr], ["tS"])
                        tt(mixT[po:po + 64, 2 + h // 2, :].rearrange("p (s q) -> p s q", s=4), obv[po:po + 64, :, h, :],
                           rcv[po:po + 64, :, h, :], ALU.mult, [obr, "tS"], ["mixT%d" % (2 + h // 2)])

                held.discard(obr)
                P.capture = None
                cur_pool[0] = None
                if do_fab:
                    keyed = []
                    for lst, off_, sc_ in ((listF, 0.0, 0.5), (listA, 0.0, 1.0), (listB, 0.0, 1.0)):
                        n_ = len(lst)
                        for i_, op_ in enumerate(lst):
                            keyed.append((off_ + sc_ * (i_ + 0.5) / n_, len(keyed), op_))
                    keyed.sort(key=lambda t_: (t_[0], t_[1]))
                    for _, _, op_ in keyed:
                        P.add(*op_)
                    for op_ in listO:
                        P.add(*op_)
                else:
                    for lst in (listF, listO, listA, listB):
                        for op_ in lst:
                            P.add(*op_)
                listC = []
                P.capture = listC
                cur_pool[0] = (0, 1, 2, 3)
                tyC, nchC, csC = 1, 4, 32
                TRIc, SAMEc, MSLNc = 0 + tyC, 2 + tyC, 4 + tyC
                act(vcT[:], cacc[:, 4:6, :], AF.Silu, ["cacc4", "cacc5"], ["vcT"])
                csl = cacc[:, 0:4, :]
                CQ = ["cacc0", "cacc1", "cacc2", "cacc3"]
                act(csl, csl, AF.Silu, CQ, CQ)
                act(sqc[:], csl, AF.Square, CQ, ["sqc"])
                bkt, bkr = bank()
                for g in range(4):
                    mm(bkt[:, 128 * g:128 * g + 128], cstb[:, BLK1, :], sqc[:, g, :], ["cstb", "sqc"], [bkr])
                act(lnt[:], bkt[:], AF.Ln, [bkr], ["lnt"], bias=EPS)
                act(lnt[:], lnt[:], AF.Exp, ["lnt"], ["lnt"], scale=-0.5)
                rn = lnt[:].rearrange("p (g t) -> p g t", g=4)
                stt(qnT[:], csl[:, 0:2, :], 0.125, rn[:, 0:2, :], ALU.mult, ALU.mult, ["cacc0", "cacc1", "lnt"], ["qnT"])
                tt(knT[:], csl[:, 2:4, :], rn[:, 2:4, :], ALU.mult, ["cacc2", "cacc3", "lnt"], ["knT"])
                btr, btrr = bank()
                for g in range(2):
                    mm(btr[:, 128 * g:128 * g + 128], vcT[:, g, :], cstb[:, IDENT, :], ["vcT", "cstb"], [btrr])
                    mm(btr[:, 256 + 128 * g:256 + 128 * g + 128], knT[:, g, :], cstb[:, IDENT, :], ["knT", "cstb"], [btrr])
                tt(sm[:, 0:4], ab[:, 0:4], hrow[:, l, 4:8], ALU.add, ["ab", "hrow"], ["sm"])
                act(sm[:, 4:8], sm[:, 0:4], AF.Exp, ["sm"], ["sm"])
                act(sm[:, 8:12], sm[:, 4:8], AF.Ln, ["sm"], ["sm"], bias=1.0)
                tt(sm[:, 12:16], sm[:, 8:12], negA[:], ALU.mult, ["sm", "negA"], ["sm"])
                act(sm[:, 16:20], ab[:, 4:8], AF.Exp, ["ab"], ["sm"], scale=-1.0)
                ts(sm[:, 16:20], sm[:, 16:20], 1.0, ALU.add, ["sm"], ["sm"])
                recip(sm[:, 20:24], sm[:, 16:20], ["sm"], ["sm"])
                tt(gm[:, 0:4 * nchC].rearrange("p (c h) -> p c h", c=nchC), bc(sm[:, 12:16].unsqueeze(1), [128, nchC, 4]),
                   bc(cm[:, tyC, 0:nchC].unsqueeze(2), [128, nchC, 4]), ALU.mult, ["sm", "cm"], ["gm"])
                bsm, bsmr = bank()
                mm(bsm[:, 0:4], cstf[:, TRIc, :], sm[:, 12:16], ["cstf", "sm"], [bsmr])
                mm(bsm[:, 4:8], cstf[:, SAMEc, :], sm[:, 12:16], ["cstf", "sm"], [bsmr])
                mm(bsm[0:64, 8:8 + 4 * nchC], cstf[:, ONES, 0:64], gm[:, 0:4 * nchC], ["cstf", "gm"], [bsmr])
                bgr, bgrr = bank()
                for h in range(4):
                    mm(bgr[:, 128 * h:128 * h + 128], bc(sm[:, 12 + h:13 + h], [128, 128]), cstf[:, TRIc, :], ["cstf", "sm"], [bgrr])
                cpy(sm[:, 24:32], bsm[:, 0:8], [bsmr], ["sm"])
                cpy(glr[:, 0:4 * nchC], bsm[0:64, 8:8 + 4 * nchC], [bsmr], ["glr"])
                act(eglr[:, 0:4 * nchC], glr[:, 0:4 * nchC], AF.Exp, ["glr"], ["eglr"])
                act(sm[:, 32:36], sm[:, 24:28], AF.Exp, ["sm"], ["sm"])
                tt(sm[:, 44:48], sm[:, 28:32], sm[:, 24:28], ALU.subtract, ["sm"], ["sm"])
                act(sm[:, 36:40], sm[:, 44:48], AF.Exp, ["sm"], ["sm"])
                tt(sm[:, 40:44], sm[:, 20:24], sm[:, 32:36], ALU.mult, ["sm"], ["sm"])
                act(Em[:].rearrange("p h t -> p (h t)"), bgr[:], AF.Exp, [bgrr], ["Em"])
                for h in range(4):
                    p0, g = 64 * (h % 2), h // 2
                    tt(qgT[:, h, :], qnT[p0:p0 + 64, g, :], Em[p0:p0 + 64, h, :], ALU.mult, ["qnT", "Em"], ["qgT"])
                tt(Y32[:, :, 0:64], btr[:, 0:256].rearrange("p (h d) -> p h d", h=4), bc(sm[:, 20:24].unsqueeze(2), [128, 4, 64]),
                   ALU.mult, [btrr, "sm"], ["Y32"])
                tt(Y32[:, :, 64:128], btr[:, 256:512].rearrange("p (h d) -> p h d", h=4), bc(sm[:, 40:44].unsqueeze(2), [128, 4, 64]),
                   ALU.mult, [btrr, "sm"], ["Y32"])
                act(Yb[0][:], Y32[:], AF.Copy, ["Y32"], ["Yb0"])
                tt(ktil[:], btr[:, 256:512].rearrange("p (h d) -> p h d", h=4), bc(sm[:, 36:40].unsqueeze(2), [128, 4, 64]),
                   ALU.mult, [btrr, "sm"], ["ktil"])
                gram = [bank(), bank()]
                for h in range(4):
                    p0, g = 64 * (h % 2), h // 2
                    gb, gbr = gram[h % 2]
                    mm(gb[:, 128 * g:128 * g + 128], knT[p0:p0 + 64, g, :], knT[p0:p0 + 64, g, :], ["knT"], [gbr])
                    mm(gb[:, 256 + 128 * g:256 + 128 * g + 128], knT[p0:p0 + 64, g, :], qnT[p0:p0 + 64, g, :], ["knT", "qnT"], [gbr])
                for h in range(4):
                    ts(Ee[:, h, :], bgr[:, 128 * h:128 * h + 128], sm[:, 24 + h:25 + h], ALU.subtract, [bgrr, "sm"], ["Ee"])
                stt(Ee[:], Ee[:], -1.0, Ee[:], ALU.mult, ALU.max, ["Ee"], ["Ee"])
                act(Ee[:], Ee[:], AF.Exp, ["Ee"], ["Ee"], scale=-1.0)
                tt(Em[:], Ee[:], bc(cstf[:, MSLNc, :].unsqueeze(1), [128, 4, 128]), ALU.mult, ["Ee", "cstf", "qgT"], ["Em"])
                for h in range(4):
                    stt(Lp[0][:, h, :], gram[h % 2][0][:, 128 * (h // 2):128 * (h // 2) + 128], sm[:, 20 + h:21 + h], Em[:, h, :],
                        ALU.mult, ALU.mult, [gram[h % 2][1], "sm", "Em"], ["Lp0"])
                tt(Em[:], Ee[:], bc(cstf[:, TRIc, :].unsqueeze(1), [128, 4, 128]), ALU.mult, ["Ee", "cstf"], ["Em"])
                for par in range(2):
                    tt(QKm[:, par::2, :], gram[par][0][:, 256:512].rearrange("p (h t) -> p h t", h=2), Em[:, par::2, :], ALU.mult,
                       [gram[par][1], "Em"], ["QKm"])
                bb_, bbr = bank()
                for h in range(4):
                    mm(bb_[:, 128 * h:128 * h + 128], Lp[0][:, h, :], cstb[:, IDENT, :], ["Lp0", "cstb"], [bbr])
                act(Bpw[0][:].rearrange("p h t -> p (h t)"), bb_[:], AF.Copy, [bbr], ["Bpw0"])
                NLV = 5
                for k in range(NLV):
                    ci, ni = k % 2, (k + 1) % 2
                    by, byr = bank()
                    for h in range(4):
                        mm(by[:, 128 * h:128 * h + 128], Bpw[ci][:, h, :], Yb[ci][:, h, :], ["Bpw%d" % ci, "Yb%d" % ci], [byr])
                    tt(Y32[:].rearrange("p h t -> p (h t)"), Y32[:].rearrange("p h t -> p (h t)"), by[:], ALU.add, ["Y32", byr], ["Y32"])
                    act(Yb[ni][:], Y32[:], AF.Copy, ["Y32"], ["Yb%d" % ni])
                    if k < NLV - 1:
                        b2, b2r = bank()
                        for h in range(4):
                            mm(b2[:, 128 * h:128 * h + 128], Lp[ci][:, h, :], Bpw[ci][:, h, :], ["Lp%d" % ci, "Bpw%d" % ci], [b2r])
                        act(Bpw[ni][:].rearrange("p h t -> p (h t)"), b2[:], AF.Copy, [b2r], ["Bpw%d" % ni])
                        if k < NLV - 2:
                            l2, l2r = bank()
                            for h in range(4):
                                mm(l2[:, 128 * h:128 * h + 128], Bpw[ci][:, h, :], Lp[ci][:, h, :], ["Lp%d" % ci, "Bpw%d" % ci], [l2r])
                            act(Lp[ni][:].rearrange("p h t -> p (h t)"), l2[:], AF.Copy, [l2r], ["Lp%d" % ni])
                yfin = NLV % 2
                Yf, Yfr = Yb[yfin], "Yb%d" % yfin
                bw, bwr = bank()
                for h in range(4):
                    mm(bw[0:64, 128 * h:128 * h + 128], Yf[:, h, 64:128], cstb[:, IDENT, :], [Yfr, "cstb"], [bwr])
                act(wT[:].rearrange("p h t -> p (h t)"), bw[0:64, :], AF.Copy, [bwr], ["wT"])
                bo, bor = bank(hold=True)
                for c in range(nchC):
                    si = (c % 2) if smp else 0
                    sr, sbr = "S32_%d" % si, "Sbf_%d" % si
                    if smp:
                        dma(S32[si][:], sdel_d[l, c], (), [sr])
                        cpy(Sbf[si][:], S32[si][:], [sr], [sbr])
                    bws, bwsr = bank()
                    for h in range(4):
                        mm(bws[:, 64 * h:64 * h + 64], wT[:, h, :], Sbf[si][:, h, :], ["wT", sbr], [bwsr])
                    tt(tmpu[:].rearrange("p (h d) -> p h d", h=4), Y32[:, :, 0:64], bws[:, 0:256].rearrange("p (h d) -> p h d", h=4),
                       ALU.subtract, ["Y32", bwsr], ["tmpu"])
                    ts(unc[:, c, :], tmpu[:], cm[:, tyC, c:c + 1], ALU.mult, ["tmpu", "cm"], ["unc%d" % c])
                    for h in range(4):
                        p0, g = 64 * (h % 2), h // 2
                        mm(bo[p0:p0 + 64, 128 * g + c * csC:128 * g + (c + 1) * csC], Sbf[si][:, h, :], qgT[:, h, c * csC:(c + 1) * csC],
                           [sbr, "qgT"], [bor], start=True, stop=False)
                    bds, bdsr = bank()
                    for h in range(4):
                        mm(bds[0:64, 64 * h:64 * h + 64], ktil[:, h, :], unc[:, c, 64 * h:64 * h + 64], ["ktil", "unc%d" % c], [bdsr])
                    tt(Sdec[:], S32[si][:], bc(eglr[:, 4 * c:4 * c + 4].unsqueeze(2), [64, 4, 64]), ALU.mult, [sr, "eglr"], ["Sdec"])
                    tt(Sbf[si][:], Sdec[:], bds[0:64, 0:256].rearrange("p (h d) -> p h d", h=4), ALU.add, ["Sdec", bdsr], [sbr])
                    tt(S32[si][:], Sdec[:], bds[0:64, 0:256].rearrange("p (h d) -> p h d", h=4), ALU.add, ["Sdec", bdsr], [sr])
                    for h in range(4):
                        p0, g = 64 * (h % 2), h // 2
                        mm(bo[p0:p0 + 64, 128 * g + c * csC:128 * g + (c + 1) * csC], unc[:, c, 64 * h:64 * h + 64],
                           QKm[:, h, c * csC:(c + 1) * csC], ["unc%d" % c, "QKm"], [bor], start=False, stop=True)
                    if smp:
                        dma(del_o[l, 1 + c], S32[si][:], [sr], ())
                if b == 15:
                    dma(del_o[l, 0], S32[0][:], ["S32_0"], ())

                def out_norm(bo_, bor_, gcol, gate, gres, kbase, sq_=None, sqr="sqo", tn_=None, tnr="tno", ln_=None, lnr="lnt"):
                    sq_ = sqo if sq_ is None else sq_
                    tn_ = tno if tn_ is None else tn_
                    ln_ = lnt[:, 0:256] if ln_ is None else ln_[:]
                    act(sq_[:], bo_[:, 0:256], AF.Square, [bor_], [sqr])
                    bn, bnr = bank()
                    for g in range(2):
                        mm(bn[:, 128 * g:128 * g + 128], cstb[:, BLK64, :], sq_[:, 128 * g:128 * g + 128], ["cstb", sqr], [bnr])
                    act(ln_, bn[:, 0:256], AF.Ln, [bnr], [lnr], bias=EPS)
                    act(ln_, ln_, AF.Exp, [lnr], [lnr], scale=-0.5)
                    stt(tn_[:], bo_[:, 0:256], pcol[:, l, gcol:gcol + 1], ln_, ALU.mult, ALU.mult, [bor_, "pcol", lnr], [tnr])
                    tt(mixT[:, kbase:kbase + 2, :], tn_[:].rearrange("p (g t) -> p g t", g=2), gate[:], ALU.mult, [tnr, gres], ["mixT%d" % kbase, "mixT%d" % (kbase + 1)])

                out_norm(bo, bor, 2, zs, "zs", 4)
                held.discard(bor)
                listD = []
                P.capture = listD
                cur_pool[0] = (4, 5)

                ck(100 * b + 6)
                bd, bdr = bank()
                mm(bd[:, 0:128], wgk[:, l, :], gkl[:, :], ["wgk", "gkl"], [bdr])
                act(e1[:], bd[:, 0:128], AF.Exp, [bdr, "pcol"], ["e1"], bias=pcol[:, l, 5:6], scale=-1.0)
                act(spd[:], e1[:], AF.Ln, ["e1"], ["spd"], bias=1.0)
                scan(Gs[:], cstf[:, RST, :], spd[:], ["cstf", "spd"], ["Gs"])
                Gv = Gs[:].rearrange("p (c t) -> p c t", c=nch)
                tt(Dm[:].rearrange("p (c t) -> p c t", c=nch), Gv, bc(Gv[:, :, cs - 1:cs], [128, nch, cs]), ALU.subtract, ["Gs"], ["Dm"])
                act(e1[:], Dm[:], AF.Exp, ["Dm"], ["e1"], scale=-1.0 / 16)
                act(e2[:], Dm[:], AF.Exp, ["Dm"], ["e2"], scale=1.0 / 16)
                act(egl[:, 0:nch], Gv[:, :, cs - 1], AF.Exp, ["Gs"], ["egl"], scale=-1.0 / 16)
                tt(qt[:], qd[:], e1[:], ALU.mult, ["qd", "e1"], ["qt"])
                tt(kt[:], kd[:], e2[:], ALU.mult, ["kd", "e2"], ["kt"])
                for h in range(4):
                    ts(qm[:, h, :], qt[:], cm[:, ty, 8 + h:9 + h], ALU.mult, ["qt", "cm"], ["qm"])
                bt2, bt2r = bank()
                mm(bt2[:, 0:128], kt[:], cstb[:, IDENT, :], ["kt", "cstb"], [bt2r])
                for c in range(nch):
                    ts(ktc[:, c, :], bt2[:, 0:128], cm[:, ty, c:c + 1], ALU.mult, [bt2r, "cm"], ["ktc"])
                bat, batr = bank()
                for h in range(4):
                    mm(bat[:, 128 * h:128 * h + 128], kt[:], qm[:, h, :], ["kt", "qm"], [batr])
                tt(attm[:], bat[:].rearrange("p (h t) -> p h t", h=4), bc(cstf[:, TRI, :].unsqueeze(1), [128, 4, 128]), ALU.mult,
                   [batr, "cstf"], ["attm"])
                bod, bodr = bank(hold=True)
                for c in range(nch):
                    si = (c % 2) if smp else 0
                    gr = "G32_%d" % si
                    if smp:
                        mset(G32[si][:], 0.0, [gr])
                        for h in range(4):
                            dma(G32[si][32 * h:32 * h + 32, 64 * h:64 * h + 64], sgla_d[l, c, h], (), [gr])
                    bx, bxr = bank()
                    mm(bx[:, 0:256], ktc[:, c, :], vD[:], ["ktc", "vD"], [bxr])
                    ts(Gp32[:], G32[si][:], egl[:, c:c + 1], ALU.mult, [gr, "egl"], ["Gp32"])
                    act(Gpbf[:], Gp32[:], AF.Copy, ["Gp32"], ["Gpbf"])
                    for h in range(4):
                        p0, g = 64 * (h % 2), h // 2
                        osl = bod[p0:p0 + 64, 128 * g + c * cs:128 * g + (c + 1) * cs]
                        mm(osl, Gpbf[:, 64 * h:64 * h + 64], qm[:, h, c * cs:(c + 1) * cs], ["Gpbf", "qm"], [bodr], start=True, stop=False)
                        mm(osl, vD[:, 64 * h:64 * h + 64], attm[:, h, c * cs:(c + 1) * cs], ["vD", "attm"], [bodr], start=False, stop=True)
                    tt(G32[si][:], Gp32[:], bx[:, 0:256], ALU.add, ["Gp32", bxr], [gr])
                    if smp:
                        for h in range(4):
                            dma(gla_o[l, 1 + c, h], G32[si][32 * h:32 * h + 32, 64 * h:64 * h + 64], [gr], ())
                if b == 15:
                    for h in range(4):
                        dma(gla_o[l, 0, h], G32[0][32 * h:32 * h + 32, 64 * h:64 * h + 64], ["G32_0"], ())
                out_norm(bod, bodr, 3, gs, "gs", 6, sqoD, "sqoD", tnoD, "tnoD", lntD, "lntD")
                held.discard(bodr)
                P.capture = None
                cur_pool[0] = None
                listP = []
                if 1 <= b + 1 <= 15:
                    P.capture = listP
                    cur_pool[0] = (6, 7)
                    front_early(b + 1)
                    P.capture = None
                    cur_pool[0] = None
                elif b == 16:
                    P.capture = listP
                    cur_pool[0] = (6, 7)
                    for bb2 in range(16):
                        h2_block(bb2)
                    P.capture = None
                    cur_pool[0] = None
                P.capture = listP
                cur_pool[0] = (6, 7)
                w_out_part(0, 4)
                P.capture = None
                cur_pool[0] = None
                keyed = []
                for lst, off_, sc_ in ((listC, 0.0, 1.0), (listD, 0.0, 1.0), (listP, 0.0, 1.0)):
                    n_ = len(lst)
                    for i_, op_ in enumerate(lst):
                        keyed.append((off_ + sc_ * (i_ + 0.5) / n_, len(keyed), op_))
                keyed.sort(key=lambda t_: (t_[0], t_[1]))
                for _, _, op_ in keyed:
                    P.add(*op_)

                if DBG and l == 0 and b == DBGB:
                    dl = [("vD", vD[:], 256), ("qt", qt[:], 128), ("kt", kt[:], 128), ("qm", qm[:].rearrange("p h t -> p (h t)"), 512),
                          ("ktc", ktc[:, 0:2, :].rearrange("p h t -> p (h t)"), 256), ("attm", attm[:].rearrange("p h t -> p (h t)"), 512),
                          ("Gs", Gs[:], 128), ("e1", e1[:], 128), ("e2", e2[:], 128), ("egl", egl[:, 0:2], 2), ("G32", G32[0][:], 256),
                          ("Gp32", Gp32[:], 256), ("Gpbf", Gpbf[:], 256), ("qd", qd[:], 128), ("kd", kd[:], 128), ("gs", gs[:].rearrange("p h t -> p (h t)"), 256)]
                    dflat = dbgt[:].rearrange("p a b -> p (a b)")
                    for di_, (nm_, ap_, n_) in enumerate(dl):
                        cpy(dflat[:, 0:n_], ap_, [nm_], ["dbgt"])
                        dma(dbg2_o[di_, :, 0:n_], dflat[:, 0:n_], ["dbgt"], ())
                if DBG and l == 0 and b in (DBGB, 16):
                    cpy(dbgt[:], mixT[:], ["mixT%d" % k_ for k_ in range(8)], ["dbgt"])
                    dma(dbg_o[0 if b == DBGB else 1], dbgt[:], ["dbgt"], ())
                ck(100 * b + 7)
                w_out_part(4, 8)

            ck(5000)
            P.phase_switch()
            if l + 1 < NL:
                load_w_out(l + 1)
            h2_block(16)
            ai = 0
            for j in range(8):
                wb = j % 2
                for fm in range(4):
                    dma(wup[wb][:, :, fm * 128:(fm + 1) * 128],
                        w_up_d[l, :, j * 512 + fm * 128:j * 512 + (fm + 1) * 128].rearrange("(kc p) f -> p kc f", p=128),
                        (), ["wup%d_%d" % (wb, fm)], eng="pool")
                for fc in range(4):
                    dma(wdn[wb][:, fc, :], w_dn_d[l, j * 512 + fc * 128:j * 512 + (fc + 1) * 128, :], (), ["wdn%d_%d" % (wb, fc)], eng="pool")
                for (t0, n) in TIL:
                    cols = slice(t0, t0 + n)
                    a_ = ai % 2
                    ai += 1
                    for fm in range(4):
                        bkt, bkr = bank()
                        for kc in range(8):
                            mm(bkt[:, :n], wup[wb][:, kc, fm * 128:(fm + 1) * 128], h2T[:, kc, cols], ["wup%d_%d" % (wb, fm), "w_in"], [bkr],
                               start=(kc == 0), stop=(kc == 7))
                        r_ = fm % 2
                        act(relu_t[r_][:, :n], bkt[:, :n], AF.Relu, [bkr], ["relu%d" % r_])
                        tt(actb[a_][:, fm, :n], relu_t[r_][:, :n], relu_t[r_][:, :n], ALU.mult, ["relu%d" % r_], ["actb%d" % a_])
                    for m in range(8):
                        bkt, bkr = bank()
                        for fc in range(4):
                            mm(bkt[:, :n], wdn[wb][:, fc, m * 128:(m + 1) * 128], actb[a_][:, fc, :n], ["wdn%d_%d" % (wb, fc), "actb%d" % a_], [bkr],
                               start=(fc == 0), stop=(fc == 3))
                        tt(xT[:, m, cols], xT[:, m, cols], bkt[:, :n], ALU.add, ["x%d" % m, bkr], ["x%d" % m])
            P.phase_switch()
            if l + 1 < NL:
                load_w_in(l + 1)

        ck(6000)
        for bb2 in range(NBLK):
            cols = slice(bb2 * 128, bb2 * 128 + 128)
            rms_stats(cols, 128, XR)
            for kc in range(8):
                yo = yout[kc % 2]
                stt(yo[:], xT[:, kc, cols], gvec[:, 64 + kc:65 + kc], rstd[:, :128], ALU.mult, ALU.mult,
                    [XR[kc], "rstd", "gvec"], ["yout%d" % (kc % 2)])
                dma(yT_o[kc * 128:(kc + 1) * 128, cols], yo[:], ["yout%d" % (kc % 2)], ())

    except _Stop:
        pass

    P.emit(nc, stack)
    stack.close()
    return nc


def _consts():
    cst = np.zeros((128, 13, 128), np.float32)
    cst[:, 12] = 1.0
    i = np.arange(128)
    cst[:, 0] = np.eye(128)
    cst[:, 1] = 1.0 / 1024
    blk = (i[:, None] // 64 == i[None, :] // 64).astype(np.float32)
    cst[:, 2] = blk
    cst[:, 3] = blk / 64
    for ty, cs in ((0, 64), (1, 32)):
        same = (i[:, None] // cs == i[None, :] // cs)
        cst[:, 4 + ty] = (same & (i[:, None] <= i[None, :]))
        cst[:, 6 + ty] = same
        cst[:, 8 + ty] = -(same & (i[:, None] > i[None, :])).astype(np.float32)
        cst[:, 10 + ty] = np.broadcast_to((i % cs != 0).astype(np.float32)[None, :], (128, 128))
    cm = np.zeros((128, 2, 16), np.float32)
    for ty, cs in ((0, 64), (1, 32)):
        for c in range(128 // cs):
            cm[:, ty, c] = (i // cs == c)
            cm[:, ty, 4 + c] = -cm[:, ty, c]
        for h in range(4):
            cm[:, ty, 8 + h] = (i // 32 == h)
    invc = np.zeros((128, 2, 128), np.float32)
    t = np.arange(128)
    for g, win in enumerate((2, 4, 8, 16)):
        kc, p0 = g // 2, 64 * (g % 2)
        invc[p0:p0 + 64, kc, :] = 1.0 / np.minimum(win, t + 1)[None, :]
    return cst, cm, invc


def _colT(v):
    return v.reshape(v.shape[:-1] + (8, 128))


def kernel(x_prompt, x_sample, cache_pool, cache_attn_k, cache_attn_v, state_conv, state_delta,
           state_gla, attn_norm_g, w_in, pool_w, pool_scale, rel_bias, conv_w, a_log, dt_bias,
           delta_norm_g, gla_w_gk, gla_b_gk, gla_norm_g, w_out, mlp_norm_g, w_up, w_down,
           final_norm_g, _NL=DEPTH, _DBG=False, _CORES=NCORES, _STOP=None):
    f = np.float32
    cst, cm, invc = _consts()
    gvec = np.zeros((128, 72), f)
    gvec[:, 0:32] = attn_norm_g.reshape(4, 8, 128).transpose(2, 0, 1).reshape(128, 32)
    gvec[:, 32:64] = mlp_norm_g.reshape(4, 8, 128).transpose(2, 0, 1).reshape(128, 32)
    gvec[:, 64:72] = final_norm_g.reshape(8, 128).T
    pcol = np.zeros((128, 4, 8), f)
    pcol[:, :, 0:2] = pool_scale.reshape(4, 2, 128).transpose(2, 0, 1)
    pcol[:, :, 2] = np.concatenate([delta_norm_g, delta_norm_g], 1).T
    pcol[:, :, 3] = np.concatenate([gla_norm_g, gla_norm_g], 1).T
    pcol[:, :, 4] = gla_b_gk.T
    convw = np.ascontiguousarray(conv_w.reshape(4, 4, 6, 128).transpose(3, 0, 2, 1)).astype(f)
    hrow = np.zeros((128, 4, 8), f)
    hrow[:, :, 0:4] = a_log[None]
    hrow[:, :, 4:8] = dt_bias[None]
    poolw = np.zeros((4, 2, 128, 128), f)
    for g in range(4):
        kc, p0 = g // 2, 64 * (g % 2)
        poolw[:, kc, p0:p0 + 64, p0:p0 + 64] = pool_w[:, g]
    wgk = np.ascontiguousarray(gla_w_gk.transpose(1, 0, 2)).astype(f)
    NEG = f(-100.0)
    k = np.arange(128)[:, None]
    q = np.arange(128)[None, :]
    bp = np.zeros((4, 128, 5, 4, 128), f)
    for r in range(5):
        idx = np.clip((r - 4) * 128 + k - q, -256, 256) + 256
        tab = rel_bias[:, :, idx]
        tab = tab.transpose(0, 2, 1, 3).copy()
        if r == 0:
            tab[:, 0:64, :, 64:128] = NEG
        if r == 4:
            tab[:, 64:128, :, 0:64] = NEG
        bp[:, :, r] = tab
    bp = bp.reshape(4, 128, 5, 512)
    q32 = np.arange(32)[None, :]
    bsc = np.zeros((4, 128, 4, 4, 32), f)
    for r in range(4):
        idx = np.clip(r * 128 + k - 512 - q32, -256, 256) + 256
        bsc[:, :, r] = rel_bias[:, :, idx].transpose(0, 2, 1, 3)
    bsc = bsc.reshape(4, 128, 4, 128)
    bsn = np.full((4, 128, 4, 4, 32), NEG, f)
    kk = np.arange(32)[:, None]
    idx = np.clip(kk - q32, -256, 256) + 256
    tabn = rel_bias[:, :, idx].transpose(0, 2, 1, 3)
    for s in range(4):
        bsn[:, 32 * s:32 * s + 32, s] = tabn
    bsn = bsn.reshape(4, 128, 4, 128)

    shared = dict(w_in=np.ascontiguousarray(w_in, f), w_out=np.ascontiguousarray(w_out, f),
                  w_up=np.ascontiguousarray(w_up, f), w_down=np.ascontiguousarray(w_down, f),
                  gvec=gvec, pcol=pcol, convw=convw, hrow=hrow, poolw=poolw, wgk=wgk, bp=bp, bsc=bsc, bsn=bsn,
                  cst=cst, cm=cm, invc=invc)
    in_maps = []
    for c in range(NCORES):
        sl = slice(4 * c, 4 * c + 4)
        xs = np.concatenate([x_prompt[c], x_sample[sl].reshape(128, D)], 0)
        m = dict(shared)
        m["xT"] = np.ascontiguousarray(xs.T, f)
        m["cpool"] = np.ascontiguousarray(cache_pool[:, sl].transpose(0, 1, 3, 2), f)
        m["ck"] = np.ascontiguousarray(cache_attn_k[:, sl].transpose(0, 1, 2, 4, 3).reshape(4, 4, 256, 512), f)
        m["cv"] = np.ascontiguousarray(cache_attn_v[:, sl], f)
        m["sconv"] = np.ascontiguousarray(state_conv[:, sl].transpose(0, 1, 3, 2), f)
        m["sdel"] = np.ascontiguousarray(state_delta[:, sl].transpose(0, 1, 3, 2, 4), f)
        m["sgla"] = np.ascontiguousarray(state_gla[:, sl], f)
        in_maps.append(m)

    nc = build(_NL, _DBG, _STOP)
    res = run_bass_kernel_spmd(nc, in_maps[:_CORES], core_ids=list(range(_CORES))).results
    res = list(res) + [res[0]] * (NCORES - _CORES)
    global _last_res
    _last_res = res

    y_prompt = np.zeros((8, 2048, D), f)
    y_sample = np.zeros((32, 32, D), f)
    pool_p = np.zeros((4, 8, 15, 256), f); pool_s = np.zeros((4, 32, 15, 256), f)
    k_p = np.zeros((4, 8, 4, 512, 64), f); v_p = np.zeros((4, 8, 4, 512, 64), f)
    k_s = np.zeros((4, 32, 4, 32, 64), f); v_s = np.zeros((4, 32, 4, 32, 64), f)
    conv_p = np.zeros((4, 8, 3, 768), f); conv_s = np.zeros((4, 32, 3, 768), f)
    delta_p = np.zeros((4, 8, 4, 64, 64), f); delta_s = np.zeros((4, 32, 4, 64, 64), f)
    gla_p = np.zeros((4, 8, 4, 32, 64), f); gla_s = np.zeros((4, 32, 4, 32, 64), f)
    for c in range(NCORES):
        r = res[c]
        sl = slice(4 * c, 4 * c + 4)
        yt = r["yT"].T
        y_prompt[c] = yt[:2048]
        y_sample[sl] = yt[2048:].reshape(4, 32, D)
        po = r["pool_o"].transpose(0, 1, 3, 2)
        pool_p[:, c] = po[:, 0]; pool_s[:, sl] = po[:, 1:]
        ko = r["k_o"]
        k_p[:, c] = ko[:, :, :512].reshape(4, 4, 64, 512).transpose(0, 1, 3, 2)
        k_s[:, sl] = ko[:, :, 512:].reshape(4, 4, 64, 4, 32).transpose(0, 3, 1, 4, 2)
        vo = r["v_o"]
        v_p[:, c] = vo[:, :512].reshape(4, 512, 4, 64).transpose(0, 2, 1, 3)
        v_s[:, sl] = vo[:, 512:].reshape(4, 4, 32, 4, 64).transpose(0, 1, 3, 2, 4)
        co = r["conv_o"].transpose(0, 1, 3, 2)
        conv_p[:, c] = co[:, 0]; conv_s[:, sl] = co[:, 1:]
        do = r["del_o"].transpose(0, 1, 3, 2, 4)
        delta_p[:, c] = do[:, 0]; delta_s[:, sl] = do[:, 1:]
        go = r["gla_o"]
        gla_p[:, c] = go[:, 0]; gla_s[:, sl] = go[:, 1:]
    return (y_prompt, y_sample, pool_p, k_p, v_p, conv_p, delta_p, gla_p,
            pool_s, k_s, v_s, conv_s, delta_s, gla_s)
```

```python
import contextlib
import numpy as np
import concourse.bass as bass
import concourse.mybir as mybir
from concourse.bass_utils import run_bass_kernel_spmd

F32 = mybir.dt.float32
BF16 = mybir.dt.bfloat16
AF = mybir.ActivationFunctionType
ALU = mybir.AluOpType

NCORES = 8
DEPTH = 4
D = 1024
NT = 2176
NBLK = 17
INC = 2840
EPS = 1e-6
ENGS = ["pe", "act", "dve", "pool", "sp"]
RING = 8
OVW_ = 15400
DBGB = 12


class Op:
    __slots__ = ("eng", "fn", "deps", "sig", "sigval", "dma", "k")


class Prog:
    def __init__(self):
        self.ops = {e: [] for e in ENGS}
        self.last_w = {}
        self.readers = {}
        self.ndma = {e: 0 for e in ENGS}
        self.alias = {}
        self.ov_res = set()
        self.ov_recent = {e: [] for e in ENGS}
        self.fence = []
        self.last_acc = {}
        self.capture = None

    def phase_switch(self):
        f = []
        for e in ENGS:
            f.extend(self.ov_recent[e])
        self.fence = f
        self.ov_recent = {e: [] for e in ENGS}

    def add(self, eng, fn, rd=(), wr=(), dma=False):
        if self.capture is not None:
            self.capture.append((eng, fn, tuple(rd), tuple(wr), dma))
            return None
        op = Op()
        op.eng, op.fn, op.dma, op.sig, op.sigval, op.k = eng, fn, dma, False, 0, 0
        wr = list(wr)
        for r in list(wr):
            wr.extend(self.alias.get(r, ()))
        deps = set()
        for r in rd:
            w = self.last_w.get(r)
            if w is not None:
                deps.add(w)
        for r in wr:
            w = self.last_w.get(r)
            if w is not None:
                deps.add(w)
            for x in self.readers.get(r, ()):
                deps.add(x)
        for r in list(rd) + list(wr):
            if r.startswith("pb"):
                la = self.last_acc.get(r)
                if la is not None and la.eng != eng:
                    deps.add(la)
                self.last_acc[r] = op
        touches_ov = any(r in self.ov_res for r in rd) or any(r in self.ov_res for r in wr)
        if touches_ov:
            deps.update(self.fence)
        for r in rd:
            self.readers.setdefault(r, []).append(op)
        for r in wr:
            self.last_w[r] = op
            self.readers[r] = []
        deps.discard(op)
        op.deps = [d for d in deps if d.dma or d.eng != eng or eng != "pe"]
        if dma:
            op.k = self.ndma[eng]
            self.ndma[eng] += 1
        self.ops[eng].append(op)
        if touches_ov:
            lst = self.ov_recent[eng]
            lst.append(op)
            keep = RING + 1
            if len(lst) > keep:
                nd = [o for o in lst if not o.dma][-1:]
                dd = [o for o in lst if o.dma][-RING:]
                self.ov_recent[eng] = dd + nd
        return op

    def emit(self, nc, stack):
        for e in ENGS:
            for op in self.ops[e]:
                for d in op.deps:
                    d.sig = True
        esem = {e: stack.enter_context(nc.semaphore("es_" + e)) for e in ENGS}
        dsem = {e: [stack.enter_context(nc.semaphore("ds_%s_%d" % (e, i))) for i in range(RING)]
                for e in ENGS if self.ndma[e] > 0}
        for e in ENGS:
            c = 0
            for op in self.ops[e]:
                if op.sig and not op.dma:
                    c += 1
                    op.sigval = c
        block = stack.enter_context(nc.Block())

        def run(e, eng):
            waited = {}

            def wait(key, sem, val):
                if waited.get(key, 0) < val:
                    eng.wait_ge(sem, val)
                    waited[key] = val

            for op in self.ops[e]:
                for d in op.deps:
                    if d.dma:
                        wait(("d", d.eng, d.k % RING), dsem[d.eng][d.k % RING], 16 * (d.k // RING + 1))
                    else:
                        wait(("e", d.eng), esem[d.eng], d.sigval)
                if op.dma:
                    if op.k >= RING:
                        wait(("d", e, op.k % RING), dsem[e][op.k % RING], 16 * (op.k // RING))
                    op.fn(eng).then_inc(dsem[e][op.k % RING], 16)
                else:
                    ins = op.fn(eng)
                    if op.sig:
                        ins.then_inc(esem[e], 1)
            n = self.ndma[e]
            for s in range(min(n, RING)):
                last = ((n - 1 - s) // RING) * RING + s
                wait(("d", e, s), dsem[e][s], 16 * (last // RING + 1))

        @block.tensor
        def _(t):
            run("pe", t)

        @block.scalar
        def _(t):
            run("act", t)

        @block.vector
        def _(t):
            run("dve", t)

        @block.gpsimd
        def _(t):
            run("pool", t)

        @block.sync
        def _(t):
            run("sp", t)


def bc(ap, shape):
    return ap.broadcast_to(list(shape))


class _Stop(Exception):
    pass


def build(NL=DEPTH, DBG=False, STOP=None):
    nc = bass.Bass("TRN2", target_bir_lowering=False)
    P = Prog()
    stack = contextlib.ExitStack()

    def din(name, shape):
        return nc.dram_tensor(name, list(shape), F32, kind="ExternalInput").ap()

    def dout(name, shape):
        return nc.dram_tensor(name, list(shape), F32, kind="ExternalOutput").ap()

    xT_d = din("xT", [D, NT])
    w_in_d = din("w_in", [DEPTH, D, INC])
    w_out_d = din("w_out", [DEPTH, D, D])
    w_up_d = din("w_up", [DEPTH, D, 4096])
    w_dn_d = din("w_down", [DEPTH, 4096, D])
    gvec_d = din("gvec", [128, 72])
    pcol_d = din("pcol", [128, DEPTH, 8])
    convw_d = din("convw", [128, DEPTH, 6, 4])
    hrow_d = din("hrow", [128, DEPTH, 8])
    poolw_d = din("poolw", [DEPTH, 2, 128, 128])
    wgk_d = din("wgk", [16, DEPTH, 128])
    bp_d = din("bp", [DEPTH, 128, 5, 512])
    bsc_d = din("bsc", [DEPTH, 128, 4, 128])
    bsn_d = din("bsn", [DEPTH, 128, 4, 128])
    cst_d = din("cst", [128, 13, 128])
    cm_d = din("cm", [128, 2, 16])
    invc_d = din("invc", [128, 2, 128])
    cpool_d = din("cpool", [DEPTH, 4, 256, 15])
    ck_d = din("ck", [DEPTH, 4, 256, 512])
    cv_d = din("cv", [DEPTH, 4, 4, 512, 64])
    sconv_d = din("sconv", [DEPTH, 4, 768, 3])
    sdel_d = din("sdel", [DEPTH, 4, 64, 4, 64])
    sgla_d = din("sgla", [DEPTH, 4, 4, 32, 64])

    yT_o = dout("yT", [D, NT])
    pool_o = dout("pool_o", [DEPTH, 5, 256, 15])
    k_o = dout("k_o", [DEPTH, 256, 640])
    v_o = dout("v_o", [DEPTH, 640, 256])
    conv_o = dout("conv_o", [DEPTH, 5, 768, 3])
    del_o = dout("del_o", [DEPTH, 5, 64, 4, 64])
    gla_o = dout("gla_o", [DEPTH, 5, 4, 32, 64])
    dbg_o = dout("dbg_o", [2, 128, 8, 128]) if DBG else None
    dbg2_o = dout("dbg2_o", [16, 128, 1024]) if DBG else None

    def sb(name, shape, dt=F32):
        return stack.enter_context(nc.sbuf_tensor("s_" + name, list(shape), dt))

    xT = sb("xT", [128, 8, NT])
    w_in = sb("w_in_sb", [128, 8, INC], BF16)
    w_out = sb("w_out_sb", [128, 8, D], BF16)
    h2T = w_in[:].rearrange("p k c -> p (k c)")[:, 0:8 * NT].rearrange("p (k t) -> p k t", k=8)
    gvec = sb("gvec", [128, 72])
    pcol = sb("pcol", [128, DEPTH, 8])
    convw = sb("convw", [128, DEPTH, 6, 4])
    hrow = sb("hrow", [128, DEPTH, 8])
    negA = sb("negA", [128, 4])
    poolw = sb("poolw", [128, 2, 128], BF16)
    wgk = sb("wgk", [16, DEPTH, 128], BF16)
    bp = sb("bp", [128, 5, 512], BF16)
    bsc = sb("bsc", [128, 4, 128], BF16)
    bsn = sb("bsn", [128, 4, 128], BF16)
    cstf = sb("cstf", [128, 9, 128])
    cstb = sb("cstb", [128, 4, 128], BF16)
    cm = sb("cm", [128, 2, 16])
    invc = sb("invc", [128, 2, 128])
    sq = [sb("sq%d" % i, [128, 128], BF16) for i in range(2)]
    lnt = sb("lnt", [128, 512])
    rstd = sb("rstd", [128, 128])

    OVW = OVW_
    OV = sb("ov", [128, OVW])
    ov_off = [0]
    ov_offs = {}
    P.alias = {}

    def ovview(off, words, shape, dt, parts):
        v = OV[0:parts, off:off + words]
        if dt != F32:
            v = v.bitcast(dt)
        if len(shape) == 3:
            v = v.rearrange("p (a b) -> p a b", a=shape[1])
        return v

    def ova(name, shape, dt=F32, parts=128):
        n = 1
        for d_ in shape[1:]:
            n *= d_
        words = n if dt == F32 else (n + 1) // 2
        off = ov_off[0]
        assert off + words <= OVW, ("overlay overflow", name, off, words)
        ov_off[0] = off + words
        ov_offs[name] = (off, words)
        P.ov_res.add(name)
        return ovview(off, words, shape, dt, parts)

    def ovat(name, off, shape, dt=F32, parts=128):
        n = 1
        for d_ in shape[1:]:
            n *= d_
        words = n if dt == F32 else (n + 1) // 2
        ov_offs[name] = (off, words)
        P.ov_res.add(name)
        return ovview(off, words, shape, dt, parts)

    def union(host, members):
        P.alias[host] = list(members)
        for m_ in members:
            P.alias[m_] = [host]

    hT = ova("hT", [128, 8, 128], BF16)
    mixT = ova("mixT", [128, 8, 128], BF16)
    uext = ova("uext", [128, 2, 192])
    pwa = ova("pwa", [128, 2, 192])
    pwb = ova("pwb", [128, 2, 192])
    sqo = ovat("sqo", ov_offs["pwa"][0], [128, 256], BF16)
    tno = ovat("tno", ov_offs["pwa"][0] + 128, [128, 256])
    union("pwa", ["sqo", "tno"])
    Sdec = ovat("Sdec", ov_offs["pwb"][0], [64, 4, 64], parts=64)
    union("pwb", ["Sdec"])
    pooled = ova("pooled", [128, 2, 128], BF16)
    qb = ova("qb", [128, 2, 128], BF16)
    kbT = ova("kbT", [128, 2, 640], BF16)
    for i in range(5):
        P.ov_res.add("kbT%d" % i)
    vb1 = ova("vb1", [128, 5, 384], BF16)
    tS = ova("tS", [128, 512])
    rc = tS
    QKm = ovat("QKm", ov_offs["tS"][0], [128, 4, 128], BF16)
    wT = ovat("wT", ov_offs["tS"][0] + 256, [64, 4, 128], BF16, parts=64)
    union("tS", ["QKm", "wT"])
    cext = ova("cext", [128, 6, 144])
    cacc = ova("cacc", [128, 6, 128])
    sqc = ova("sqc", [128, 4, 128], BF16)
    qnT = ova("qnT", [128, 2, 128], BF16)
    knT = ova("knT", [128, 2, 128], BF16)
    vcT = ova("vcT", [128, 2, 128], BF16)
    zs = ova("zs", [128, 2, 128], BF16)
    gs = ova("gs", [128, 2, 128], BF16)
    ab = ova("ab", [128, 8])
    sm = ova("sm", [128, 64])
    gm = ova("gm", [128, 16])
    glr = ova("glr", [64, 16], parts=64)
    eglr = ova("eglr", [64, 16], parts=64)
    qgT = ova("qgT", [64, 4, 128], BF16, parts=64)
    Ee = ova("Ee", [128, 4, 128])
    qm = ova("qm", [128, 4, 128], BF16)
    ktc = ova("ktc", [128, 4, 128], BF16)
    Em = ova("Em", [128, 4, 128])
    sqoD = ovat("sqoD", ov_offs["Ee"][0], [128, 256], BF16)
    tnoD = ovat("tnoD", ov_offs["Ee"][0] + 128, [128, 256])
    lntD = ovat("lntD", ov_offs["Em"][0], [128, 256])
    union("Ee", ["sqoD", "tnoD"])
    union("Em", ["lntD"])
    attm = ova("attm", [128, 4, 128], BF16)
    Gp32 = ova("Gp32", [128, 256])
    Lp = [ova("Lp%d" % i, [128, 4, 128], BF16) for i in range(2)]
    Bpw = [ova("Bpw%d" % i, [128, 4, 128], BF16) for i in range(2)]
    Yb = [ova("Yb%d" % i, [128, 4, 128], BF16) for i in range(2)]
    kcs = ovat("kcs", ov_offs["Lp0"][0], [128, 2, 512], BF16)
    vcs = ovat("vcs", ov_offs["Bpw0"][0], [128, 4, 384], BF16)
    union("kcs", ["Lp0", "Lp1"])
    union("vcs", ["Bpw0", "Bpw1", "Yb0"])
    pT1 = ovat("pT", ov_offs["Yb1"][0], [128, 512], BF16)
    union("Yb1", ["pT"])
    ktil = ova("ktil", [128, 4, 64], BF16)
    Y32 = ova("Y32", [128, 4, 128])
    kf32 = ovat("kf32", ov_offs["Y32"][0], [128, 2, 128])
    vf32 = ovat("vf32", ov_offs["Y32"][0] + 256, [128, 256])
    union("Y32", ["kf32", "vf32"])
    tmpu = ova("tmpu", [128, 256])
    spd = ova("spd", [128, 128])
    Gpbf = ova("Gpbf", [128, 256], BF16)
    unc = ova("unc", [128, 4, 256], BF16)
    Gs = ova("Gs", [128, 128])
    Dm = ova("Dm", [128, 128])
    e1 = ova("e1", [128, 128])
    e2 = ova("e2", [128, 128])
    S32 = [ova("S32_%d" % i, [64, 4, 64], parts=64) for i in range(2)]
    Sbf = [ova("Sbf_%d" % i, [64, 4, 64], BF16, parts=64) for i in range(2)]
    qd = ova("qd", [128, 128])
    kd = ova("kd", [128, 128])
    gkl = ova("gkl", [16, 128], BF16, parts=16)
    egl = ova("egl", [128, 4])
    qt = ova("qt", [128, 128], BF16)
    kt = ova("kt", [128, 128], BF16)
    vD = ova("vD", [128, 256], BF16)
    G32 = [ova("G32_%d" % i, [128, 256]) for i in range(2)]
    dbgt = ova("dbgt", [128, 8, 128]) if DBG else None
    lntP = ova("lntP", [128, 128])
    mixer_words = ov_off[0]
    for nm_, k_ in (("hT", 8), ("mixT", 8), ("cacc", 6), ("cext", 6), ("unc", 4)):
        for i_ in range(k_):
            P.ov_res.add("%s%d" % (nm_, i_))
    for i_ in range(5):
        P.ov_res.add("vb1_%d" % i_)
    ov_off[0] = 0
    wup = [ova("wup%d" % i, [128, 8, 512], BF16) for i in range(2)]
    wdn = [ova("wdn%d" % i, [128, 4, D], BF16) for i in range(2)]
    actb = [ova("actb%d" % i, [128, 4, 512], BF16) for i in range(2)]
    relu_t = [ova("relu%d" % i, [128, 512], BF16) for i in range(2)]
    yout = [ova("yout%d" % i, [128, 128]) for i in range(2)]
    mlp_words = ov_off[0]
    for i_ in range(2):
        for k_ in range(4):
            P.ov_res.add("wup%d_%d" % (i_, k_))
            P.ov_res.add("wdn%d_%d" % (i_, k_))
    print("overlay words: mixer", mixer_words, "mlp", mlp_words, "of", OVW)

    banks = [stack.enter_context(nc.psum_tensor("pb%d" % i, [128, 512], F32)) for i in range(8)]
    bk_i = [0]

    bank_sess = {}

    held = set()
    cur_pool = [None]
    pool_i = {}

    def bank(hold=False):
        pool = cur_pool[0]
        while True:
            if pool is None:
                i = bk_i[0] % 8
                bk_i[0] += 1
            else:
                k_ = pool_i.get(pool, 0)
                pool_i[pool] = k_ + 1
                i = pool[k_ % len(pool)]
            if ("pb%d" % i) not in held:
                break
        if hold:
            held.add("pb%d" % i)
        bank_sess["pb%d" % i] = set()
        return banks[i], "pb%d" % i

    def mm(out, lhsT, rhs, rd, wr, start=True, stop=True):
        st = bank_sess[wr[0]]
        base = out.base_partition()
        quads = set(range(base // 32, (base + out.shape[0] + 31) // 32))
        newq = quads - st
        if newq:
            assert newq == quads, ("mixed psum quadrants", wr, base, out.shape)
            s_ = True
            st |= quads
        else:
            s_ = False
        P.add("pe", lambda e: e.matmul(out, lhsT, rhs, start=s_, stop=True, skip_group_check=True), rd, wr)

    def act(out, in_, func, rd, wr, bias=None, scale=None):
        kw = {}
        if bias is not None:
            kw["bias"] = bias
        if scale is not None:
            kw["scale"] = scale
        P.add("act", lambda e: e.activation(out, in_, func, **kw), rd, wr)

    def tt(out, in0, in1, op, rd, wr, eng="dve"):
        P.add(eng, lambda e: e.tensor_tensor(out=out, in0=in0, in1=in1, op=op), rd, wr)

    def ts(out, in0, s1, op0, rd, wr, s2=None, op1=None, eng="dve"):
        if op1 is None:
            P.add(eng, lambda e: e.tensor_scalar(out=out, in0=in0, scalar1=s1, scalar2=None, op0=op0), rd, wr)
        else:
            P.add(eng, lambda e: e.tensor_scalar(out=out, in0=in0, scalar1=s1, scalar2=s2, op0=op0, op1=op1), rd, wr)

    def stt(out, in0, scalar, in1, op0, op1, rd, wr):
        P.add("dve", lambda e: e.scalar_tensor_tensor(out=out, in0=in0, scalar=scalar, in1=in1, op0=op0, op1=op1), rd, wr)

    def cpy(out, in_, rd, wr, eng="dve"):
        P.add(eng, lambda e: e.tensor_copy(out, in_), rd, wr)

    def mset(ap, val, wr, eng="dve"):
        P.add(eng, lambda e: e.memset(ap, val), (), wr)

    def dma(out, in_, rd, wr, eng="sp"):
        P.add(eng, lambda e: e.dma_start(out=out, in_=in_), rd, wr, dma=True)

    def scan(out, d0, d1, rd, wr):
        P.add("dve", lambda e: e.tensor_tensor_scan(out=out, data0=d0, data1=d1, initial=0.0, op0=ALU.mult, op1=ALU.add), rd, wr)

    def recip(out, in_, rd, wr):
        P.add("dve", lambda e: e.reciprocal(out, in_), rd, wr)

    IDENT, ONESN, BLK1, BLK64, ONES = 0, 1, 2, 3, 8
    VOFF = [0, 64, 192, 256]
    VCOL = [0, 128, 192, 320]

    def ck(n):
        if STOP is not None and STOP == n:
            raise _Stop()

    try:
        for kc in range(8):
            dma(xT[:, kc, :], xT_d[kc * 128:(kc + 1) * 128, :], (), ["x%d" % kc])
        XR = ["x%d" % k for k in range(8)]
        dma(gvec[:], gvec_d, (), ["gvec"])
        dma(pcol[:], pcol_d, (), ["pcol"])
        ts(pcol[:, :, 5], pcol[:, :, 4], -1.0, ALU.mult, ["pcol"], ["pcol"])
        dma(convw[:], convw_d, (), ["convw"])
        dma(hrow[:], hrow_d, (), ["hrow"])
        dma(cstf[:], cst_d[:, 4:13, :], (), ["cstf"])
        dma(cm[:], cm_d, (), ["cm"])
        dma(invc[:], invc_d, (), ["invc"])
        dma(cstb[:], cst_d[:, 0:4, :], (), ["cstb"], eng="pool")
        dma(wgk[:], wgk_d, (), ["wgk"], eng="pool")

        def load_w_in(l):
            for kc in range(8):
                dma(w_in[:, kc, :], w_in_d[l, kc * 128:(kc + 1) * 128, :], (), ["w_in"], eng="pool")

        def load_w_out(l):
            for kc in range(8):
                dma(w_out[:, kc, :], w_out_d[l, kc * 128:(kc + 1) * 128, :], (), ["w_out"], eng="pool")

        load_w_in(0)
        load_w_out(0)

        def rms_stats(cols, n, xres, lbuf=None, lres="lnt"):
            if lbuf is None:
                lbuf = lnt
            bkt, bkr = bank()
            for kc in range(8):
                s = sq[kc % 2]
                act(s[:, :n], xT[:, kc, cols], AF.Square, [xres[kc]], ["sq%d" % (kc % 2)])
                mm(bkt[:, :n], cstb[:, ONESN, :], s[:, :n], ["sq%d" % (kc % 2), "cstb"], [bkr], start=(kc == 0), stop=(kc == 7))
            act(lbuf[:, :n], bkt[:, :n], AF.Ln, [bkr], [lres], bias=EPS)
            act(rstd[:, :n], lbuf[:, :n], AF.Exp, [lres], ["rstd"], scale=-0.5)

        TIL = [(0, 512), (512, 512), (1024, 512), (1536, 512), (2048, 128)]

        for l in range(NL):
            dma(bp[:], bp_d[l], (), ["bp"], eng="pool")
            dma(bsc[:], bsc_d[l], (), ["bsc"], eng="pool")
            dma(bsn[:], bsn_d[l], (), ["bsn"], eng="pool")
            dma(poolw[:], poolw_d[l].rearrange("k p c -> p k c"), (), ["poolw"], eng="pool")
            act(negA[:], hrow[:, l, 0:4], AF.Exp, ["hrow"], ["negA"])
            ts(negA[:], negA[:], -1.0, ALU.mult, ["negA"], ["negA"])
            mset(vb1[:], 1.0, ["vb1_%d" % s_ for s_ in range(5)])
            mset(S32[0][:], 0.0, ["S32_0"])
            mset(Sbf[0][:], 0.0, ["Sbf_0"])
            mset(G32[0][:], 0.0, ["G32_0"])

            def h2_block(bb2):
                cols2 = slice(bb2 * 128, bb2 * 128 + 128)
                rms_stats(cols2, 128, XR, lntP, "lntP")
                for kc in range(8):
                    stt(h2T[:, kc, cols2], xT[:, kc, cols2], gvec[:, 32 + l * 8 + kc:32 + l * 8 + kc + 1], rstd[:, :128],
                        ALU.mult, ALU.mult, [XR[kc], "rstd", "gvec"], ["w_in"])

            for b in range(NBLK):
                smp = (b == 16)
                ty = 1 if smp else 0
                nch = 4 if smp else 2
                cs = 32 if smp else 64
                cols = slice(b * 128, (b + 1) * 128)
                TRI, SAME, MSLN, RST = 0 + ty, 2 + ty, 4 + ty, 6 + ty
                xres = XR
                CX = ["cext%d" % g_ for g_ in range(6)]

                if smp:
                    uxs = uext[:, :, 0:188].rearrange("p g (s t) -> p g s t", s=4)
                    cxs = cext[:, :, 0:140].rearrange("p g (s t) -> p g s t", s=4)
                slot = b % 5
                need_kv = smp or b >= 12

                def proj(c0, M):
                    bkt, bkr = bank()
                    for kc in range(8):
                        mm(bkt[0:M, 0:128], w_in[:, kc, c0:c0 + M], hT[:, kc, :], ["w_in", "hT%d" % kc], [bkr],
                           start=(kc == 0), stop=(kc == 7))
                    return bkt, bkr

                def front_early(bq):
                    smq = (bq == 16)
                    colq = slice(bq * 128, (bq + 1) * 128)
                    slq = bq % 5
                    nkv = smq or bq >= 12
                    rms_stats(colq, 128, XR, lntP, "lntP")
                    for kc in range(8):
                        stt(hT[:, kc, :], xT[:, kc, colq], gvec[:, l * 8 + kc:l * 8 + kc + 1], rstd[:, :128],
                            ALU.mult, ALU.mult, [XR[kc], "rstd", "gvec"], ["hT%d" % kc])
                    if smq:
                        for g in range(2):
                            dma(uxs[:, g, :, 0:15], cpool_d[l, :, g * 128:(g + 1) * 128, :].rearrange("s p t -> p s t"), (), ["uext"])
                        for g in range(6):
                            dma(cxs[:, g, :, 0:3], sconv_d[l, :, g * 128:(g + 1) * 128, :].rearrange("s p t -> p s t"), (), ["cext%d" % g])
                    elif bq == 0:
                        mset(uext[:, :, 0:15], 0.0, ["uext"])
                        mset(cext[:, :, 0:3], 0.0, CX)
                    else:
                        cpy(uext[:, :, 0:15], uext[:, :, 128:143], ["uext"], ["uext"])
                        cpy(cext[:, :, 0:3], cext[:, :, 128:131], CX, CX)
                    for g in range(2):
                        bkt, bkr = proj(g * 128, 128)
                        if smq:
                            act(uxs[:, g, :, 15:47], bkt[:, 0:128].rearrange("p (s t) -> p s t", s=4), AF.Copy, [bkr], ["uext"])
                        else:
                            act(uext[:, g, 15:143], bkt[:, 0:128], AF.Copy, [bkr], ["uext"])
                    for g in range(2):
                        bkt, bkr = proj(256 + g * 128, 128)
                        act(qb[:, g, :], bkt[:, 0:128], AF.Copy, [bkr], ["qb"], scale=0.125)
                    for g in range(2):
                        bkt, bkr = proj(512 + g * 128, 128)
                        act(kbT[:, g, slq * 128:(slq + 1) * 128], bkt[:, 0:128], AF.Copy, [bkr], ["kbT%d" % slq])
                    for g in range(6):
                        bkt, bkr = proj(1024 + g * 128, 128)
                        if smq:
                            cpy(cxs[:, g, :, 3:35], bkt[:, 0:128].rearrange("p (s t) -> p s t", s=4), [bkr], ["cext%d" % g])
                        else:
                            cpy(cext[:, g, 3:131], bkt[:, 0:128], [bkr], ["cext%d" % g])
                    bkt, bkr = proj(2056, 128)
                    act(qd[:], bkt[:, 0:128], AF.Copy, [bkr], ["qd"], scale=float(32 ** -0.5))
                    bkt, bkr = proj(2184, 128)
                    cpy(kd[:], bkt[:, 0:128], [bkr], ["kd"])
                    bkt, bkr = proj(2824, 16)
                    cpy(gkl[:, :], bkt[0:16, 0:128], [bkr], ["gkl"])
                    for g in range(6):
                        for j in range(4):
                            if smq:
                                src = cxs[:, g, :, j:j + 32]
                                dst = cacc[:, g, :].rearrange("p (s t) -> p s t", s=4)
                            else:
                                src = cext[:, g, j:j + 128]
                                dst = cacc[:, g, :]
                            if j == 0:
                                ts(dst, src, convw[:, l, g, 0:1], ALU.mult, ["cext%d" % g, "convw"], ["cacc%d" % g])
                            else:
                                stt(dst, src, convw[:, l, g, j:j + 1], dst, ALU.mult, ALU.add, ["cext%d" % g, "convw", "cacc%d" % g], ["cacc%d" % g])

                def front_late(bq):
                    slq = bq % 5
                    nkv = (bq == 16) or bq >= 12
                    if nkv:
                        for g in range(2):
                            bkt, bkr = proj(512 + g * 128, 128)
                            act(kf32[:, g, :], bkt[:, 0:128], AF.Copy, [bkr], ["kf32"])
                    for g in range(2):
                        bkt, bkr = proj(1792 + g * 128, 128)
                        act(zs[:, g, :], bkt[:, 0:128], AF.Silu, [bkr], ["zs"])
                    for g in range(2):
                        bkt, bkr = proj(2568 + g * 128, 128)
                        act(gs[:, g, :], bkt[:, 0:128], AF.Silu, [bkr], ["gs"])
                    bkt, bkr = bank()
                    bk2, bk2r = bank()
                    for kc in range(8):
                        mm(bkt[:, 0:256], hT[:, kc, :], w_in[:, kc, 768:1024], ["w_in", "hT%d" % kc], [bkr], start=(kc == 0), stop=(kc == 7))
                        mm(bkt[:, 256:512], hT[:, kc, :], w_in[:, kc, 2312:2568], ["w_in", "hT%d" % kc], [bkr], start=(kc == 0), stop=(kc == 7))
                        mm(bk2[:, 0:8], hT[:, kc, :], w_in[:, kc, 2048:2056], ["w_in", "hT%d" % kc], [bk2r], start=(kc == 0), stop=(kc == 7))
                    for h in range(4):
                        cpy(vb1[:, slq, VCOL[h]:VCOL[h] + 64], bkt[:, 64 * h:64 * h + 64], [bkr], ["vb1_%d" % slq])
                    if nkv:
                        act(vf32[:], bkt[:, 0:256], AF.Copy, [bkr], ["vf32"])
                    act(vD[:], bkt[:, 256:512], AF.Copy, [bkr], ["vD"])
                    cpy(ab[:], bk2[:, 0:8], [bk2r], ["ab"])

                def w_out_part(k0, k1):
                    for m in range(8):
                        bkt, bkr = bank()
                        for kc in range(k0, k1):
                            mm(bkt[:, 0:128], w_out[:, kc, m * 128:(m + 1) * 128], mixT[:, kc, :], ["w_out", "mixT%d" % kc], [bkr],
                               start=(kc == k0), stop=(kc == k1 - 1))
                        tt(xT[:, m, cols], xT[:, m, cols], bkt[:, 0:128], ALU.add, ["x%d" % m, bkr], ["x%d" % m])

                prefetched = (1 <= b <= 15)
                if not prefetched:
                    front_early(b)
                do_fab = (1 <= b <= 15)
                listF, listO, listA, listB = [], [], [], []
                P.capture = listF
                cur_pool[0] = (0, 1, 2) if do_fab else None
                front_late(b)
                P.capture = listO

                ck(100 * b + 2)
                if need_kv:
                    c0 = 512 if smp else (b - 12) * 128
                    for g in range(2):
                        dma(k_o[l, g * 128:(g + 1) * 128, c0:c0 + 128], kf32[:, g, :], ["kf32"], ())
                    dma(v_o[l, c0:c0 + 128, :], vf32[:], ["vf32"], ())
                if smp:
                    for g in range(2):
                        dma(pool_o[l, 1:5, g * 128:(g + 1) * 128, :].rearrange("s p t -> p s t"), uxs[:, g, :, 32:47], ["uext"], ())
                    for g in range(6):
                        dma(conv_o[l, 1:5, g * 128:(g + 1) * 128, :].rearrange("s p t -> p s t"), cxs[:, g, :, 32:35], ["cext%d" % g], ())
                elif b == 15:
                    for g in range(2):
                        dma(pool_o[l, 0, g * 128:(g + 1) * 128, :], uext[:, g, 128:143], ["uext"], ())
                    for g in range(6):
                        dma(conv_o[l, 0, g * 128:(g + 1) * 128, :], cext[:, g, 128:131], ["cext%d" % g], ())

                P.capture = listA
                cur_pool[0] = (3,) if do_fab else None
                if smp:
                    def V(t, a, bb):
                        return t[:, :, 0:188].rearrange("p g (s t) -> p g s t", s=4)[:, :, :, a:bb]
                    L_ = 47
                else:
                    def V(t, a, bb):
                        return t[:, :, a:bb]
                    L_ = 143
                wins = [(uext, pwa, 1), (pwa, pwb, 2), (pwb, pwa, 4), (pwa, pwb, 8)]
                for gi, (src, dst, sh) in enumerate(wins):
                    lo = 2 * sh - 1
                    dres = "pwa" if dst is pwa else "pwb"
                    tt(V(dst, lo, L_), V(src, lo, L_), V(src, lo - sh, L_ - sh), ALU.add,
                       ["uext", "pwa", "pwb"], [dres])
                    kc, p0 = gi // 2, 64 * (gi % 2)
                    win = 2 * sh
                    if smp:
                        o_ = pooled[p0:p0 + 64, kc, :].rearrange("p (s t) -> p s t", s=4)
                        i0 = V(dst, 15, 47)[p0:p0 + 64, kc]
                        i1 = V(uext, 15, 47)[p0:p0 + 64, kc]
                    else:
                        o_ = pooled[p0:p0 + 64, kc, :]
                        i0 = dst[p0:p0 + 64, kc, 15:143]
                        i1 = uext[p0:p0 + 64, kc, 15:143]
                    if b == 0:
                        tt(tS[p0:p0 + 64, 0:128], i0, invc[p0:p0 + 64, kc, :], ALU.mult, ["invc", dres], ["tS"])
                        tt(o_, tS[p0:p0 + 64, 0:128], i1, ALU.subtract, ["uext", "tS"], ["pooled"])
                    else:
                        stt(o_, i0, 1.0 / win, i1, ALU.mult, ALU.subtract, ["uext", dres], ["pooled"])
                for kc in range(2):
                    bkt, bkr = bank()
                    mm(bkt[:, 0:128], poolw[:, kc, :], pooled[:, kc, :], ["poolw", "pooled"], [bkr])
                    ts(mixT[:, kc, :], bkt[:, 0:128], pcol[:, l, kc:kc + 1], ALU.mult, [bkr, "pcol"], ["mixT%d" % kc])

                P.capture = listB
                cur_pool[0] = (4, 5, 6, 7) if do_fab else None
                ob, obr = bank(hold=True)
                if not smp:
                    rlist = [r for r in range(5) if b - 4 + r >= 0]
                    for ri, r in enumerate(rlist):
                        ks = (b - 4 + r) % 5
                        scp = [bank(), bank()]
                        for h in range(4):
                            p0, g = 64 * (h % 2), h // 2
                            sc, scr = scp[h % 2]
                            mm(sc[:, 128 * g:128 * g + 128], kbT[p0:p0 + 64, g, ks * 128:(ks + 1) * 128], qb[p0:p0 + 64, g, :],
                               ["kbT%d" % ks, "qb"], [scr])
                            mm(sc[:, 128 * g:128 * g + 128], cstb[:, IDENT, :], bp[:, r, 128 * h:128 * h + 128], ["cstb", "bp"], [scr])
                        i2 = ri % 2
                        for par in range(2):
                            sc, scr = scp[par]
                            act(pT1[:].rearrange("p (h q) -> p h q", h=4)[:, par::2, :], sc[:, 0:256].rearrange("p (h q) -> p h q", h=2),
                                AF.Exp, [scr], ["pT"])
                        for h in range(4):
                            mm(ob[:, 128 * h:128 * h + 128], vb1[:, ks, VOFF[h]:VOFF[h] + 128], pT1[:, 128 * h:128 * h + 128],
                               ["vb1_%d" % ks, "pT"], [obr], start=(ri == 0), stop=(ri == len(rlist) - 1))
                    ck(100 * b + 44)
                    for h in range(4):
                        po, ps_ = (0, 64) if h % 2 == 0 else (64, 0)
                        recip(rc[po:po + 64, 128 * h:128 * h + 128], ob[ps_:ps_ + 64, 128 * h:128 * h + 128], [obr], ["tS"])
                        ck(100 * b + 45)
                        tt(mixT[po:po + 64, 2 + h // 2, :], ob[po:po + 64, 128 * h:128 * h + 128], rc[po:po + 64, 128 * h:128 * h + 128],
                           ALU.mult, [obr, "tS"], ["mixT%d" % (2 + h // 2)])
                else:
                    for s in range(4):
                        dma(kcs[:], ck_d[l, s].rearrange("(g p) t -> p g t", p=128), (), ["kcs"], eng="pool")
                        mset(vcs[:], 1.0, ["vcs"])
                        for h in range(4):
                            dma(vcs[:, :, VCOL[h]:VCOL[h] + 64], cv_d[l, s, h].rearrange("(r p) d -> p r d", p=128), (), ["vcs"], eng="pool")
                        for r in range(5):
                            scp = [bank(), bank()]
                            for h in range(4):
                                p0, g = 64 * (h % 2), h // 2
                                sc, scr = scp[h % 2]
                                if r < 4:
                                    lhsT = kcs[p0:p0 + 64, g, r * 128:(r + 1) * 128]
                                    rdk = "kcs"
                                else:
                                    lhsT = kbT[p0:p0 + 64, g, slot * 128:(slot + 1) * 128]
                                    rdk = "kbT%d" % slot
                                mm(sc[:, 32 * g:32 * g + 32], lhsT, qb[p0:p0 + 64, g, 32 * s:32 * s + 32], [rdk, "qb"], [scr])
                                tab = bsc[:, r, :] if r < 4 else bsn[:, s, :]
                                mm(sc[:, 32 * g:32 * g + 32], cstb[:, IDENT, :], tab[:, 32 * h:32 * h + 32], ["cstb", "bsc", "bsn"], [scr])
                            i2 = r % 2
                            for par in range(2):
                                sc, scr = scp[par]
                                act(pT1[:, 0:128].rearrange("p (h q) -> p h q", h=4)[:, par::2, :], sc[:, 0:64].rearrange("p (h q) -> p h q", h=2),
                                    AF.Exp, [scr], ["pT"])
                            for h in range(4):
                                lhsT = vcs[:, r, VOFF[h]:VOFF[h] + 128] if r < 4 else vb1[:, slot, VOFF[h]:VOFF[h] + 128]
                                mm(ob[:, 128 * s + 32 * h:128 * s + 32 * h + 32], lhsT, pT1[:, 32 * h:32 * h + 32],
                                   ["vcs", "vb1_%d" % slot, "pT"], [obr], start=(r == 0), stop=(r == 4))
                    obv = ob[:].rearrange("p (s h q) -> p s h q", s=4, h=4)
                    rcv = rc[:].rearrange("p (s h q) -> p s h q", s=4, h=4)
                    for h in range(4):
                        po, ps_ = (0, 64) if h % 2 == 0 else (64, 0)
                        recip(rcv[po:po + 64, :, h, :], obv[ps_:ps_ + 64, :, h, :], [obr], ["tS"])
                        tt(mixT[po:po + 64, 2 + h // 2, :].rearrange("p (s q) -> p s q", s=4), obv[po:po + 64, :, h, :],
                           rcv[po:po + 64, :, h, :], ALU.mult, [obr, "tS"], ["mixT%d" % (2 + h // 2)])

                held.discard(obr)
                P.capture = None
                cur_pool[0] = None
                if do_fab:
                    keyed = []
                    for lst, off_, sc_ in ((listF, 0.0, 0.5), (listA, 0.0, 1.0), (listB, 0.0, 1.0)):
                        n_ = len(lst)
                        for i_, op_ in enumerate(lst):
                            keyed.append((off_ + sc_ * (i_ + 0.5) / n_, len(keyed), op_))
                    keyed.sort(key=lambda t_: (t_[0], t_[1]))
                    for _, _, op_ in keyed:
                        P.add(*op_)
                    for op_ in listO:
                        P.add(*op_)
                else:
                    for lst in (listF, listO, listA, listB):
                        for op_ in lst:
                            P.add(*op_)
                listC = []
                P.capture = listC
                cur_pool[0] = (0, 1, 2, 3)
                tyC, nchC, csC = 1, 4, 32
                TRIc, SAMEc, MSLNc = 0 + tyC, 2 + tyC, 4 + tyC
                act(vcT[:], cacc[:, 4:6, :], AF.Silu, ["cacc4", "cacc5"], ["vcT"])
                csl = cacc[:, 0:4, :]
                CQ = ["cacc0", "cacc1", "cacc2", "cacc3"]
                act(csl, csl, AF.Silu, CQ, CQ)
                act(sqc[:], csl, AF.Square, CQ, ["sqc"])
                bkt, bkr = bank()
                for g in range(4):
                    mm(bkt[:, 128 * g:128 * g + 128], cstb[:, BLK1, :], sqc[:, g, :], ["cstb", "sqc"], [bkr])
                act(lnt[:], bkt[:], AF.Ln, [bkr], ["lnt"], bias=EPS)
                act(lnt[:], lnt[:], AF.Exp, ["lnt"], ["lnt"], scale=-0.5)
                rn = lnt[:].rearrange("p (g t) -> p g t", g=4)
                stt(qnT[:], csl[:, 0:2, :], 0.125, rn[:, 0:2, :], ALU.mult, ALU.mult, ["cacc0", "cacc1", "lnt"], ["qnT"])
                tt(knT[:], csl[:, 2:4, :], rn[:, 2:4, :], ALU.mult, ["cacc2", "cacc3", "lnt"], ["knT"])
                btr, btrr = bank()
                for g in range(2):
                    mm(btr[:, 128 * g:128 * g + 128], vcT[:, g, :], cstb[:, IDENT, :], ["vcT", "cstb"], [btrr])
                    mm(btr[:, 256 + 128 * g:256 + 128 * g + 128], knT[:, g, :], cstb[:, IDENT, :], ["knT", "cstb"], [btrr])
                tt(sm[:, 0:4], ab[:, 0:4], hrow[:, l, 4:8], ALU.add, ["ab", "hrow"], ["sm"])
                act(sm[:, 4:8], sm[:, 0:4], AF.Exp, ["sm"], ["sm"])
                act(sm[:, 8:12], sm[:, 4:8], AF.Ln, ["sm"], ["sm"], bias=1.0)
                tt(sm[:, 12:16], sm[:, 8:12], negA[:], ALU.mult, ["sm", "negA"], ["sm"])
                act(sm[:, 16:20], ab[:, 4:8], AF.Exp, ["ab"], ["sm"], scale=-1.0)
                ts(sm[:, 16:20], sm[:, 16:20], 1.0, ALU.add, ["sm"], ["sm"])
                recip(sm[:, 20:24], sm[:, 16:20], ["sm"], ["sm"])
                tt(gm[:, 0:4 * nchC].rearrange("p (c h) -> p c h", c=nchC), bc(sm[:, 12:16].unsqueeze(1), [128, nchC, 4]),
                   bc(cm[:, tyC, 0:nchC].unsqueeze(2), [128, nchC, 4]), ALU.mult, ["sm", "cm"], ["gm"])
                bsm, bsmr = bank()
                mm(bsm[:, 0:4], cstf[:, TRIc, :], sm[:, 12:16], ["cstf", "sm"], [bsmr])
                mm(bsm[:, 4:8], cstf[:, SAMEc, :], sm[:, 12:16], ["cstf", "sm"], [bsmr])
                mm(bsm[0:64, 8:8 + 4 * nchC], cstf[:, ONES, 0:64], gm[:, 0:4 * nchC], ["cstf", "gm"], [bsmr])
                bgr, bgrr = bank()
                for h in range(4):
                    mm(bgr[:, 128 * h:128 * h + 128], bc(sm[:, 12 + h:13 + h], [128, 128]), cstf[:, TRIc, :], ["cstf", "sm"], [bgrr])
                cpy(sm[:, 24:32], bsm[:, 0:8], [bsmr], ["sm"])
                cpy(glr[:, 0:4 * nchC], bsm[0:64, 8:8 + 4 * nchC], [bsmr], ["glr"])
                act(eglr[:, 0:4 * nchC], glr[:, 0:4 * nchC], AF.Exp, ["glr"], ["eglr"])
                act(sm[:, 32:36], sm[:, 24:28], AF.Exp, ["sm"], ["sm"])
                tt(sm[:, 44:48], sm[:, 28:32], sm[:, 24:28], ALU.subtract, ["sm"], ["sm"])
                act(sm[:, 36:40], sm[:, 44:48], AF.Exp, ["sm"], ["sm"])
                tt(sm[:, 40:44], sm[:, 20:24], sm[:, 32:36], ALU.mult, ["sm"], ["sm"])
                act(Em[:].rearrange("p h t -> p (h t)"), bgr[:], AF.Exp, [bgrr], ["Em"])
                for h in range(4):
                    p0, g = 64 * (h % 2), h // 2
                    tt(qgT[:, h, :], qnT[p0:p0 + 64, g, :], Em[p0:p0 + 64, h, :], ALU.mult, ["qnT", "Em"], ["qgT"])
                tt(Y32[:, :, 0:64], btr[:, 0:256].rearrange("p (h d) -> p h d", h=4), bc(sm[:, 20:24].unsqueeze(2), [128, 4, 64]),
                   ALU.mult, [btrr, "sm"], ["Y32"])
                tt(Y32[:, :, 64:128], btr[:, 256:512].rearrange("p (h d) -> p h d", h=4), bc(sm[:, 40:44].unsqueeze(2), [128, 4, 64]),
                   ALU.mult, [btrr, "sm"], ["Y32"])
                act(Yb[0][:], Y32[:], AF.Copy, ["Y32"], ["Yb0"])
                tt(ktil[:], btr[:, 256:512].rearrange("p (h d) -> p h d", h=4), bc(sm[:, 36:40].unsqueeze(2), [128, 4, 64]),
                   ALU.mult, [btrr, "sm"], ["ktil"])
                gram = [bank(), bank()]
                for h in range(4):
                    p0, g = 64 * (h % 2), h // 2
                    gb, gbr = gram[h % 2]
                    mm(gb[:, 128 * g:128 * g + 128], knT[p0:p0 + 64, g, :], knT[p0:p0 + 64, g, :], ["knT"], [gbr])
                    mm(gb[:, 256 + 128 * g:256 + 128 * g + 128], knT[p0:p0 + 64, g, :], qnT[p0:p0 + 64, g, :], ["knT", "qnT"], [gbr])
                for h in range(4):
                    ts(Ee[:, h, :], bgr[:, 128 * h:128 * h + 128], sm[:, 24 + h:25 + h], ALU.subtract, [bgrr, "sm"], ["Ee"])
                stt(Ee[:], Ee[:], -1.0, Ee[:], ALU.mult, ALU.max, ["Ee"], ["Ee"])
                act(Ee[:], Ee[:], AF.Exp, ["Ee"], ["Ee"], scale=-1.0)
                tt(Em[:], Ee[:], bc(cstf[:, MSLNc, :].unsqueeze(1), [128, 4, 128]), ALU.mult, ["Ee", "cstf", "qgT"], ["Em"])
                for h in range(4):
                    stt(Lp[0][:, h, :], gram[h % 2][0][:, 128 * (h // 2):128 * (h // 2) + 128], sm[:, 20 + h:21 + h], Em[:, h, :],
                        ALU.mult, ALU.mult, [gram[h % 2][1], "sm", "Em"], ["Lp0"])
                tt(Em[:], Ee[:], bc(cstf[:, TRIc, :].unsqueeze(1), [128, 4, 128]), ALU.mult, ["Ee", "cstf"], ["Em"])
                for par in range(2):
                    tt(QKm[:, par::2, :], gram[par][0][:, 256:512].rearrange("p (h t) -> p h t", h=2), Em[:, par::2, :], ALU.mult,
                       [gram[par][1], "Em"], ["QKm"])
                bb_, bbr = bank()
                for h in range(4):
                    mm(bb_[:, 128 * h:128 * h + 128], Lp[0][:, h, :], cstb[:, IDENT, :], ["Lp0", "cstb"], [bbr])
                act(Bpw[0][:].rearrange("p h t -> p (h t)"), bb_[:], AF.Copy, [bbr], ["Bpw0"])
                NLV = 5
                for k in range(NLV):
                    ci, ni = k % 2, (k + 1) % 2
                    by, byr = bank()
                    for h in range(4):
                        mm(by[:, 128 * h:128 * h + 128], Bpw[ci][:, h, :], Yb[ci][:, h, :], ["Bpw%d" % ci, "Yb%d" % ci], [byr])
                    tt(Y32[:].rearrange("p h t -> p (h t)"), Y32[:].rearrange("p h t -> p (h t)"), by[:], ALU.add, ["Y32", byr], ["Y32"])
                    act(Yb[ni][:], Y32[:], AF.Copy, ["Y32"], ["Yb%d" % ni])
                    if k < NLV - 1:
                        b2, b2r = bank()
                        for h in range(4):
                            mm(b2[:, 128 * h:128 * h + 128], Lp[ci][:, h, :], Bpw[ci][:, h, :], ["Lp%d" % ci, "Bpw%d" % ci], [b2r])
                        act(Bpw[ni][:].rearrange("p h t -> p (h t)"), b2[:], AF.Copy, [b2r], ["Bpw%d" % ni])
                        if k < NLV - 2:
                            l2, l2r = bank()
                            for h in range(4):
                                mm(l2[:, 128 * h:128 * h + 128], Bpw[ci][:, h, :], Lp[ci][:, h, :], ["Lp%d" % ci, "Bpw%d" % ci], [l2r])
                            act(Lp[ni][:].rearrange("p h t -> p (h t)"), l2[:], AF.Copy, [l2r], ["Lp%d" % ni])
                yfin = NLV % 2
                Yf, Yfr = Yb[yfin], "Yb%d" % yfin
                bw, bwr = bank()
                for h in range(4):
                    mm(bw[0:64, 128 * h:128 * h + 128], Yf[:, h, 64:128], cstb[:, IDENT, :], [Yfr, "cstb"], [bwr])
                act(wT[:].rearrange("p h t -> p (h t)"), bw[0:64, :], AF.Copy, [bwr], ["wT"])
                bo, bor = bank(hold=True)
                for c in range(nchC):
                    si = (c % 2) if smp else 0
                    sr, sbr = "S32_%d" % si, "Sbf_%d" % si
                    if smp:
                        dma(S32[si][:], sdel_d[l, c], (), [sr])
                        cpy(Sbf[si][:], S32[si][:], [sr], [sbr])
                    bws, bwsr = bank()
                    for h in range(4):
                        mm(bws[:, 64 * h:64 * h + 64], wT[:, h, :], Sbf[si][:, h, :], ["wT", sbr], [bwsr])
                    tt(tmpu[:].rearrange("p (h d) -> p h d", h=4), Y32[:, :, 0:64], bws[:, 0:256].rearrange("p (h d) -> p h d", h=4),
                       ALU.subtract, ["Y32", bwsr], ["tmpu"])
                    ts(unc[:, c, :], tmpu[:], cm[:, tyC, c:c + 1], ALU.mult, ["tmpu", "cm"], ["unc%d" % c])
                    for h in range(4):
                        p0, g = 64 * (h % 2), h // 2
                        mm(bo[p0:p0 + 64, 128 * g + c * csC:128 * g + (c + 1) * csC], Sbf[si][:, h, :], qgT[:, h, c * csC:(c + 1) * csC],
                           [sbr, "qgT"], [bor], start=True, stop=False)
                    bds, bdsr = bank()
                    for h in range(4):
                        mm(bds[0:64, 64 * h:64 * h + 64], ktil[:, h, :], unc[:, c, 64 * h:64 * h + 64], ["ktil", "unc%d" % c], [bdsr])
                    tt(Sdec[:], S32[si][:], bc(eglr[:, 4 * c:4 * c + 4].unsqueeze(2), [64, 4, 64]), ALU.mult, [sr, "eglr"], ["Sdec"])
                    tt(Sbf[si][:], Sdec[:], bds[0:64, 0:256].rearrange("p (h d) -> p h d", h=4), ALU.add, ["Sdec", bdsr], [sbr])
                    tt(S32[si][:], Sdec[:], bds[0:64, 0:256].rearrange("p (h d) -> p h d", h=4), ALU.add, ["Sdec", bdsr], [sr])
                    for h in range(4):
                        p0, g = 64 * (h % 2), h // 2
                        mm(bo[p0:p0 + 64, 128 * g + c * csC:128 * g + (c + 1) * csC], unc[:, c, 64 * h:64 * h + 64],
                           QKm[:, h, c * csC:(c + 1) * csC], ["unc%d" % c, "QKm"], [bor], start=False, stop=True)
                    if smp:
                        dma(del_o[l, 1 + c], S32[si][:], [sr], ())
                if b == 15:
                    dma(del_o[l, 0], S32[0][:], ["S32_0"], ())

                def out_norm(bo_, bor_, gcol, gate, gres, kbase, sq_=None, sqr="sqo", tn_=None, tnr="tno", ln_=None, lnr="lnt"):
                    sq_ = sqo if sq_ is None else sq_
                    tn_ = tno if tn_ is None else tn_
                    ln_ = lnt[:, 0:256] if ln_ is None else ln_[:]
                    act(sq_[:], bo_[:, 0:256], AF.Square, [bor_], [sqr])
                    bn, bnr = bank()
                    for g in range(2):
                        mm(bn[:, 128 * g:128 * g + 128], cstb[:, BLK64, :], sq_[:, 128 * g:128 * g + 128], ["cstb", sqr], [bnr])
                    act(ln_, bn[:, 0:256], AF.Ln, [bnr], [lnr], bias=EPS)
                    act(ln_, ln_, AF.Exp, [lnr], [lnr], scale=-0.5)
                    stt(tn_[:], bo_[:, 0:256], pcol[:, l, gcol:gcol + 1], ln_, ALU.mult, ALU.mult, [bor_, "pcol", lnr], [tnr])
                    tt(mixT[:, kbase:kbase + 2, :], tn_[:].rearrange("p (g t) -> p g t", g=2), gate[:], ALU.mult, [tnr, gres], ["mixT%d" % kbase, "mixT%d" % (kbase + 1)])

                out_norm(bo, bor, 2, zs, "zs", 4)
                held.discard(bor)
                listD = []
                P.capture = listD
                cur_pool[0] = (4, 5)

                ck(100 * b + 6)
                bd, bdr = bank()
                mm(bd[:, 0:128], wgk[:, l, :], gkl[:, :], ["wgk", "gkl"], [bdr])
                act(e1[:], bd[:, 0:128], AF.Exp, [bdr, "pcol"], ["e1"], bias=pcol[:, l, 5:6], scale=-1.0)
                act(spd[:], e1[:], AF.Ln, ["e1"], ["spd"], bias=1.0)
                scan(Gs[:], cstf[:, RST, :], spd[:], ["cstf", "spd"], ["Gs"])
                Gv = Gs[:].rearrange("p (c t) -> p c t", c=nch)
                tt(Dm[:].rearrange("p (c t) -> p c t", c=nch), Gv, bc(Gv[:, :, cs - 1:cs], [128, nch, cs]), ALU.subtract, ["Gs"], ["Dm"])
                act(e1[:], Dm[:], AF.Exp, ["Dm"], ["e1"], scale=-1.0 / 16)
                act(e2[:], Dm[:], AF.Exp, ["Dm"], ["e2"], scale=1.0 / 16)
                act(egl[:, 0:nch], Gv[:, :, cs - 1], AF.Exp, ["Gs"], ["egl"], scale=-1.0 / 16)
                tt(qt[:], qd[:], e1[:], ALU.mult, ["qd", "e1"], ["qt"])
                tt(kt[:], kd[:], e2[:], ALU.mult, ["kd", "e2"], ["kt"])
                for h in range(4):
                    ts(qm[:, h, :], qt[:], cm[:, ty, 8 + h:9 + h], ALU.mult, ["qt", "cm"], ["qm"])
                bt2, bt2r = bank()
                mm(bt2[:, 0:128], kt[:], cstb[:, IDENT, :], ["kt", "cstb"], [bt2r])
                for c in range(nch):
                    ts(ktc[:, c, :], bt2[:, 0:128], cm[:, ty, c:c + 1], ALU.mult, [bt2r, "cm"], ["ktc"])
                bat, batr = bank()
                for h in range(4):
                    mm(bat[:, 128 * h:128 * h + 128], kt[:], qm[:, h, :], ["kt", "qm"], [batr])
                tt(attm[:], bat[:].rearrange("p (h t) -> p h t", h=4), bc(cstf[:, TRI, :].unsqueeze(1), [128, 4, 128]), ALU.mult,
                   [batr, "cstf"], ["attm"])
                bod, bodr = bank(hold=True)
                for c in range(nch):
                    si = (c % 2) if smp else 0
                    gr = "G32_%d" % si
                    if smp:
                        mset(G32[si][:], 0.0, [gr])
                        for h in range(4):
                            dma(G32[si][32 * h:32 * h + 32, 64 * h:64 * h + 64], sgla_d[l, c, h], (), [gr])
                    bx, bxr = bank()
                    mm(bx[:, 0:256], ktc[:, c, :], vD[:], ["ktc", "vD"], [bxr])
                    ts(Gp32[:], G32[si][:], egl[:, c:c + 1], ALU.mult, [gr, "egl"], ["Gp32"])
                    act(Gpbf[:], Gp32[:], AF.Copy, ["Gp32"], ["Gpbf"])
                    for h in range(4):
                        p0, g = 64 * (h % 2), h // 2
                        osl = bod[p0:p0 + 64, 128 * g + c * cs:128 * g + (c + 1) * cs]
                        mm(osl, Gpbf[:, 64 * h:64 * h + 64], qm[:, h, c * cs:(c + 1) * cs], ["Gpbf", "qm"], [bodr], start=True, stop=False)
                        mm(osl, vD[:, 64 * h:64 * h + 64], attm[:, h, c * cs:(c + 1) * cs], ["vD", "attm"], [bodr], start=False, stop=True)
                    tt(G32[si][:], Gp32[:], bx[:, 0:256], ALU.add, ["Gp32", bxr], [gr])
                    if smp:
                        for h in range(4):
                            dma(gla_o[l, 1 + c, h], G32[si][32 * h:32 * h + 32, 64 * h:64 * h + 64], [gr], ())
                if b == 15:
                    for h in range(4):
                        dma(gla_o[l, 0, h], G32[0][32 * h:32 * h + 32, 64 * h:64 * h + 64], ["G32_0"], ())
                out_norm(bod, bodr, 3, gs, "gs", 6, sqoD, "sqoD", tnoD, "tnoD", lntD, "lntD")
                held.discard(bodr)
                P.capture = None
                cur_pool[0] = None
                listP = []
                if 1 <= b + 1 <= 15:
                    P.capture = listP
                    cur_pool[0] = (6, 7)
                    front_early(b + 1)
                    P.capture = None
                    cur_pool[0] = None
                elif b == 16:
                    P.capture = listP
                    cur_pool[0] = (6, 7)
                    for bb2 in range(16):
                        h2_block(bb2)
                    P.capture = None
                    cur_pool[0] = None
                P.capture = listP
                cur_pool[0] = (6, 7)
                w_out_part(0, 4)
                P.capture = None
                cur_pool[0] = None
                keyed = []
                for lst, off_, sc_ in ((listC, 0.0, 1.0), (listD, 0.0, 1.0), (listP, 0.0, 1.0)):
                    n_ = len(lst)
                    for i_, op_ in enumerate(lst):
                        keyed.append((off_ + sc_ * (i_ + 0.5) / n_, len(keyed), op_))
                keyed.sort(key=lambda t_: (t_[0], t_[1]))
                for _, _, op_ in keyed:
                    P.add(*op_)

                if DBG and l == 0 and b == DBGB:
                    dl = [("vD", vD[:], 256), ("qt", qt[:], 128), ("kt", kt[:], 128), ("qm", qm[:].rearrange("p h t -> p (h t)"), 512),
                          ("ktc", ktc[:, 0:2, :].rearrange("p h t -> p (h t)"), 256), ("attm", attm[:].rearrange("p h t -> p (h t)"), 512),
                          ("Gs", Gs[:], 128), ("e1", e1[:], 128), ("e2", e2[:], 128), ("egl", egl[:, 0:2], 2), ("G32", G32[0][:], 256),
                          ("Gp32", Gp32[:], 256), ("Gpbf", Gpbf[:], 256), ("qd", qd[:], 128), ("kd", kd[:], 128), ("gs", gs[:].rearrange("p h t -> p (h t)"), 256)]
                    dflat = dbgt[:].rearrange("p a b -> p (a b)")
                    for di_, (nm_, ap_, n_) in enumerate(dl):
                        cpy(dflat[:, 0:n_], ap_, [nm_], ["dbgt"])
                        dma(dbg2_o[di_, :, 0:n_], dflat[:, 0:n_], ["dbgt"], ())
                if DBG and l == 0 and b in (DBGB, 16):
                    cpy(dbgt[:], mixT[:], ["mixT%d" % k_ for k_ in range(8)], ["dbgt"])
                    dma(dbg_o[0 if b == DBGB else 1], dbgt[:], ["dbgt"], ())
                ck(100 * b + 7)
                w_out_part(4, 8)

            ck(5000)
            P.phase_switch()
            if l + 1 < NL:
                load_w_out(l + 1)
            h2_block(16)
            ai = 0
            for j in range(8):
                wb = j % 2
                for fm in range(4):
                    dma(wup[wb][:, :, fm * 128:(fm + 1) * 128],
                        w_up_d[l, :, j * 512 + fm * 128:j * 512 + (fm + 1) * 128].rearrange("(kc p) f -> p kc f", p=128),
                        (), ["wup%d_%d" % (wb, fm)], eng="pool")
                for fc in range(4):
                    dma(wdn[wb][:, fc, :], w_dn_d[l, j * 512 + fc * 128:j * 512 + (fc + 1) * 128, :], (), ["wdn%d_%d" % (wb, fc)], eng="pool")
                for (t0, n) in TIL:
                    cols = slice(t0, t0 + n)
                    a_ = ai % 2
                    ai += 1
                    for fm in range(4):
                        bkt, bkr = bank()
                        for kc in range(8):
                            mm(bkt[:, :n], wup[wb][:, kc, fm * 128:(fm + 1) * 128], h2T[:, kc, cols], ["wup%d_%d" % (wb, fm), "w_in"], [bkr],
                               start=(kc == 0), stop=(kc == 7))
                        r_ = fm % 2
                        act(relu_t[r_][:, :n], bkt[:, :n], AF.Relu, [bkr], ["relu%d" % r_])
                        tt(actb[a_][:, fm, :n], relu_t[r_][:, :n], relu_t[r_][:, :n], ALU.mult, ["relu%d" % r_], ["actb%d" % a_])
                    for m in range(8):
                        bkt, bkr = bank()
                        for fc in range(4):
                            mm(bkt[:, :n], wdn[wb][:, fc, m * 128:(m + 1) * 128], actb[a_][:, fc, :n], ["wdn%d_%d" % (wb, fc), "actb%d" % a_], [bkr],
                               start=(fc == 0), stop=(fc == 3))
                        tt(xT[:, m, cols], xT[:, m, cols], bkt[:, :n], ALU.add, ["x%d" % m, bkr], ["x%d" % m])
            P.phase_switch()
            if l + 1 < NL:
                load_w_in(l + 1)

        ck(6000)
        for bb2 in range(NBLK):
            cols = slice(bb2 * 128, bb2 * 128 + 128)
            rms_stats(cols, 128, XR)
            for kc in range(8):
                yo = yout[kc % 2]
                stt(yo[:], xT[:, kc, cols], gvec[:, 64 + kc:65 + kc], rstd[:, :128], ALU.mult, ALU.mult,
                    [XR[kc], "rstd", "gvec"], ["yout%d" % (kc % 2)])
                dma(yT_o[kc * 128:(kc + 1) * 128, cols], yo[:], ["yout%d" % (kc % 2)], ())

    except _Stop:
        pass

    P.emit(nc, stack)
    stack.close()
    return nc


def _consts():
    cst = np.zeros((128, 13, 128), np.float32)
    cst[:, 12] = 1.0
    i = np.arange(128)
    cst[:, 0] = np.eye(128)
    cst[:, 1] = 1.0 / 1024
    blk = (i[:, None] // 64 == i[None, :] // 64).astype(np.float32)
    cst[:, 2] = blk
    cst[:, 3] = blk / 64
    for ty, cs in ((0, 64), (1, 32)):
        same = (i[:, None] // cs == i[None, :] // cs)
        cst[:, 4 + ty] = (same & (i[:, None] <= i[None, :]))
        cst[:, 6 + ty] = same
        cst[:, 8 + ty] = -(same & (i[:, None] > i[None, :])).astype(np.float32)
        cst[:, 10 + ty] = np.broadcast_to((i % cs != 0).astype(np.float32)[None, :], (128, 128))
    cm = np.zeros((128, 2, 16), np.float32)
    for ty, cs in ((0, 64), (1, 32)):
        for c in range(128 // cs):
            cm[:, ty, c] = (i // cs == c)
            cm[:, ty, 4 + c] = -cm[:, ty, c]
        for h in range(4):
            cm[:, ty, 8 + h] = (i // 32 == h)
    invc = np.zeros((128, 2, 128), np.float32)
    t = np.arange(128)
    for g, win in enumerate((2, 4, 8, 16)):
        kc, p0 = g // 2, 64 * (g % 2)
        invc[p0:p0 + 64, kc, :] = 1.0 / np.minimum(win, t + 1)[None, :]
    return cst, cm, invc


def _colT(v):
    return v.reshape(v.shape[:-1] + (8, 128))


def kernel(x_prompt, x_sample, cache_pool, cache_attn_k, cache_attn_v, state_conv, state_delta,
           state_gla, attn_norm_g, w_in, pool_w, pool_scale, rel_bias, conv_w, a_log, dt_bias,
           delta_norm_g, gla_w_gk, gla_b_gk, gla_norm_g, w_out, mlp_norm_g, w_up, w_down,
           final_norm_g, _NL=DEPTH, _DBG=False, _CORES=NCORES, _STOP=None):
    f = np.float32
    cst, cm, invc = _consts()
    gvec = np.zeros((128, 72), f)
    gvec[:, 0:32] = attn_norm_g.reshape(4, 8, 128).transpose(2, 0, 1).reshape(128, 32)
    gvec[:, 32:64] = mlp_norm_g.reshape(4, 8, 128).transpose(2, 0, 1).reshape(128, 32)
    gvec[:, 64:72] = final_norm_g.reshape(8, 128).T
    pcol = np.zeros((128, 4, 8), f)
    pcol[:, :, 0:2] = pool_scale.reshape(4, 2, 128).transpose(2, 0, 1)
    pcol[:, :, 2] = np.concatenate([delta_norm_g, delta_norm_g], 1).T
    pcol[:, :, 3] = np.concatenate([gla_norm_g, gla_norm_g], 1).T
    pcol[:, :, 4] = gla_b_gk.T
    convw = np.ascontiguousarray(conv_w.reshape(4, 4, 6, 128).transpose(3, 0, 2, 1)).astype(f)
    hrow = np.zeros((128, 4, 8), f)
    hrow[:, :, 0:4] = a_log[None]
    hrow[:, :, 4:8] = dt_bias[None]
    poolw = np.zeros((4, 2, 128, 128), f)
    for g in range(4):
        kc, p0 = g // 2, 64 * (g % 2)
        poolw[:, kc, p0:p0 + 64, p0:p0 + 64] = pool_w[:, g]
    wgk = np.ascontiguousarray(gla_w_gk.transpose(1, 0, 2)).astype(f)
    NEG = f(-100.0)
    k = np.arange(128)[:, None]
    q = np.arange(128)[None, :]
    bp = np.zeros((4, 128, 5, 4, 128), f)
    for r in range(5):
        idx = np.clip((r - 4) * 128 + k - q, -256, 256) + 256
        tab = rel_bias[:, :, idx]
        tab = tab.transpose(0, 2, 1, 3).copy()
        if r == 0:
            tab[:, 0:64, :, 64:128] = NEG
        if r == 4:
            tab[:, 64:128, :, 0:64] = NEG
        bp[:, :, r] = tab
    bp = bp.reshape(4, 128, 5, 512)
    q32 = np.arange(32)[None, :]
    bsc = np.zeros((4, 128, 4, 4, 32), f)
    for r in range(4):
        idx = np.clip(r * 128 + k - 512 - q32, -256, 256) + 256
        bsc[:, :, r] = rel_bias[:, :, idx].transpose(0, 2, 1, 3)
    bsc = bsc.reshape(4, 128, 4, 128)
    bsn = np.full((4, 128, 4, 4, 32), NEG, f)
    kk = np.arange(32)[:, None]
    idx = np.clip(kk - q32, -256, 256) + 256
    tabn = rel_bias[:, :, idx].transpose(0, 2, 1, 3)
    for s in range(4):
        bsn[:, 32 * s:32 * s + 32, s] = tabn
    bsn = bsn.reshape(4, 128, 4, 128)

    shared = dict(w_in=np.ascontiguousarray(w_in, f), w_out=np.ascontiguousarray(w_out, f),
                  w_up=np.ascontiguousarray(w_up, f), w_down=np.ascontiguousarray(w_down, f),
                  gvec=gvec, pcol=pcol, convw=convw, hrow=hrow, poolw=poolw, wgk=wgk, bp=bp, bsc=bsc, bsn=bsn,
                  cst=cst, cm=cm, invc=invc)
    in_maps = []
    for c in range(NCORES):
        sl = slice(4 * c, 4 * c + 4)
        xs = np.concatenate([x_prompt[c], x_sample[sl].reshape(128, D)], 0)
        m = dict(shared)
        m["xT"] = np.ascontiguousarray(xs.T, f)
        m["cpool"] = np.ascontiguousarray(cache_pool[:, sl].transpose(0, 1, 3, 2), f)
        m["ck"] = np.ascontiguousarray(cache_attn_k[:, sl].transpose(0, 1, 2, 4, 3).reshape(4, 4, 256, 512), f)
        m["cv"] = np.ascontiguousarray(cache_attn_v[:, sl], f)
        m["sconv"] = np.ascontiguousarray(state_conv[:, sl].transpose(0, 1, 3, 2), f)
        m["sdel"] = np.ascontiguousarray(state_delta[:, sl].transpose(0, 1, 3, 2, 4), f)
        m["sgla"] = np.ascontiguousarray(state_gla[:, sl], f)
        in_maps.append(m)

    nc = build(_NL, _DBG, _STOP)
    res = run_bass_kernel_spmd(nc, in_maps[:_CORES], core_ids=list(range(_CORES))).results
    res = list(res) + [res[0]] * (NCORES - _CORES)
    global _last_res
    _last_res = res

    y_prompt = np.zeros((8, 2048, D), f)
    y_sample = np.zeros((32, 32, D), f)
    pool_p = np.zeros((4, 8, 15, 256), f); pool_s = np.zeros((4, 32, 15, 256), f)
    k_p = np.zeros((4, 8, 4, 512, 64), f); v_p = np.zeros((4, 8, 4, 512, 64), f)
    k_s = np.zeros((4, 32, 4, 32, 64), f); v_s = np.zeros((4, 32, 4, 32, 64), f)
    conv_p = np.zeros((4, 8, 3, 768), f); conv_s = np.zeros((4, 32, 3, 768), f)
    delta_p = np.zeros((4, 8, 4, 64, 64), f); delta_s = np.zeros((4, 32, 4, 64, 64), f)
    gla_p = np.zeros((4, 8, 4, 32, 64), f); gla_s = np.zeros((4, 32, 4, 32, 64), f)
    for c in range(NCORES):
        r = res[c]
        sl = slice(4 * c, 4 * c + 4)
        yt = r["yT"].T
        y_prompt[c] = yt[:2048]
        y_sample[sl] = yt[2048:].reshape(4, 32, D)
        po = r["pool_o"].transpose(0, 1, 3, 2)
        pool_p[:, c] = po[:, 0]; pool_s[:, sl] = po[:, 1:]
        ko = r["k_o"]
        k_p[:, c] = ko[:, :, :512].reshape(4, 4, 64, 512).transpose(0, 1, 3, 2)
        k_s[:, sl] = ko[:, :, 512:].reshape(4, 4, 64, 4, 32).transpose(0, 3, 1, 4, 2)
        vo = r["v_o"]
        v_p[:, c] = vo[:, :512].reshape(4, 512, 4, 64).transpose(0, 2, 1, 3)
        v_s[:, sl] = vo[:, 512:].reshape(4, 4, 32, 4, 64).transpose(0, 1, 3, 2, 4)
        co = r["conv_o"].transpose(0, 1, 3, 2)
        conv_p[:, c] = co[:, 0]; conv_s[:, sl] = co[:, 1:]
        do = r["del_o"].transpose(0, 1, 3, 2, 4)
        delta_p[:, c] = do[:, 0]; delta_s[:, sl] = do[:, 1:]
        go = r["gla_o"]
        gla_p[:, c] = go[:, 0]; gla_s[:, sl] = go[:, 1:]
    return (y_prompt, y_sample, pool_p, k_p, v_p, conv_p, delta_p, gla_p,
            pool_s, k_s, v_s, conv_s, delta_s, gla_s)
```

```python
import contextlib
import numpy as np
import concourse.bass as bass
import concourse.mybir as mybir
from concourse.bass_utils import run_bass_kernel_spmd

F32 = mybir.dt.float32
BF16 = mybir.dt.bfloat16
AF = mybir.ActivationFunctionType
ALU = mybir.AluOpType

NCORES = 8
DEPTH = 4
D = 1024
NT = 2176
NBLK = 17
INC = 2840
EPS = 1e-6
ENGS = ["pe", "act", "dve", "pool", "sp"]
RING = 8
OVW_ = 15400
DBGB = 12


class Op:
    __slots__ = ("eng", "fn", "deps", "sig", "sigval", "dma", "k")


class Prog:
    def __init__(self):
        self.ops = {e: [] for e in ENGS}
        self.last_w = {}
        self.readers = {}
        self.ndma = {e: 0 for e in ENGS}
        self.alias = {}
        self.ov_res = set()
        self.ov_recent = {e: [] for e in ENGS}
        self.fence = []
        self.last_acc = {}
        self.capture = None

    def phase_switch(self):
        f = []
        for e in ENGS:
            f.extend(self.ov_recent[e])
        self.fence = f
        self.ov_recent = {e: [] for e in ENGS}

    def add(self, eng, fn, rd=(), wr=(), dma=False):
        if self.capture is not None:
            self.capture.append((eng, fn, tuple(rd), tuple(wr), dma))
            return None
        op = Op()
        op.eng, op.fn, op.dma, op.sig, op.sigval, op.k = eng, fn, dma, False, 0, 0
        wr = list(wr)
        for r in list(wr):
            wr.extend(self.alias.get(r, ()))
        deps = set()
        for r in rd:
            w = self.last_w.get(r)
            if w is not None:
                deps.add(w)
        for r in wr:
            w = self.last_w.get(r)
            if w is not None:
                deps.add(w)
            for x in self.readers.get(r, ()):
                deps.add(x)
        for r in list(rd) + list(wr):
            if r.startswith("pb"):
                la = self.last_acc.get(r)
                if la is not None and la.eng != eng:
                    deps.add(la)
                self.last_acc[r] = op
        touches_ov = any(r in self.ov_res for r in rd) or any(r in self.ov_res for r in wr)
        if touches_ov:
            deps.update(self.fence)
        for r in rd:
            self.readers.setdefault(r, []).append(op)
        for r in wr:
            self.last_w[r] = op
            self.readers[r] = []
        deps.discard(op)
        op.deps = [d for d in deps if d.dma or d.eng != eng or eng != "pe"]
        if dma:
            op.k = self.ndma[eng]
            self.ndma[eng] += 1
        self.ops[eng].append(op)
        if touches_ov:
            lst = self.ov_recent[eng]
            lst.append(op)
            keep = RING + 1
            if len(lst) > keep:
                nd = [o for o in lst if not o.dma][-1:]
                dd = [o for o in lst if o.dma][-RING:]
                self.ov_recent[eng] = dd + nd
        return op

    def emit(self, nc, stack):
        for e in ENGS:
            for op in self.ops[e]:
                for d in op.deps:
                    d.sig = True
        esem = {e: stack.enter_context(nc.semaphore("es_" + e)) for e in ENGS}
        dsem = {e: [stack.enter_context(nc.semaphore("ds_%s_%d" % (e, i))) for i in range(RING)]
                for e in ENGS if self.ndma[e] > 0}
        for e in ENGS:
            c = 0
            for op in self.ops[e]:
                if op.sig and not op.dma:
                    c += 1
                    op.sigval = c
        block = stack.enter_context(nc.Block())

        def run(e, eng):
            waited = {}

            def wait(key, sem, val):
                if waited.get(key, 0) < val:
                    eng.wait_ge(sem, val)
                    waited[key] = val

            for op in self.ops[e]:
                for d in op.deps:
                    if d.dma:
                        wait(("d", d.eng, d.k % RING), dsem[d.eng][d.k % RING], 16 * (d.k // RING + 1))
                    else:
                        wait(("e", d.eng), esem[d.eng], d.sigval)
                if op.dma:
                    if op.k >= RING:
                        wait(("d", e, op.k % RING), dsem[e][op.k % RING], 16 * (op.k // RING))
                    op.fn(eng).then_inc(dsem[e][op.k % RING], 16)
                else:
                    ins = op.fn(eng)
                    if op.sig:
                        ins.then_inc(esem[e], 1)
            n = self.ndma[e]
            for s in range(min(n, RING)):
                last = ((n - 1 - s) // RING) * RING + s
                wait(("d", e, s), dsem[e][s], 16 * (last // RING + 1))

        @block.tensor
        def _(t):
            run("pe", t)

        @block.scalar
        def _(t):
            run("act", t)

        @block.vector
        def _(t):
            run("dve", t)

        @block.gpsimd
        def _(t):
            run("pool", t)

        @block.sync
        def _(t):
            run("sp", t)


def bc(ap, shape):
    return ap.broadcast_to(list(shape))


class _Stop(Exception):
    pass


def build(NL=DEPTH, DBG=False, STOP=None):
    nc = bass.Bass("TRN2", target_bir_lowering=False)
    P = Prog()
    stack = contextlib.ExitStack()

    def din(name, shape):
        return nc.dram_tensor(name, list(shape), F32, kind="ExternalInput").ap()

    def dout(name, shape):
        return nc.dram_tensor(name, list(shape), F32, kind="ExternalOutput").ap()

    xT_d = din("xT", [D, NT])
    w_in_d = din("w_in", [DEPTH, D, INC])
    w_out_d = din("w_out", [DEPTH, D, D])
    w_up_d = din("w_up", [DEPTH, D, 4096])
    w_dn_d = din("w_down", [DEPTH, 4096, D])
    gvec_d = din("gvec", [128, 72])
    pcol_d = din("pcol", [128, DEPTH, 8])
    convw_d = din("convw", [128, DEPTH, 6, 4])
    hrow_d = din("hrow", [128, DEPTH, 8])
    poolw_d = din("poolw", [DEPTH, 2, 128, 128])
    wgk_d = din("wgk", [16, DEPTH, 128])
    bp_d = din("bp", [DEPTH, 128, 5, 512])
    bsc_d = din("bsc", [DEPTH, 128, 4, 128])
    bsn_d = din("bsn", [DEPTH, 128, 4, 128])
    cst_d = din("cst", [128, 13, 128])
    cm_d = din("cm", [128, 2, 16])
    invc_d = din("invc", [128, 2, 128])
    cpool_d = din("cpool", [DEPTH, 4, 256, 15])
    ck_d = din("ck", [DEPTH, 4, 256, 512])
    cv_d = din("cv", [DEPTH, 4, 4, 512, 64])
    sconv_d = din("sconv", [DEPTH, 4, 768, 3])
    sdel_d = din("sdel", [DEPTH, 4, 64, 4, 64])
    sgla_d = din("sgla", [DEPTH, 4, 4, 32, 64])

    yT_o = dout("yT", [D, NT])
    pool_o = dout("pool_o", [DEPTH, 5, 256, 15])
    k_o = dout("k_o", [DEPTH, 256, 640])
    v_o = dout("v_o", [DEPTH, 640, 256])
    conv_o = dout("conv_o", [DEPTH, 5, 768, 3])
    del_o = dout("del_o", [DEPTH, 5, 64, 4, 64])
    gla_o = dout("gla_o", [DEPTH, 5, 4, 32, 64])
    dbg_o = dout("dbg_o", [2, 128, 8, 128]) if DBG else None
    dbg2_o = dout("dbg2_o", [16, 128, 1024]) if DBG else None

    def sb(name, shape, dt=F32):
        return stack.enter_context(nc.sbuf_tensor("s_" + name, list(shape), dt))

    xT = sb("xT", [128, 8, NT])
    w_in = sb("w_in_sb", [128, 8, INC], BF16)
    w_out = sb("w_out_sb", [128, 8, D], BF16)
    h2T = w_in[:].rearrange("p k c -> p (k c)")[:, 0:8 * NT].rearrange("p (k t) -> p k t", k=8)
    gvec = sb("gvec", [128, 72])
    pcol = sb("pcol", [128, DEPTH, 8])
    convw = sb("convw", [128, DEPTH, 6, 4])
    hrow = sb("hrow", [128, DEPTH, 8])
    negA = sb("negA", [128, 4])
    poolw = sb("poolw", [128, 2, 128], BF16)
    wgk = sb("wgk", [16, DEPTH, 128], BF16)
    bp = sb("bp", [128, 5, 512], BF16)
    bsc = sb("bsc", [128, 4, 128], BF16)
    bsn = sb("bsn", [128, 4, 128], BF16)
    cstf = sb("cstf", [128, 9, 128])
    cstb = sb("cstb", [128, 4, 128], BF16)
    cm = sb("cm", [128, 2, 16])
    invc = sb("invc", [128, 2, 128])
    sq = [sb("sq%d" % i, [128, 128], BF16) for i in range(2)]
    lnt = sb("lnt", [128, 512])
    rstd = sb("rstd", [128, 128])

    OVW = OVW_
    OV = sb("ov", [128, OVW])
    ov_off = [0]
    ov_offs = {}
    P.alias = {}

    def ovview(off, words, shape, dt, parts):
        v = OV[0:parts, off:off + words]
        if dt != F32:
            v = v.bitcast(dt)
        if len(shape) == 3:
            v = v.rearrange("p (a b) -> p a b", a=shape[1])
        return v

    def ova(name, shape, dt=F32, parts=128):
        n = 1
        for d_ in shape[1:]:
            n *= d_
        words = n if dt == F32 else (n + 1) // 2
        off = ov_off[0]
        assert off + words <= OVW, ("overlay overflow", name, off, words)
        ov_off[0] = off + words
        ov_offs[name] = (off, words)
        P.ov_res.add(name)
        return ovview(off, words, shape, dt, parts)

    def ovat(name, off, shape, dt=F32, parts=128):
        n = 1
        for d_ in shape[1:]:
            n *= d_
        words = n if dt == F32 else (n + 1) // 2
        ov_offs[name] = (off, words)
        P.ov_res.add(name)
        return ovview(off, words, shape, dt, parts)

    def union(host, members):
        P.alias[host] = list(members)
        for m_ in members:
            P.alias[m_] = [host]

    hT = ova("hT", [128, 8, 128], BF16)
    mixT = ova("mixT", [128, 8, 128], BF16)
    uext = ova("uext", [128, 2, 192])
    pwa = ova("pwa", [128, 2, 192])
    pwb = ova("pwb", [128, 2, 192])
    sqo = ovat("sqo", ov_offs["pwa"][0], [128, 256], BF16)
    tno = ovat("tno", ov_offs["pwa"][0] + 128, [128, 256])
    union("pwa", ["sqo", "tno"])
    Sdec = ovat("Sdec", ov_offs["pwb"][0], [64, 4, 64], parts=64)
    union("pwb", ["Sdec"])
    pooled = ova("pooled", [128, 2, 128], BF16)
    qb = ova("qb", [128, 2, 128], BF16)
    kbT = ova("kbT", [128, 2, 640], BF16)
    for i in range(5):
        P.ov_res.add("kbT%d" % i)
    vb1 = ova("vb1", [128, 5, 384], BF16)
    tS = ova("tS", [128, 512])
    rc = tS
    QKm = ovat("QKm", ov_offs["tS"][0], [128, 4, 128], BF16)
    wT = ovat("wT", ov_offs["tS"][0] + 256, [64, 4, 128], BF16, parts=64)
    union("tS", ["QKm", "wT"])
    cext = ova("cext", [128, 6, 144])
    cacc = ova("cacc", [128, 6, 128])
    sqc = ova("sqc", [128, 4, 128], BF16)
    qnT = ova("qnT", [128, 2, 128], BF16)
    knT = ova("knT", [128, 2, 128], BF16)
    vcT = ova("vcT", [128, 2, 128], BF16)
    zs = ova("zs", [128, 2, 128], BF16)
    gs = ova("gs", [128, 2, 128], BF16)
    ab = ova("ab", [128, 8])
    sm = ova("sm", [128, 64])
    gm = ova("gm", [128, 16])
    glr = ova("glr", [64, 16], parts=64)
    eglr = ova("eglr", [64, 16], parts=64)
    qgT = ova("qgT", [64, 4, 128], BF16, parts=64)
    Ee = ova("Ee", [128, 4, 128])
    qm = ova("qm", [128, 4, 128], BF16)
    ktc = ova("ktc", [128, 4, 128], BF16)
    Em = ova("Em", [128, 4, 128])
    sqoD = ovat("sqoD", ov_offs["Ee"][0], [128, 256], BF16)
    tnoD = ovat("tnoD", ov_offs["Ee"][0] + 128, [128, 256])
    lntD = ovat("lntD", ov_offs["Em"][0], [128, 256])
    union("Ee", ["sqoD", "tnoD"])
    union("Em", ["lntD"])
    attm = ova("attm", [128, 4, 128], BF16)
    Gp32 = ova("Gp32", [128, 256])
    Lp = [ova("Lp%d" % i, [128, 4, 128], BF16) for i in range(2)]
    Bpw = [ova("Bpw%d" % i, [128, 4, 128], BF16) for i in range(2)]
    Yb = [ova("Yb%d" % i, [128, 4, 128], BF16) for i in range(2)]
    kcs = ovat("kcs", ov_offs["Lp0"][0], [128, 2, 512], BF16)
    vcs = ovat("vcs", ov_offs["Bpw0"][0], [128, 4, 384], BF16)
    union("kcs", ["Lp0", "Lp1"])
    union("vcs", ["Bpw0", "Bpw1", "Yb0"])
    pT1 = ovat("pT", ov_offs["Yb1"][0], [128, 512], BF16)
    union("Yb1", ["pT"])
    ktil = ova("ktil", [128, 4, 64], BF16)
    Y32 = ova("Y32", [128, 4, 128])
    kf32 = ovat("kf32", ov_offs["Y32"][0], [128, 2, 128])
    vf32 = ovat("vf32", ov_offs["Y32"][0] + 256, [128, 256])
    union("Y32", ["kf32", "vf32"])
    tmpu = ova("tmpu", [128, 256])
    spd = ova("spd", [128, 128])
    Gpbf = ova("Gpbf", [128, 256], BF16)
    unc = ova("unc", [128, 4, 256], BF16)
    Gs = ova("Gs", [128, 128])
    Dm = ova("Dm", [128, 128])
    e1 = ova("e1", [128, 128])
    e2 = ova("e2", [128, 128])
    S32 = [ova("S32_%d" % i, [64, 4, 64], parts=64) for i in range(2)]
    Sbf = [ova("Sbf_%d" % i, [64, 4, 64], BF16, parts=64) for i in range(2)]
    qd = ova("qd", [128, 128])
    kd = ova("kd", [128, 128])
    gkl = ova("gkl", [16, 128], BF16, parts=16)
    egl = ova("egl", [128, 4])
    qt = ova("qt", [128, 128], BF16)
    kt = ova("kt", [128, 128], BF16)
    vD = ova("vD", [128, 256], BF16)
    G32 = [ova("G32_%d" % i, [128, 256]) for i in range(2)]
    dbgt = ova("dbgt", [128, 8, 128]) if DBG else None
    lntP = ova("lntP", [128, 128])
    mixer_words = ov_off[0]
    for nm_, k_ in (("hT", 8), ("mixT", 8), ("cacc", 6), ("cext", 6), ("unc", 4)):
        for i_ in range(k_):
            P.ov_res.add("%s%d" % (nm_, i_))
    for i_ in range(5):
        P.ov_res.add("vb1_%d" % i_)
    ov_off[0] = 0
    wup = [ova("wup%d" % i, [128, 8, 512], BF16) for i in range(2)]
    wdn = [ova("wdn%d" % i, [128, 4, D], BF16) for i in range(2)]
    actb = [ova("actb%d" % i, [128, 4, 512], BF16) for i in range(2)]
    relu_t = [ova("relu%d" % i, [128, 512], BF16) for i in range(2)]
    yout = [ova("yout%d" % i, [128, 128]) for i in range(2)]
    mlp_words = ov_off[0]
    for i_ in range(2):
        for k_ in range(4):
            P.ov_res.add("wup%d_%d" % (i_, k_))
            P.ov_res.add("wdn%d_%d" % (i_, k_))
    print("overlay words: mixer", mixer_words, "mlp", mlp_words, "of", OVW)

    banks = [stack.enter_context(nc.psum_tensor("pb%d" % i, [128, 512], F32)) for i in range(8)]
    bk_i = [0]

    bank_sess = {}

    held = set()
    cur_pool = [None]
    pool_i = {}

    def bank(hold=False):
        pool = cur_pool[0]
        while True:
            if pool is None:
                i = bk_i[0] % 8
                bk_i[0] += 1
            else:
                k_ = pool_i.get(pool, 0)
                pool_i[pool] = k_ + 1
                i = pool[k_ % len(pool)]
            if ("pb%d" % i) not in held:
                break
        if hold:
            held.add("pb%d" % i)
        bank_sess["pb%d" % i] = set()
        return banks[i], "pb%d" % i

    def mm(out, lhsT, rhs, rd, wr, start=True, stop=True):
        st = bank_sess[wr[0]]
        base = out.base_partition()
        quads = set(range(base // 32, (base + out.shape[0] + 31) // 32))
        newq = quads - st
        if newq:
            assert newq == quads, ("mixed psum quadrants", wr, base, out.shape)
            s_ = True
            st |= quads
        else:
            s_ = False
        P.add("pe", lambda e: e.matmul(out, lhsT, rhs, start=s_, stop=True, skip_group_check=True), rd, wr)

    def act(out, in_, func, rd, wr, bias=None, scale=None):
        kw = {}
        if bias is not None:
            kw["bias"] = bias
        if scale is not None:
            kw["scale"] = scale
        P.add("act", lambda e: e.activation(out, in_, func, **kw), rd, wr)

    def tt(out, in0, in1, op, rd, wr, eng="dve"):
        P.add(eng, lambda e: e.tensor_tensor(out=out, in0=in0, in1=in1, op=op), rd, wr)

    def ts(out, in0, s1, op0, rd, wr, s2=None, op1=None, eng="dve"):
        if op1 is None:
            P.add(eng, lambda e: e.tensor_scalar(out=out, in0=in0, scalar1=s1, scalar2=None, op0=op0), rd, wr)
        else:
            P.add(eng, lambda e: e.tensor_scalar(out=out, in0=in0, scalar1=s1, scalar2=s2, op0=op0, op1=op1), rd, wr)

    def stt(out, in0, scalar, in1, op0, op1, rd, wr):
        P.add("dve", lambda e: e.scalar_tensor_tensor(out=out, in0=in0, scalar=scalar, in1=in1, op0=op0, op1=op1), rd, wr)

    def cpy(out, in_, rd, wr, eng="dve"):
        P.add(eng, lambda e: e.tensor_copy(out, in_), rd, wr)

    def mset(ap, val, wr, eng="dve"):
        P.add(eng, lambda e: e.memset(ap, val), (), wr)

    def dma(out, in_, rd, wr, eng="sp"):
        P.add(eng, lambda e: e.dma_start(out=out, in_=in_), rd, wr, dma=True)

    def scan(out, d0, d1, rd, wr):
        P.add("dve", lambda e: e.tensor_tensor_scan(out=out, data0=d0, data1=d1, initial=0.0, op0=ALU.mult, op1=ALU.add), rd, wr)

    def recip(out, in_, rd, wr):
        P.add("dve", lambda e: e.reciprocal(out, in_), rd, wr)

    IDENT, ONESN, BLK1, BLK64, ONES = 0, 1, 2, 3, 8
    VOFF = [0, 64, 192, 256]
    VCOL = [0, 128, 192, 320]

    def ck(n):
        if STOP is not None and STOP == n:
            raise _Stop()

    try:
        for kc in range(8):
            dma(xT[:, kc, :], xT_d[kc * 128:(kc + 1) * 128, :], (), ["x%d" % kc])
        XR = ["x%d" % k for k in range(8)]
        dma(gvec[:], gvec_d, (), ["gvec"])
        dma(pcol[:], pcol_d, (), ["pcol"])
        ts(pcol[:, :, 5], pcol[:, :, 4], -1.0, ALU.mult, ["pcol"], ["pcol"])
        dma(convw[:], convw_d, (), ["convw"])
        dma(hrow[:], hrow_d, (), ["hrow"])
        dma(cstf[:], cst_d[:, 4:13, :], (), ["cstf"])
        dma(cm[:], cm_d, (), ["cm"])
        dma(invc[:], invc_d, (), ["invc"])
        dma(cstb[:], cst_d[:, 0:4, :], (), ["cstb"], eng="pool")
        dma(wgk[:], wgk_d, (), ["wgk"], eng="pool")

        def load_w_in(l):
            for kc in range(8):
                dma(w_in[:, kc, :], w_in_d[l, kc * 128:(kc + 1) * 128, :], (), ["w_in"], eng="pool")

        def load_w_out(l):
            for kc in range(8):
                dma(w_out[:, kc, :], w_out_d[l, kc * 128:(kc + 1) * 128, :], (), ["w_out"], eng="pool")

        load_w_in(0)
        load_w_out(0)

        def rms_stats(cols, n, xres, lbuf=None, lres="lnt"):
            if lbuf is None:
                lbuf = lnt
            bkt, bkr = bank()
            for kc in range(8):
                s = sq[kc % 2]
                act(s[:, :n], xT[:, kc, cols], AF.Square, [xres[kc]], ["sq%d" % (kc % 2)])
                mm(bkt[:, :n], cstb[:, ONESN, :], s[:, :n], ["sq%d" % (kc % 2), "cstb"], [bkr], start=(kc == 0), stop=(kc == 7))
            act(lbuf[:, :n], bkt[:, :n], AF.Ln, [bkr], [lres], bias=EPS)
            act(rstd[:, :n], lbuf[:, :n], AF.Exp, [lres], ["rstd"], scale=-0.5)

        TIL = [(0, 512), (512, 512), (1024, 512), (1536, 512), (2048, 128)]

        for l in range(NL):
            dma(bp[:], bp_d[l], (), ["bp"], eng="pool")
            dma(bsc[:], bsc_d[l], (), ["bsc"], eng="pool")
            dma(bsn[:], bsn_d[l], (), ["bsn"], eng="pool")
            dma(poolw[:], poolw_d[l].rearrange("k p c -> p k c"), (), ["poolw"], eng="pool")
            act(negA[:], hrow[:, l, 0:4], AF.Exp, ["hrow"], ["negA"])
            ts(negA[:], negA[:], -1.0, ALU.mult, ["negA"], ["negA"])
            mset(vb1[:], 1.0, ["vb1_%d" % s_ for s_ in range(5)])
            mset(S32[0][:], 0.0, ["S32_0"])
            mset(Sbf[0][:], 0.0, ["Sbf_0"])
            mset(G32[0][:], 0.0, ["G32_0"])

            def h2_block(bb2):
                cols2 = slice(bb2 * 128, bb2 * 128 + 128)
                rms_stats(cols2, 128, XR, lntP, "lntP")
                for kc in range(8):
                    stt(h2T[:, kc, cols2], xT[:, kc, cols2], gvec[:, 32 + l * 8 + kc:32 + l * 8 + kc + 1], rstd[:, :128],
                        ALU.mult, ALU.mult, [XR[kc], "rstd", "gvec"], ["w_in"])

            for b in range(NBLK):
                smp = (b == 16)
                ty = 1 if smp else 0
                nch = 4 if smp else 2
                cs = 32 if smp else 64
                cols = slice(b * 128, (b + 1) * 128)
                TRI, SAME, MSLN, RST = 0 + ty, 2 + ty, 4 + ty, 6 + ty
                xres = XR
                CX = ["cext%d" % g_ for g_ in range(6)]

                if smp:
                    uxs = uext[:, :, 0:188].rearrange("p g (s t) -> p g s t", s=4)
                    cxs = cext[:, :, 0:140].rearrange("p g (s t) -> p g s t", s=4)
                slot = b % 5
                need_kv = smp or b >= 12

                def proj(c0, M):
                    bkt, bkr = bank()
                    for kc in range(8):
                        mm(bkt[0:M, 0:128], w_in[:, kc, c0:c0 + M], hT[:, kc, :], ["w_in", "hT%d" % kc], [bkr],
                           start=(kc == 0), stop=(kc == 7))
                    return bkt, bkr

                def front_early(bq):
                    smq = (bq == 16)
                    colq = slice(bq * 128, (bq + 1) * 128)
                    slq = bq % 5
                    nkv = smq or bq >= 12
                    rms_stats(colq, 128, XR, lntP, "lntP")
                    for kc in range(8):
                        stt(hT[:, kc, :], xT[:, kc, colq], gvec[:, l * 8 + kc:l * 8 + kc + 1], rstd[:, :128],
                            ALU.mult, ALU.mult, [XR[kc], "rstd", "gvec"], ["hT%d" % kc])
                    if smq:
                        for g in range(2):
                            dma(uxs[:, g, :, 0:15], cpool_d[l, :, g * 128:(g + 1) * 128, :].rearrange("s p t -> p s t"), (), ["uext"])
                        for g in range(6):
                            dma(cxs[:, g, :, 0:3], sconv_d[l, :, g * 128:(g + 1) * 128, :].rearrange("s p t -> p s t"), (), ["cext%d" % g])
                    elif bq == 0:
                        mset(uext[:, :, 0:15], 0.0, ["uext"])
                        mset(cext[:, :, 0:3], 0.0, CX)
                    else:
                        cpy(uext[:, :, 0:15], uext[:, :, 128:143], ["uext"], ["uext"])
                        cpy(cext[:, :, 0:3], cext[:, :, 128:131], CX, CX)
                    for g in range(2):
                        bkt, bkr = proj(g * 128, 128)
                        if smq:
                            act(uxs[:, g, :, 15:47], bkt[:, 0:128].rearrange("p (s t) -> p s t", s=4), AF.Copy, [bkr], ["uext"])
                        else:
                            act(uext[:, g, 15:143], bkt[:, 0:128], AF.Copy, [bkr], ["uext"])
                    for g in range(2):
                        bkt, bkr = proj(256 + g * 128, 128)
                        act(qb[:, g, :], bkt[:, 0:128], AF.Copy, [bkr], ["qb"], scale=0.125)
                    for g in range(2):
                        bkt, bkr = proj(512 + g * 128, 128)
                        act(kbT[:, g, slq * 128:(slq + 1) * 128], bkt[:, 0:128], AF.Copy, [bkr], ["kbT%d" % slq])
                    for g in range(6):
                        bkt, bkr = proj(1024 + g * 128, 128)
                        if smq:
                            cpy(cxs[:, g, :, 3:35], bkt[:, 0:128].rearrange("p (s t) -> p s t", s=4), [bkr], ["cext%d" % g])
                        else:
                            cpy(cext[:, g, 3:131], bkt[:, 0:128], [bkr], ["cext%d" % g])
                    bkt, bkr = proj(2056, 128)
                    act(qd[:], bkt[:, 0:128], AF.Copy, [bkr], ["qd"], scale=float(32 ** -0.5))
                    bkt, bkr = proj(2184, 128)
                    cpy(kd[:], bkt[:, 0:128], [bkr], ["kd"])
                    bkt, bkr = proj(2824, 16)
                    cpy(gkl[:, :], bkt[0:16, 0:128], [bkr], ["gkl"])
                    for g in range(6):
                        for j in range(4):
                            if smq:
                                src = cxs[:, g, :, j:j + 32]
                                dst = cacc[:, g, :].rearrange("p (s t) -> p s t", s=4)
                            else:
                                src = cext[:, g, j:j + 128]
                                dst = cacc[:, g, :]
                            if j == 0:
                                ts(dst, src, convw[:, l, g, 0:1], ALU.mult, ["cext%d" % g, "convw"], ["cacc%d" % g])
                            else:
                                stt(dst, src, convw[:, l, g, j:j + 1], dst, ALU.mult, ALU.add, ["cext%d" % g, "convw", "cacc%d" % g], ["cacc%d" % g])

                def front_late(bq):
                    slq = bq % 5
                    nkv = (bq == 16) or bq >= 12
                    if nkv:
                        for g in range(2):
                            bkt, bkr = proj(512 + g * 128, 128)
                            act(kf32[:, g, :], bkt[:, 0:128], AF.Copy, [bkr], ["kf32"])
                    for g in range(2):
                        bkt, bkr = proj(1792 + g * 128, 128)
                        act(zs[:, g, :], bkt[:, 0:128], AF.Silu, [bkr], ["zs"])
                    for g in range(2):
                        bkt, bkr = proj(2568 + g * 128, 128)
                        act(gs[:, g, :], bkt[:, 0:128], AF.Silu, [bkr], ["gs"])
                    bkt, bkr = bank()
                    bk2, bk2r = bank()
                    for kc in range(8):
                        mm(bkt[:, 0:256], hT[:, kc, :], w_in[:, kc, 768:1024], ["w_in", "hT%d" % kc], [bkr], start=(kc == 0), stop=(kc == 7))
                        mm(bkt[:, 256:512], hT[:, kc, :], w_in[:, kc, 2312:2568], ["w_in", "hT%d" % kc], [bkr], start=(kc == 0), stop=(kc == 7))
                        mm(bk2[:, 0:8], hT[:, kc, :], w_in[:, kc, 2048:2056], ["w_in", "hT%d" % kc], [bk2r], start=(kc == 0), stop=(kc == 7))
                    for h in range(4):
                        cpy(vb1[:, slq, VCOL[h]:VCOL[h] + 64], bkt[:, 64 * h:64 * h + 64], [bkr], ["vb1_%d" % slq])
                    if nkv:
                        act(vf32[:], bkt[:, 0:256], AF.Copy, [bkr], ["vf32"])
                    act(vD[:], bkt[:, 256:512], AF.Copy, [bkr], ["vD"])
                    cpy(ab[:], bk2[:, 0:8], [bk2r], ["ab"])

                def w_out_part(k0, k1):
                    for m in range(8):
                        bkt, bkr = bank()
                        for kc in range(k0, k1):
                            mm(bkt[:, 0:128], w_out[:, kc, m * 128:(m + 1) * 128], mixT[:, kc, :], ["w_out", "mixT%d" % kc], [bkr],
                               start=(kc == k0), stop=(kc == k1 - 1))
                        tt(xT[:, m, cols], xT[:, m, cols], bkt[:, 0:128], ALU.add, ["x%d" % m, bkr], ["x%d" % m])

                prefetched = (1 <= b <= 15)
                if not prefetched:
                    front_early(b)
                do_fab = (1 <= b <= 15)
                listF, listO, listA, listB = [], [], [], []
                P.capture = listF
                cur_pool[0] = (0, 1, 2) if do_fab else None
                front_late(b)
                P.capture = listO

                ck(100 * b + 2)
                if need_kv:
                    c0 = 512 if smp else (b - 12) * 128
                    for g in range(2):
                        dma(k_o[l, g * 128:(g + 1) * 128, c0:c0 + 128], kf32[:, g, :], ["kf32"], ())
                    dma(v_o[l, c0:c0 + 128, :], vf32[:], ["vf32"], ())
                if smp:
                    for g in range(2):
                        dma(pool_o[l, 1:5, g * 128:(g + 1) * 128, :].rearrange("s p t -> p s t"), uxs[:, g, :, 32:47], ["uext"], ())
                    for g in range(6):
                        dma(conv_o[l, 1:5, g * 128:(g + 1) * 128, :].rearrange("s p t -> p s t"), cxs[:, g, :, 32:35], ["cext%d" % g], ())
                elif b == 15:
                    for g in range(2):
                        dma(pool_o[l, 0, g * 128:(g + 1) * 128, :], uext[:, g, 128:143], ["uext"], ())
                    for g in range(6):
                        dma(conv_o[l, 0, g * 128:(g + 1) * 128, :], cext[:, g, 128:131], ["cext%d" % g], ())

                P.capture = listA
                cur_pool[0] = (3,) if do_fab else None
                if smp:
                    def V(t, a, bb):
                        return t[:, :, 0:188].rearrange("p g (s t) -> p g s t", s=4)[:, :, :, a:bb]
                    L_ = 47
                else:
                    def V(t, a, bb):
                        return t[:, :, a:bb]
                    L_ = 143
                wins = [(uext, pwa, 1), (pwa, pwb, 2), (pwb, pwa, 4), (pwa, pwb, 8)]
                for gi, (src, dst, sh) in enumerate(wins):
                    lo = 2 * sh - 1
                    dres = "pwa" if dst is pwa else "pwb"
                    tt(V(dst, lo, L_), V(src, lo, L_), V(src, lo - sh, L_ - sh), ALU.add,
                       ["uext", "pwa", "pwb"], [dres])
                    kc, p0 = gi // 2, 64 * (gi % 2)
                    win = 2 * sh
                    if smp:
                        o_ = pooled[p0:p0 + 64, kc, :].rearrange("p (s t) -> p s t", s=4)
                        i0 = V(dst, 15, 47)[p0:p0 + 64, kc]
                        i1 = V(uext, 15, 47)[p0:p0 + 64, kc]
                    else:
                        o_ = pooled[p0:p0 + 64, kc, :]
                        i0 = dst[p0:p0 + 64, kc, 15:143]
                        i1 = uext[p0:p0 + 64, kc, 15:143]
                    if b == 0:
                        tt(tS[p0:p0 + 64, 0:128], i0, invc[p0:p0 + 64, kc, :], ALU.mult, ["invc", dres], ["tS"])
                        tt(o_, tS[p0:p0 + 64, 0:128], i1, ALU.subtract, ["uext", "tS"], ["pooled"])
                    else:
                        stt(o_, i0, 1.0 / win, i1, ALU.mult, ALU.subtract, ["uext", dres], ["pooled"])
                for kc in range(2):
                    bkt, bkr = bank()
                    mm(bkt[:, 0:128], poolw[:, kc, :], pooled[:, kc, :], ["poolw", "pooled"], [bkr])
                    ts(mixT[:, kc, :], bkt[:, 0:128], pcol[:, l, kc:kc + 1], ALU.mult, [bkr, "pcol"], ["mixT%d" % kc])

                P.capture = listB
                cur_pool[0] = (4, 5, 6, 7) if do_fab else None
                ob, obr = bank(hold=True)
                if not smp:
                    rlist = [r for r in range(5) if b - 4 + r >= 0]
                    for ri, r in enumerate(rlist):
                        ks = (b - 4 + r) % 5
                        scp = [bank(), bank()]
                        for h in range(4):
                            p0, g = 64 * (h % 2), h // 2
                            sc, scr = scp[h % 2]
                            mm(sc[:, 128 * g:128 * g + 128], kbT[p0:p0 + 64, g, ks * 128:(ks + 1) * 128], qb[p0:p0 + 64, g, :],
                               ["kbT%d" % ks, "qb"], [scr])
                            mm(sc[:, 128 * g:128 * g + 128], cstb[:, IDENT, :], bp[:, r, 128 * h:128 * h + 128], ["cstb", "bp"], [scr])
                        i2 = ri % 2
                        for par in range(2):
                            sc, scr = scp[par]
                            act(pT1[:].rearrange("p (h q) -> p h q", h=4)[:, par::2, :], sc[:, 0:256].rearrange("p (h q) -> p h q", h=2),
                                AF.Exp, [scr], ["pT"])
                        for h in range(4):
                            mm(ob[:, 128 * h:128 * h + 128], vb1[:, ks, VOFF[h]:VOFF[h] + 128], pT1[:, 128 * h:128 * h + 128],
                               ["vb1_%d" % ks, "pT"], [obr], start=(ri == 0), stop=(ri == len(rlist) - 1))
                    ck(100 * b + 44)
                    for h in range(4):
                        po, ps_ = (0, 64) if h % 2 == 0 else (64, 0)
                        recip(rc[po:po + 64, 128 * h:128 * h + 128], ob[ps_:ps_ + 64, 128 * h:128 * h + 128], [obr], ["tS"])
                        ck(100 * b + 45)
                        tt(mixT[po:po + 64, 2 + h // 2, :], ob[po:po + 64, 128 * h:128 * h + 128], rc[po:po + 64, 128 * h:128 * h + 128],
                           ALU.mult, [obr, "tS"], ["mixT%d" % (2 + h // 2)])
                else:
                    for s in range(4):
                        dma(kcs[:], ck_d[l, s].rearrange("(g p) t -> p g t", p=128), (), ["kcs"], eng="pool")
                        mset(vcs[:], 1.0, ["vcs"])
                        for h in range(4):
                            dma(vcs[:, :, VCOL[h]:VCOL[h] + 64], cv_d[l, s, h].rearrange("(r p) d -> p r d", p=128), (), ["vcs"], eng="pool")
                        for r in range(5):
                            scp = [bank(), bank()]
                            for h in range(4):
                                p0, g = 64 * (h % 2), h // 2
                                sc, scr = scp[h % 2]
                                if r < 4:
                                    lhsT = kcs[p0:p0 + 64, g, r * 128:(r + 1) * 128]
                                    rdk = "kcs"
                                else:
                                    lhsT = kbT[p0:p0 + 64, g, slot * 128:(slot + 1) * 128]
                                    rdk = "kbT%d" % slot
                                mm(sc[:, 32 * g:32 * g + 32], lhsT, qb[p0:p0 + 64, g, 32 * s:32 * s + 32], [rdk, "qb"], [scr])
                                tab = bsc[:, r, :] if r < 4 else bsn[:, s, :]
                                mm(sc[:, 32 * g:32 * g + 32], cstb[:, IDENT, :], tab[:, 32 * h:32 * h + 32], ["cstb", "bsc", "bsn"], [scr])
                            i2 = r % 2
                            for par in range(2):
                                sc, scr = scp[par]
                                act(pT1[:, 0:128].rearrange("p (h q) -> p h q", h=4)[:, par::2, :], sc[:, 0:64].rearrange("p (h q) -> p h q", h=2),
                                    AF.Exp, [scr], ["pT"])
                            for h in range(4):
                                lhsT = vcs[:, r, VOFF[h]:VOFF[h] + 128] if r < 4 else vb1[:, slot, VOFF[h]:VOFF[h] + 128]
                                mm(ob[:, 128 * s + 32 * h:128 * s + 32 * h + 32], lhsT, pT1[:, 32 * h:32 * h + 32],
                                   ["vcs", "vb1_%d" % slot, "pT"], [obr], start=(r == 0), stop=(r == 4))
                    obv = ob[:].rearrange("p (s h q) -> p s h q", s=4, h=4)
                    rcv = rc[:].rearrange("p (s h q) -> p s h q", s=4, h=4)
                    for h in range(4):
                        po, ps_ = (0, 64) if h % 2 == 0 else (64, 0)
                        recip(rcv[po:po + 64, :, h, :], obv[ps_:ps_ + 64, :, h, :], [obr], ["tS"])
                        tt(mixT[po:po + 64, 2 + h // 2, :].rearrange("p (s q) -> p s q", s=4), obv[po:po + 64, :, h, :],
                           rcv[po:po + 64, :, h, :], ALU.mult, [obr, "tS"], ["mixT%d" % (2 + h // 2)])

                held.discard(obr)
                P.capture = None
                cur_pool[0] = None
                if do_fab:
                    keyed = []
                    for lst, off_, sc_ in ((listF, 0.0, 0.5), (listA, 0.0, 1.0), (listB, 0.0, 1.0)):
                        n_ = len(lst)
                        for i_, op_ in enumerate(lst):
                            keyed.append((off_ + sc_ * (i_ + 0.5) / n_, len(keyed), op_))
                    keyed.sort(key=lambda t_: (t_[0], t_[1]))
                    for _, _, op_ in keyed:
                        P.add(*op_)
                    for op_ in listO:
                        P.add(*op_)
                else:
                    for lst in (listF, listO, listA, listB):
                        for op_ in lst:
                            P.add(*op_)
                listC = []
                P.capture = listC
                cur_pool[0] = (0, 1, 2, 3)
                tyC, nchC, csC = 1, 4, 32
                TRIc, SAMEc, MSLNc = 0 + tyC, 2 + tyC, 4 + tyC
                act(vcT[:], cacc[:, 4:6, :], AF.Silu, ["cacc4", "cacc5"], ["vcT"])
                csl = cacc[:, 0:4, :]
                CQ = ["cacc0", "cacc1", "cacc2", "cacc3"]
                act(csl, csl, AF.Silu, CQ, CQ)
                act(sqc[:], csl, AF.Square, CQ, ["sqc"])
                bkt, bkr = bank()
                for g in range(4):
                    mm(bkt[:, 128 * g:128 * g + 128], cstb[:, BLK1, :], sqc[:, g, :], ["cstb", "sqc"], [bkr])
                act(lnt[:], bkt[:], AF.Ln, [bkr], ["lnt"], bias=EPS)
                act(lnt[:], lnt[:], AF.Exp, ["lnt"], ["lnt"], scale=-0.5)
                rn = lnt[:].rearrange("p (g t) -> p g t", g=4)
                stt(qnT[:], csl[:, 0:2, :], 0.125, rn[:, 0:2, :], ALU.mult, ALU.mult, ["cacc0", "cacc1", "lnt"], ["qnT"])
                tt(knT[:], csl[:, 2:4, :], rn[:, 2:4, :], ALU.mult, ["cacc2", "cacc3", "lnt"], ["knT"])
                btr, btrr = bank()
                for g in range(2):
                    mm(btr[:, 128 * g:128 * g + 128], vcT[:, g, :], cstb[:, IDENT, :], ["vcT", "cstb"], [btrr])
                    mm(btr[:, 256 + 128 * g:256 + 128 * g + 128], knT[:, g, :], cstb[:, IDENT, :], ["knT", "cstb"], [btrr])
                tt(sm[:, 0:4], ab[:, 0:4], hrow[:, l, 4:8], ALU.add, ["ab", "hrow"], ["sm"])
                act(sm[:, 4:8], sm[:, 0:4], AF.Exp, ["sm"], ["sm"])
                act(sm[:, 8:12], sm[:, 4:8], AF.Ln, ["sm"], ["sm"], bias=1.0)
                tt(sm[:, 12:16], sm[:, 8:12], negA[:], ALU.mult, ["sm", "negA"], ["sm"])
                act(sm[:, 16:20], ab[:, 4:8], AF.Exp, ["ab"], ["sm"], scale=-1.0)
                ts(sm[:, 16:20], sm[:, 16:20], 1.0, ALU.add, ["sm"], ["sm"])
                recip(sm[:, 20:24], sm[:, 16:20], ["sm"], ["sm"])
                tt(gm[:, 0:4 * nchC].rearrange("p (c h) -> p c h", c=nchC), bc(sm[:, 12:16].unsqueeze(1), [128, nchC, 4]),
                   bc(cm[:, tyC, 0:nchC].unsqueeze(2), [128, nchC, 4]), ALU.mult, ["sm", "cm"], ["gm"])
                bsm, bsmr = bank()
                mm(bsm[:, 0:4], cstf[:, TRIc, :], sm[:, 12:16], ["cstf", "sm"], [bsmr])
                mm(bsm[:, 4:8], cstf[:, SAMEc, :], sm[:, 12:16], ["cstf", "sm"], [bsmr])
                mm(bsm[0:64, 8:8 + 4 * nchC], cstf[:, ONES, 0:64], gm[:, 0:4 * nchC], ["cstf", "gm"], [bsmr])
                bgr, bgrr = bank()
                for h in range(4):
                    mm(bgr[:, 128 * h:128 * h + 128], bc(sm[:, 12 + h:13 + h], [128, 128]), cstf[:, TRIc, :], ["cstf", "sm"], [bgrr])
                cpy(sm[:, 24:32], bsm[:, 0:8], [bsmr], ["sm"])
                cpy(glr[:, 0:4 * nchC], bsm[0:64, 8:8 + 4 * nchC], [bsmr], ["glr"])
                act(eglr[:, 0:4 * nchC], glr[:, 0:4 * nchC], AF.Exp, ["glr"], ["eglr"])
                act(sm[:, 32:36], sm[:, 24:28], AF.Exp, ["sm"], ["sm"])
                tt(sm[:, 44:48], sm[:, 28:32], sm[:, 24:28], ALU.subtract, ["sm"], ["sm"])
                act(sm[:, 36:40], sm[:, 44:48], AF.Exp, ["sm"], ["sm"])
                tt(sm[:, 40:44], sm[:, 20:24], sm[:, 32:36], ALU.mult, ["sm"], ["sm"])
                act(Em[:].rearrange("p h t -> p (h t)"), bgr[:], AF.Exp, [bgrr], ["Em"])
                for h in range(4):
                    p0, g = 64 * (h % 2), h // 2
                    tt(qgT[:, h, :], qnT[p0:p0 + 64, g, :], Em[p0:p0 + 64, h, :], ALU.mult, ["qnT", "Em"], ["qgT"])
                tt(Y32[:, :, 0:64], btr[:, 0:256].rearrange("p (h d) -> p h d", h=4), bc(sm[:, 20:24].unsqueeze(2), [128, 4, 64]),
                   ALU.mult, [btrr, "sm"], ["Y32"])
                tt(Y32[:, :, 64:128], btr[:, 256:512].rearrange("p (h d) -> p h d", h=4), bc(sm[:, 40:44].unsqueeze(2), [128, 4, 64]),
                   ALU.mult, [btrr, "sm"], ["Y32"])
                act(Yb[0][:], Y32[:], AF.Copy, ["Y32"], ["Yb0"])
                tt(ktil[:], btr[:, 256:512].rearrange("p (h d) -> p h d", h=4), bc(sm[:, 36:40].unsqueeze(2), [128, 4, 64]),
                   ALU.mult, [btrr, "sm"], ["ktil"])
                gram = [bank(), bank()]
                for h in range(4):
                    p0, g = 64 * (h % 2), h // 2
                    gb, gbr = gram[h % 2]
                    mm(gb[:, 128 * g:128 * g + 128], knT[p0:p0 + 64, g, :], knT[p0:p0 + 64, g, :], ["knT"], [gbr])
                    mm(gb[:, 256 + 128 * g:256 + 128 * g + 128], knT[p0:p0 + 64, g, :], qnT[p0:p0 + 64, g, :], ["knT", "qnT"], [gbr])
                for h in range(4):
                    ts(Ee[:, h, :], bgr[:, 128 * h:128 * h + 128], sm[:, 24 + h:25 + h], ALU.subtract, [bgrr, "sm"], ["Ee"])
                stt(Ee[:], Ee[:], -1.0, Ee[:], ALU.mult, ALU.max, ["Ee"], ["Ee"])
                act(Ee[:], Ee[:], AF.Exp, ["Ee"], ["Ee"], scale=-1.0)
                tt(Em[:], Ee[:], bc(cstf[:, MSLNc, :].unsqueeze(1), [128, 4, 128]), ALU.mult, ["Ee", "cstf", "qgT"], ["Em"])
                for h in range(4):
                    stt(Lp[0][:, h, :], gram[h % 2][0][:, 128 * (h // 2):128 * (h // 2) + 128], sm[:, 20 + h:21 + h], Em[:, h, :],
                        ALU.mult, ALU.mult, [gram[h % 2][1], "sm", "Em"], ["Lp0"])
                tt(Em[:], Ee[:], bc(cstf[:, TRIc, :].unsqueeze(1), [128, 4, 128]), ALU.mult, ["Ee", "cstf"], ["Em"])
                for par in range(2):
                    tt(QKm[:, par::2, :], gram[par][0][:, 256:512].rearrange("p (h t) -> p h t", h=2), Em[:, par::2, :], ALU.mult,
                       [gram[par][1], "Em"], ["QKm"])
                bb_, bbr = bank()
                for h in range(4):
                    mm(bb_[:, 128 * h:128 * h + 128], Lp[0][:, h, :], cstb[:, IDENT, :], ["Lp0", "cstb"], [bbr])
                act(Bpw[0][:].rearrange("p h t -> p (h t)"), bb_[:], AF.Copy, [bbr], ["Bpw0"])
                NLV = 5
                for k in range(NLV):
                    ci, ni = k % 2, (k + 1) % 2
                    by, byr = bank()
                    for h in range(4):
                        mm(by[:, 128 * h:128 * h + 128], Bpw[ci][:, h, :], Yb[ci][:, h, :], ["Bpw%d" % ci, "Yb%d" % ci], [byr])
                    tt(Y32[:].rearrange("p h t -> p (h t)"), Y32[:].rearrange("p h t -> p (h t)"), by[:], ALU.add, ["Y32", byr], ["Y32"])
                    cpy(Yb[ni][:], Y32[:], ["Y32"], ["Yb%d" % ni])
                    if k < NLV - 1:
                        b2, b2r = bank()
                        for h in range(4):
                            mm(b2[:, 128 * h:128 * h + 128], Lp[ci][:, h, :], Bpw[ci][:, h, :], ["Lp%d" % ci, "Bpw%d" % ci], [b2r])
                        act(Bpw[ni][:].rearrange("p h t -> p (h t)"), b2[:], AF.Copy, [b2r], ["Bpw%d" % ni])
                        if k < NLV - 2:
                            l2, l2r = bank()
                            for h in range(4):
                                mm(l2[:, 128 * h:128 * h + 128], Bpw[ci][:, h, :], Lp[ci][:, h, :], ["Lp%d" % ci, "Bpw%d" % ci], [l2r])
                            act(Lp[ni][:].rearrange("p h t -> p (h t)"), l2[:], AF.Copy, [l2r], ["Lp%d" % ni])
                yfin = NLV % 2
                Yf, Yfr = Yb[yfin], "Yb%d" % yfin
                bw, bwr = bank()
                for h in range(4):
                    mm(bw[0:64, 128 * h:128 * h + 128], Yf[:, h, 64:128], cstb[:, IDENT, :], [Yfr, "cstb"], [bwr])
                act(wT[:].rearrange("p h t -> p (h t)"), bw[0:64, :], AF.Copy, [bwr], ["wT"])
                bo, bor = bank(hold=True)
                for c in range(nchC):
                    si = (c % 2) if smp else 0
                    sr, sbr = "S32_%d" % si, "Sbf_%d" % si
                    if smp:
                        dma(S32[si][:], sdel_d[l, c], (), [sr])
                        cpy(Sbf[si][:], S32[si][:], [sr], [sbr])
                    bws, bwsr = bank()
                    for h in range(4):
                        mm(bws[:, 64 * h:64 * h + 64], wT[:, h, :], Sbf[si][:, h, :], ["wT", sbr], [bwsr])
                    tt(tmpu[:].rearrange("p (h d) -> p h d", h=4), Y32[:, :, 0:64], bws[:, 0:256].rearrange("p (h d) -> p h d", h=4),
                       ALU.subtract, ["Y32", bwsr], ["tmpu"])
                    ts(unc[:, c, :], tmpu[:], cm[:, tyC, c:c + 1], ALU.mult, ["tmpu", "cm"], ["unc%d" % c])
                    for h in range(4):
                        p0, g = 64 * (h % 2), h // 2
                        mm(bo[p0:p0 + 64, 128 * g + c * csC:128 * g + (c + 1) * csC], Sbf[si][:, h, :], qgT[:, h, c * csC:(c + 1) * csC],
                           [sbr, "qgT"], [bor], start=True, stop=False)
                    bds, bdsr = bank()
                    for h in range(4):
                        mm(bds[0:64, 64 * h:64 * h + 64], ktil[:, h, :], unc[:, c, 64 * h:64 * h + 64], ["ktil", "unc%d" % c], [bdsr])
                    tt(Sdec[:], S32[si][:], bc(eglr[:, 4 * c:4 * c + 4].unsqueeze(2), [64, 4, 64]), ALU.mult, [sr, "eglr"], ["Sdec"])
                    tt(Sbf[si][:], Sdec[:], bds[0:64, 0:256].rearrange("p (h d) -> p h d", h=4), ALU.add, ["Sdec", bdsr], [sbr])
                    tt(S32[si][:], Sdec[:], bds[0:64, 0:256].rearrange("p (h d) -> p h d", h=4), ALU.add, ["Sdec", bdsr], [sr])
                    for h in range(4):
                        p0, g = 64 * (h % 2), h // 2
                        mm(bo[p0:p0 + 64, 128 * g + c * csC:128 * g + (c + 1) * csC], unc[:, c, 64 * h:64 * h + 64],
                           QKm[:, h, c * csC:(c + 1) * csC], ["unc%d" % c, "QKm"], [bor], start=False, stop=True)
                    if smp:
                        dma(del_o[l, 1 + c], S32[si][:], [sr], ())
                if b == 15:
                    dma(del_o[l, 0], S32[0][:], ["S32_0"], ())

                def out_norm(bo_, bor_, gcol, gate, gres, kbase, sq_=None, sqr="sqo", tn_=None, tnr="tno", ln_=None, lnr="lnt"):
                    sq_ = sqo if sq_ is None else sq_
                    tn_ = tno if tn_ is None else tn_
                    ln_ = lnt[:, 0:256] if ln_ is None else ln_[:]
                    act(sq_[:], bo_[:, 0:256], AF.Square, [bor_], [sqr])
                    bn, bnr = bank()
                    for g in range(2):
                        mm(bn[:, 128 * g:128 * g + 128], cstb[:, BLK64, :], sq_[:, 128 * g:128 * g + 128], ["cstb", sqr], [bnr])
                    act(ln_, bn[:, 0:256], AF.Ln, [bnr], [lnr], bias=EPS)
                    act(ln_, ln_, AF.Exp, [lnr], [lnr], scale=-0.5)
                    stt(tn_[:], bo_[:, 0:256], pcol[:, l, gcol:gcol + 1], ln_, ALU.mult, ALU.mult, [bor_, "pcol", lnr], [tnr])
                    tt(mixT[:, kbase:kbase + 2, :], tn_[:].rearrange("p (g t) -> p g t", g=2), gate[:], ALU.mult, [tnr, gres], ["mixT%d" % kbase, "mixT%d" % (kbase + 1)])

                out_norm(bo, bor, 2, zs, "zs", 4)
                held.discard(bor)
                listD = []
                P.capture = listD
                cur_pool[0] = (4, 5)

                ck(100 * b + 6)
                bd, bdr = bank()
                mm(bd[:, 0:128], wgk[:, l, :], gkl[:, :], ["wgk", "gkl"], [bdr])
                act(e1[:], bd[:, 0:128], AF.Exp, [bdr, "pcol"], ["e1"], bias=pcol[:, l, 5:6], scale=-1.0)
                act(spd[:], e1[:], AF.Ln, ["e1"], ["spd"], bias=1.0)
                scan(Gs[:], cstf[:, RST, :], spd[:], ["cstf", "spd"], ["Gs"])
                Gv = Gs[:].rearrange("p (c t) -> p c t", c=nch)
                tt(Dm[:].rearrange("p (c t) -> p c t", c=nch), Gv, bc(Gv[:, :, cs - 1:cs], [128, nch, cs]), ALU.subtract, ["Gs"], ["Dm"])
                act(e1[:], Dm[:], AF.Exp, ["Dm"], ["e1"], scale=-1.0 / 16)
                act(e2[:], Dm[:], AF.Exp, ["Dm"], ["e2"], scale=1.0 / 16)
                act(egl[:, 0:nch], Gv[:, :, cs - 1], AF.Exp, ["Gs"], ["egl"], scale=-1.0 / 16)
                tt(qt[:], qd[:], e1[:], ALU.mult, ["qd", "e1"], ["qt"])
                tt(kt[:], kd[:], e2[:], ALU.mult, ["kd", "e2"], ["kt"])
                for h in range(4):
                    ts(qm[:, h, :], qt[:], cm[:, ty, 8 + h:9 + h], ALU.mult, ["qt", "cm"], ["qm"])
                bt2, bt2r = bank()
                mm(bt2[:, 0:128], kt[:], cstb[:, IDENT, :], ["kt", "cstb"], [bt2r])
                for c in range(nch):
                    ts(ktc[:, c, :], bt2[:, 0:128], cm[:, ty, c:c + 1], ALU.mult, [bt2r, "cm"], ["ktc"])
                bat, batr = bank()
                for h in range(4):
                    mm(bat[:, 128 * h:128 * h + 128], kt[:], qm[:, h, :], ["kt", "qm"], [batr])
                tt(attm[:], bat[:].rearrange("p (h t) -> p h t", h=4), bc(cstf[:, TRI, :].unsqueeze(1), [128, 4, 128]), ALU.mult,
                   [batr, "cstf"], ["attm"])
                bod, bodr = bank(hold=True)
                for c in range(nch):
                    si = (c % 2) if smp else 0
                    gr = "G32_%d" % si
                    if smp:
                        mset(G32[si][:], 0.0, [gr])
                        for h in range(4):
                            dma(G32[si][32 * h:32 * h + 32, 64 * h:64 * h + 64], sgla_d[l, c, h], (), [gr])
                    bx, bxr = bank()
                    mm(bx[:, 0:256], ktc[:, c, :], vD[:], ["ktc", "vD"], [bxr])
                    ts(Gp32[:], G32[si][:], egl[:, c:c + 1], ALU.mult, [gr, "egl"], ["Gp32"])
                    cpy(Gpbf[:], Gp32[:], ["Gp32"], ["Gpbf"])
                    for h in range(4):
                        p0, g = 64 * (h % 2), h // 2
                        osl = bod[p0:p0 + 64, 128 * g + c * cs:128 * g + (c + 1) * cs]
                        mm(osl, Gpbf[:, 64 * h:64 * h + 64], qm[:, h, c * cs:(c + 1) * cs], ["Gpbf", "qm"], [bodr], start=True, stop=False)
                        mm(osl, vD[:, 64 * h:64 * h + 64], attm[:, h, c * cs:(c + 1) * cs], ["vD", "attm"], [bodr], start=False, stop=True)
                    tt(G32[si][:], Gp32[:], bx[:, 0:256], ALU.add, ["Gp32", bxr], [gr])
                    if smp:
                        for h in range(4):
                            dma(gla_o[l, 1 + c, h], G32[si][32 * h:32 * h + 32, 64 * h:64 * h + 64], [gr], ())
                if b == 15:
                    for h in range(4):
                        dma(gla_o[l, 0, h], G32[0][32 * h:32 * h + 32, 64 * h:64 * h + 64], ["G32_0"], ())
                out_norm(bod, bodr, 3, gs, "gs", 6, sqoD, "sqoD", tnoD, "tnoD", lntD, "lntD")
                held.discard(bodr)
                P.capture = None
                cur_pool[0] = None
                listP = []
                if 1 <= b + 1 <= 15:
                    P.capture = listP
                    cur_pool[0] = (6, 7)
                    front_early(b + 1)
                    P.capture = None
                    cur_pool[0] = None
                elif b == 16:
                    P.capture = listP
                    cur_pool[0] = (6, 7)
                    for bb2 in range(16):
                        h2_block(bb2)
                    P.capture = None
                    cur_pool[0] = None
                P.capture = listP
                cur_pool[0] = (6, 7)
                w_out_part(0, 4)
                P.capture = None
                cur_pool[0] = None
                keyed = []
                for lst, off_, sc_ in ((listC, 0.0, 1.0), (listD, 0.0, 1.0), (listP, 0.0, 1.0)):
                    n_ = len(lst)
                    for i_, op_ in enumerate(lst):
                        keyed.append((off_ + sc_ * (i_ + 0.5) / n_, len(keyed), op_))
                keyed.sort(key=lambda t_: (t_[0], t_[1]))
                for _, _, op_ in keyed:
                    P.add(*op_)

                if DBG and l == 0 and b == DBGB:
                    dl = [("vD", vD[:], 256), ("qt", qt[:], 128), ("kt", kt[:], 128), ("qm", qm[:].rearrange("p h t -> p (h t)"), 512),
                          ("ktc", ktc[:, 0:2, :].rearrange("p h t -> p (h t)"), 256), ("attm", attm[:].rearrange("p h t -> p (h t)"), 512),
                          ("Gs", Gs[:], 128), ("e1", e1[:], 128), ("e2", e2[:], 128), ("egl", egl[:, 0:2], 2), ("G32", G32[0][:], 256),
                          ("Gp32", Gp32[:], 256), ("Gpbf", Gpbf[:], 256), ("qd", qd[:], 128), ("kd", kd[:], 128), ("gs", gs[:].rearrange("p h t -> p (h t)"), 256)]
                    dflat = dbgt[:].rearrange("p a b -> p (a b)")
                    for di_, (nm_, ap_, n_) in enumerate(dl):
                        cpy(dflat[:, 0:n_], ap_, [nm_], ["dbgt"])
                        dma(dbg2_o[di_, :, 0:n_], dflat[:, 0:n_], ["dbgt"], ())
                if DBG and l == 0 and b in (DBGB, 16):
                    cpy(dbgt[:], mixT[:], ["mixT%d" % k_ for k_ in range(8)], ["dbgt"])
                    dma(dbg_o[0 if b == DBGB else 1], dbgt[:], ["dbgt"], ())
                ck(100 * b + 7)
                w_out_part(4, 8)

            ck(5000)
            P.phase_switch()
            if l + 1 < NL:
                load_w_out(l + 1)
            h2_block(16)
            ai = 0
            for j in range(8):
                wb = j % 2
                for fm in range(4):
                    dma(wup[wb][:, :, fm * 128:(fm + 1) * 128],
                        w_up_d[l, :, j * 512 + fm * 128:j * 512 + (fm + 1) * 128].rearrange("(kc p) f -> p kc f", p=128),
                        (), ["wup%d_%d" % (wb, fm)], eng="pool")
                for fc in range(4):
                    dma(wdn[wb][:, fc, :], w_dn_d[l, j * 512 + fc * 128:j * 512 + (fc + 1) * 128, :], (), ["wdn%d_%d" % (wb, fc)], eng="pool")
                for (t0, n) in TIL:
                    cols = slice(t0, t0 + n)
                    a_ = ai % 2
                    ai += 1
                    for fm in range(4):
                        bkt, bkr = bank()
                        for kc in range(8):
                            mm(bkt[:, :n], wup[wb][:, kc, fm * 128:(fm + 1) * 128], h2T[:, kc, cols], ["wup%d_%d" % (wb, fm), "w_in"], [bkr],
                               start=(kc == 0), stop=(kc == 7))
                        r_ = fm % 2
                        act(relu_t[r_][:, :n], bkt[:, :n], AF.Relu, [bkr], ["relu%d" % r_])
                        tt(actb[a_][:, fm, :n], relu_t[r_][:, :n], relu_t[r_][:, :n], ALU.mult, ["relu%d" % r_], ["actb%d" % a_])
                    for m in range(8):
                        bkt, bkr = bank()
                        for fc in range(4):
                            mm(bkt[:, :n], wdn[wb][:, fc, m * 128:(m + 1) * 128], actb[a_][:, fc, :n], ["wdn%d_%d" % (wb, fc), "actb%d" % a_], [bkr],
                               start=(fc == 0), stop=(fc == 3))
                        tt(xT[:, m, cols], xT[:, m, cols], bkt[:, :n], ALU.add, ["x%d" % m, bkr], ["x%d" % m])
            P.phase_switch()
            if l + 1 < NL:
                load_w_in(l + 1)

        ck(6000)
        for bb2 in range(NBLK):
            cols = slice(bb2 * 128, bb2 * 128 + 128)
            rms_stats(cols, 128, XR)
            for kc in range(8):
                yo = yout[kc % 2]
                stt(yo[:], xT[:, kc, cols], gvec[:, 64 + kc:65 + kc], rstd[:, :128], ALU.mult, ALU.mult,
                    [XR[kc], "rstd", "gvec"], ["yout%d" % (kc % 2)])
                dma(yT_o[kc * 128:(kc + 1) * 128, cols], yo[:], ["yout%d" % (kc % 2)], ())

    except _Stop:
        pass

    P.emit(nc, stack)
    stack.close()
    return nc


def _consts():
    cst = np.zeros((128, 13, 128), np.float32)
    cst[:, 12] = 1.0
    i = np.arange(128)
    cst[:, 0] = np.eye(128)
    cst[:, 1] = 1.0 / 1024
    blk = (i[:, None] // 64 == i[None, :] // 64).astype(np.float32)
    cst[:, 2] = blk
    cst[:, 3] = blk / 64
    for ty, cs in ((0, 64), (1, 32)):
        same = (i[:, None] // cs == i[None, :] // cs)
        cst[:, 4 + ty] = (same & (i[:, None] <= i[None, :]))
        cst[:, 6 + ty] = same
        cst[:, 8 + ty] = -(same & (i[:, None] > i[None, :])).astype(np.float32)
        cst[:, 10 + ty] = np.broadcast_to((i % cs != 0).astype(np.float32)[None, :], (128, 128))
    cm = np.zeros((128, 2, 16), np.float32)
    for ty, cs in ((0, 64), (1, 32)):
        for c in range(128 // cs):
            cm[:, ty, c] = (i // cs == c)
            cm[:, ty, 4 + c] = -cm[:, ty, c]
        for h in range(4):
            cm[:, ty, 8 + h] = (i // 32 == h)
    invc = np.zeros((128, 2, 128), np.float32)
    t = np.arange(128)
    for g, win in enumerate((2, 4, 8, 16)):
        kc, p0 = g // 2, 64 * (g % 2)
        invc[p0:p0 + 64, kc, :] = 1.0 / np.minimum(win, t + 1)[None, :]
    return cst, cm, invc


def _colT(v):
    return v.reshape(v.shape[:-1] + (8, 128))


def kernel(x_prompt, x_sample, cache_pool, cache_attn_k, cache_attn_v, state_conv, state_delta,
           state_gla, attn_norm_g, w_in, pool_w, pool_scale, rel_bias, conv_w, a_log, dt_bias,
           delta_norm_g, gla_w_gk, gla_b_gk, gla_norm_g, w_out, mlp_norm_g, w_up, w_down,
           final_norm_g, _NL=DEPTH, _DBG=False, _CORES=NCORES, _STOP=None):
    f = np.float32
    cst, cm, invc = _consts()
    gvec = np.zeros((128, 72), f)
    gvec[:, 0:32] = attn_norm_g.reshape(4, 8, 128).transpose(2, 0, 1).reshape(128, 32)
    gvec[:, 32:64] = mlp_norm_g.reshape(4, 8, 128).transpose(2, 0, 1).reshape(128, 32)
    gvec[:, 64:72] = final_norm_g.reshape(8, 128).T
    pcol = np.zeros((128, 4, 8), f)
    pcol[:, :, 0:2] = pool_scale.reshape(4, 2, 128).transpose(2, 0, 1)
    pcol[:, :, 2] = np.concatenate([delta_norm_g, delta_norm_g], 1).T
    pcol[:, :, 3] = np.concatenate([gla_norm_g, gla_norm_g], 1).T
    pcol[:, :, 4] = gla_b_gk.T
    convw = np.ascontiguousarray(conv_w.reshape(4, 4, 6, 128).transpose(3, 0, 2, 1)).astype(f)
    hrow = np.zeros((128, 4, 8), f)
    hrow[:, :, 0:4] = a_log[None]
    hrow[:, :, 4:8] = dt_bias[None]
    poolw = np.zeros((4, 2, 128, 128), f)
    for g in range(4):
        kc, p0 = g // 2, 64 * (g % 2)
        poolw[:, kc, p0:p0 + 64, p0:p0 + 64] = pool_w[:, g]
    wgk = np.ascontiguousarray(gla_w_gk.transpose(1, 0, 2)).astype(f)
    NEG = f(-100.0)
    k = np.arange(128)[:, None]
    q = np.arange(128)[None, :]
    bp = np.zeros((4, 128, 5, 4, 128), f)
    for r in range(5):
        idx = np.clip((r - 4) * 128 + k - q, -256, 256) + 256
        tab = rel_bias[:, :, idx]
        tab = tab.transpose(0, 2, 1, 3).copy()
        if r == 0:
            tab[:, 0:64, :, 64:128] = NEG
        if r == 4:
            tab[:, 64:128, :, 0:64] = NEG
        bp[:, :, r] = tab
    bp = bp.reshape(4, 128, 5, 512)
    q32 = np.arange(32)[None, :]
    bsc = np.zeros((4, 128, 4, 4, 32), f)
    for r in range(4):
        idx = np.clip(r * 128 + k - 512 - q32, -256, 256) + 256
        bsc[:, :, r] = rel_bias[:, :, idx].transpose(0, 2, 1, 3)
    bsc = bsc.reshape(4, 128, 4, 128)
    bsn = np.full((4, 128, 4, 4, 32), NEG, f)
    kk = np.arange(32)[:, None]
    idx = np.clip(kk - q32, -256, 256) + 256
    tabn = rel_bias[:, :, idx].transpose(0, 2, 1, 3)
    for s in range(4):
        bsn[:, 32 * s:32 * s + 32, s] = tabn
    bsn = bsn.reshape(4, 128, 4, 128)

    shared = dict(w_in=np.ascontiguousarray(w_in, f), w_out=np.ascontiguousarray(w_out, f),
                  w_up=np.ascontiguousarray(w_up, f), w_down=np.ascontiguousarray(w_down, f),
                  gvec=gvec, pcol=pcol, convw=convw, hrow=hrow, poolw=poolw, wgk=wgk, bp=bp, bsc=bsc, bsn=bsn,
                  cst=cst, cm=cm, invc=invc)
    in_maps = []
    for c in range(NCORES):
        sl = slice(4 * c, 4 * c + 4)
        xs = np.concatenate([x_prompt[c], x_sample[sl].reshape(128, D)], 0)
        m = dict(shared)
        m["xT"] = np.ascontiguousarray(xs.T, f)
        m["cpool"] = np.ascontiguousarray(cache_pool[:, sl].transpose(0, 1, 3, 2), f)
        m["ck"] = np.ascontiguousarray(cache_attn_k[:, sl].transpose(0, 1, 2, 4, 3).reshape(4, 4, 256, 512), f)
        m["cv"] = np.ascontiguousarray(cache_attn_v[:, sl], f)
        m["sconv"] = np.ascontiguousarray(state_conv[:, sl].transpose(0, 1, 3, 2), f)
        m["sdel"] = np.ascontiguousarray(state_delta[:, sl].transpose(0, 1, 3, 2, 4), f)
        m["sgla"] = np.ascontiguousarray(state_gla[:, sl], f)
        in_maps.append(m)

    nc = build(_NL, _DBG, _STOP)
    res = run_bass_kernel_spmd(nc, in_maps[:_CORES], core_ids=list(range(_CORES))).results
    res = list(res) + [res[0]] * (NCORES - _CORES)
    global _last_res
    _last_res = res

    y_prompt = np.zeros((8, 2048, D), f)
    y_sample = np.zeros((32, 32, D), f)
    pool_p = np.zeros((4, 8, 15, 256), f); pool_s = np.zeros((4, 32, 15, 256), f)
    k_p = np.zeros((4, 8, 4, 512, 64), f); v_p = np.zeros((4, 8, 4, 512, 64), f)
    k_s = np.zeros((4, 32, 4, 32, 64), f); v_s = np.zeros((4, 32, 4, 32, 64), f)
    conv_p = np.zeros((4, 8, 3, 768), f); conv_s = np.zeros((4, 32, 3, 768), f)
    delta_p = np.zeros((4, 8, 4, 64, 64), f); delta_s = np.zeros((4, 32, 4, 64, 64), f)
    gla_p = np.zeros((4, 8, 4, 32, 64), f); gla_s = np.zeros((4, 32, 4, 32, 64), f)
    for c in range(NCORES):
        r = res[c]
        sl = slice(4 * c, 4 * c + 4)
        yt = r["yT"].T
        y_prompt[c] = yt[:2048]
        y_sample[sl] = yt[2048:].reshape(4, 32, D)
        po = r["pool_o"].transpose(0, 1, 3, 2)
        pool_p[:, c] = po[:, 0]; pool_s[:, sl] = po[:, 1:]
        ko = r["k_o"]
        k_p[:, c] = ko[:, :, :512].reshape(4, 4, 64, 512).transpose(0, 1, 3, 2)
        k_s[:, sl] = ko[:, :, 512:].reshape(4, 4, 64, 4, 32).transpose(0, 3, 1, 4, 2)
        vo = r["v_o"]
        v_p[:, c] = vo[:, :512].reshape(4, 512, 4, 64).transpose(0, 2, 1, 3)
        v_s[:, sl] = vo[:, 512:].reshape(4, 4, 32, 4, 64).transpose(0, 1, 3, 2, 4)
        co = r["conv_o"].transpose(0, 1, 3, 2)
        conv_p[:, c] = co[:, 0]; conv_s[:, sl] = co[:, 1:]
        do = r["del_o"].transpose(0, 1, 3, 2, 4)
        delta_p[:, c] = do[:, 0]; delta_s[:, sl] = do[:, 1:]
        go = r["gla_o"]
        gla_p[:, c] = go[:, 0]; gla_s[:, sl] = go[:, 1:]
    return (y_prompt, y_sample, pool_p, k_p, v_p, conv_p, delta_p, gla_p,
            pool_s, k_s, v_s, conv_s, delta_s, gla_s)
```

```python
import contextlib
import numpy as np
import concourse.bass as bass
import concourse.mybir as mybir
from concourse.bass_utils import run_bass_kernel_spmd

F32 = mybir.dt.float32
BF16 = mybir.dt.bfloat16
AF = mybir.ActivationFunctionType
ALU = mybir.AluOpType

NCORES = 8
DEPTH = 4
D = 1024
NT = 2176
NBLK = 17
INC = 2840
EPS = 1e-6
ENGS = ["pe", "act", "dve", "pool", "sp"]
RING = 8
OVW_ = 15400
DBGB = 12


class Op:
    __slots__ = ("eng", "fn", "deps", "sig", "sigval", "dma", "k")


class Prog:
    def __init__(self):
        self.ops = {e: [] for e in ENGS}
        self.last_w = {}
        self.readers = {}
        self.ndma = {e: 0 for e in ENGS}
        self.alias = {}
        self.ov_res = set()
        self.ov_recent = {e: [] for e in ENGS}
        self.fence = []
        self.last_acc = {}
        self.capture = None

    def phase_switch(self):
        f = []
        for e in ENGS:
            f.extend(self.ov_recent[e])
        self.fence = f
        self.ov_recent = {e: [] for e in ENGS}

    def add(self, eng, fn, rd=(), wr=(), dma=False):
        if self.capture is not None:
            self.capture.append((eng, fn, tuple(rd), tuple(wr), dma))
            return None
        op = Op()
        op.eng, op.fn, op.dma, op.sig, op.sigval, op.k = eng, fn, dma, False, 0, 0
        wr = list(wr)
        for r in list(wr):
            wr.extend(self.alias.get(r, ()))
        deps = set()
        for r in rd:
            w = self.last_w.get(r)
            if w is not None:
                deps.add(w)
        for r in wr:
            w = self.last_w.get(r)
            if w is not None:
                deps.add(w)
            for x in self.readers.get(r, ()):
                deps.add(x)
        for r in list(rd) + list(wr):
            if r.startswith("pb"):
                la = self.last_acc.get(r)
                if la is not None and la.eng != eng:
                    deps.add(la)
                self.last_acc[r] = op
        touches_ov = any(r in self.ov_res for r in rd) or any(r in self.ov_res for r in wr)
        if touches_ov:
            deps.update(self.fence)
        for r in rd:
            self.readers.setdefault(r, []).append(op)
        for r in wr:
            self.last_w[r] = op
            self.readers[r] = []
        deps.discard(op)
        op.deps = [d for d in deps if d.dma or d.eng != eng or eng != "pe"]
        if dma:
            op.k = self.ndma[eng]
            self.ndma[eng] += 1
        self.ops[eng].append(op)
        if touches_ov:
            lst = self.ov_recent[eng]
            lst.append(op)
            keep = RING + 1
            if len(lst) > keep:
                nd = [o for o in lst if not o.dma][-1:]
                dd = [o for o in lst if o.dma][-RING:]
                self.ov_recent[eng] = dd + nd
        return op

    def emit(self, nc, stack):
        for e in ENGS:
            for op in self.ops[e]:
                for d in op.deps:
                    d.sig = True
        esem = {e: stack.enter_context(nc.semaphore("es_" + e)) for e in ENGS}
        dsem = {e: [stack.enter_context(nc.semaphore("ds_%s_%d" % (e, i))) for i in range(RING)]
                for e in ENGS if self.ndma[e] > 0}
        for e in ENGS:
            c = 0
            for op in self.ops[e]:
                if op.sig and not op.dma:
                    c += 1
                    op.sigval = c
        block = stack.enter_context(nc.Block())

        def run(e, eng):
            waited = {}

            def wait(key, sem, val):
                if waited.get(key, 0) < val:
                    eng.wait_ge(sem, val)
                    waited[key] = val

            for op in self.ops[e]:
                for d in op.deps:
                    if d.dma:
                        wait(("d", d.eng, d.k % RING), dsem[d.eng][d.k % RING], 16 * (d.k // RING + 1))
                    else:
                        wait(("e", d.eng), esem[d.eng], d.sigval)
                if op.dma:
                    if op.k >= RING:
                        wait(("d", e, op.k % RING), dsem[e][op.k % RING], 16 * (op.k // RING))
                    op.fn(eng).then_inc(dsem[e][op.k % RING], 16)
                else:
                    ins = op.fn(eng)
                    if op.sig:
                        ins.then_inc(esem[e], 1)
            n = self.ndma[e]
            for s in range(min(n, RING)):
                last = ((n - 1 - s) // RING) * RING + s
                wait(("d", e, s), dsem[e][s], 16 * (last // RING + 1))

        @block.tensor
        def _(t):
            run("pe", t)

        @block.scalar
        def _(t):
            run("act", t)

        @block.vector
        def _(t):
            run("dve", t)

        @block.gpsimd
        def _(t):
            run("pool", t)

        @block.sync
        def _(t):
            run("sp", t)


def bc(ap, shape):
    return ap.broadcast_to(list(shape))


class _Stop(Exception):
    pass


def build(NL=DEPTH, DBG=False, STOP=None):
    nc = bass.Bass("TRN2", target_bir_lowering=False)
    P = Prog()
    stack = contextlib.ExitStack()

    def din(name, shape):
        return nc.dram_tensor(name, list(shape), F32, kind="ExternalInput").ap()

    def dout(name, shape):
        return nc.dram_tensor(name, list(shape), F32, kind="ExternalOutput").ap()

    xT_d = din("xT", [D, NT])
    w_in_d = din("w_in", [DEPTH, D, INC])
    w_out_d = din("w_out", [DEPTH, D, D])
    w_up_d = din("w_up", [DEPTH, D, 4096])
    w_dn_d = din("w_down", [DEPTH, 4096, D])
    gvec_d = din("gvec", [128, 72])
    pcol_d = din("pcol", [128, DEPTH, 8])
    convw_d = din("convw", [128, DEPTH, 6, 4])
    hrow_d = din("hrow", [128, DEPTH, 8])
    poolw_d = din("poolw", [DEPTH, 2, 128, 128])
    wgk_d = din("wgk", [16, DEPTH, 128])
    bp_d = din("bp", [DEPTH, 128, 5, 512])
    bsc_d = din("bsc", [DEPTH, 128, 4, 128])
    bsn_d = din("bsn", [DEPTH, 128, 4, 128])
    cst_d = din("cst", [128, 13, 128])
    cm_d = din("cm", [128, 2, 16])
    invc_d = din("invc", [128, 2, 128])
    cpool_d = din("cpool", [DEPTH, 4, 256, 15])
    ck_d = din("ck", [DEPTH, 4, 256, 512])
    cv_d = din("cv", [DEPTH, 4, 4, 512, 64])
    sconv_d = din("sconv", [DEPTH, 4, 768, 3])
    sdel_d = din("sdel", [DEPTH, 4, 64, 4, 64])
    sgla_d = din("sgla", [DEPTH, 4, 4, 32, 64])

    yT_o = dout("yT", [D, NT])
    pool_o = dout("pool_o", [DEPTH, 5, 256, 15])
    k_o = dout("k_o", [DEPTH, 256, 640])
    v_o = dout("v_o", [DEPTH, 640, 256])
    conv_o = dout("conv_o", [DEPTH, 5, 768, 3])
    del_o = dout("del_o", [DEPTH, 5, 64, 4, 64])
    gla_o = dout("gla_o", [DEPTH, 5, 4, 32, 64])
    dbg_o = dout("dbg_o", [2, 128, 8, 128]) if DBG else None
    dbg2_o = dout("dbg2_o", [16, 128, 1024]) if DBG else None

    def sb(name, shape, dt=F32):
        return stack.enter_context(nc.sbuf_tensor("s_" + name, list(shape), dt))

    xT = sb("xT", [128, 8, NT])
    w_in = sb("w_in_sb", [128, 8, INC], BF16)
    w_out = sb("w_out_sb", [128, 8, D], BF16)
    h2T = w_in[:].rearrange("p k c -> p (k c)")[:, 0:8 * NT].rearrange("p (k t) -> p k t", k=8)
    gvec = sb("gvec", [128, 72])
    pcol = sb("pcol", [128, DEPTH, 8])
    convw = sb("convw", [128, DEPTH, 6, 4])
    hrow = sb("hrow", [128, DEPTH, 8])
    negA = sb("negA", [128, 4])
    poolw = sb("poolw", [128, 2, 128], BF16)
    wgk = sb("wgk", [16, DEPTH, 128], BF16)
    bp = sb("bp", [128, 5, 512], BF16)
    bsc = sb("bsc", [128, 4, 128], BF16)
    bsn = sb("bsn", [128, 4, 128], BF16)
    cstf = sb("cstf", [128, 9, 128])
    cstb = sb("cstb", [128, 4, 128], BF16)
    cm = sb("cm", [128, 2, 16])
    invc = sb("invc", [128, 2, 128])
    sq = [sb("sq%d" % i, [128, 128], BF16) for i in range(2)]
    lnt = sb("lnt", [128, 512])
    rstd = sb("rstd", [128, 128])

    OVW = OVW_
    OV = sb("ov", [128, OVW])
    ov_off = [0]
    ov_offs = {}
    P.alias = {}

    def ovview(off, words, shape, dt, parts):
        v = OV[0:parts, off:off + words]
        if dt != F32:
            v = v.bitcast(dt)
        if len(shape) == 3:
            v = v.rearrange("p (a b) -> p a b", a=shape[1])
        return v

    def ova(name, shape, dt=F32, parts=128):
        n = 1
        for d_ in shape[1:]:
            n *= d_
        words = n if dt == F32 else (n + 1) // 2
        off = ov_off[0]
        assert off + words <= OVW, ("overlay overflow", name, off, words)
        ov_off[0] = off + words
        ov_offs[name] = (off, words)
        P.ov_res.add(name)
        return ovview(off, words, shape, dt, parts)

    def ovat(name, off, shape, dt=F32, parts=128):
        n = 1
        for d_ in shape[1:]:
            n *= d_
        words = n if dt == F32 else (n + 1) // 2
        ov_offs[name] = (off, words)
        P.ov_res.add(name)
        return ovview(off, words, shape, dt, parts)

    def union(host, members):
        P.alias[host] = list(members)
        for m_ in members:
            P.alias[m_] = [host]

    hT = ova("hT", [128, 8, 128], BF16)
    mixT = ova("mixT", [128, 8, 128], BF16)
    uext = ova("uext", [128, 2, 192])
    pwa = ova("pwa", [128, 2, 192])
    pwb = ova("pwb", [128, 2, 192])
    sqo = ovat("sqo", ov_offs["pwa"][0], [128, 256], BF16)
    tno = ovat("tno", ov_offs["pwa"][0] + 128, [128, 256])
    union("pwa", ["sqo", "tno"])
    Sdec = ovat("Sdec", ov_offs["pwb"][0], [64, 4, 64], parts=64)
    union("pwb", ["Sdec"])
    pooled = ova("pooled", [128, 2, 128], BF16)
    qb = ova("qb", [128, 2, 128], BF16)
    kbT = ova("kbT", [128, 2, 640], BF16)
    for i in range(5):
        P.ov_res.add("kbT%d" % i)
    vb1 = ova("vb1", [128, 5, 384], BF16)
    tS = ova("tS", [128, 512])
    rc = tS
    QKm = ovat("QKm", ov_offs["tS"][0], [128, 4, 128], BF16)
    wT = ovat("wT", ov_offs["tS"][0] + 256, [64, 4, 128], BF16, parts=64)
    union("tS", ["QKm", "wT"])
    cext = ova("cext", [128, 6, 144])
    cacc = ova("cacc", [128, 6, 128])
    sqc = ova("sqc", [128, 4, 128], BF16)
    qnT = ova("qnT", [128, 2, 128], BF16)
    knT = ova("knT", [128, 2, 128], BF16)
    vcT = ova("vcT", [128, 2, 128], BF16)
    zs = ova("zs", [128, 2, 128], BF16)
    gs = ova("gs", [128, 2, 128], BF16)
    ab = ova("ab", [128, 8])
    sm = ova("sm", [128, 64])
    gm = ova("gm", [128, 16])
    glr = ova("glr", [64, 16], parts=64)
    eglr = ova("eglr", [64, 16], parts=64)
    qgT = ova("qgT", [64, 4, 128], BF16, parts=64)
    Ee = ova("Ee", [128, 4, 128])
    qm = ova("qm", [128, 4, 128], BF16)
    ktc = ova("ktc", [128, 4, 128], BF16)
    Em = ova("Em", [128, 4, 128])
    sqoD = ovat("sqoD", ov_offs["Ee"][0], [128, 256], BF16)
    tnoD = ovat("tnoD", ov_offs["Ee"][0] + 128, [128, 256])
    lntD = ovat("lntD", ov_offs["Em"][0], [128, 256])
    union("Ee", ["sqoD", "tnoD"])
    union("Em", ["lntD"])
    attm = ova("attm", [128, 4, 128], BF16)
    Gp32 = ova("Gp32", [128, 256])
    Lp = [ova("Lp%d" % i, [128, 4, 128], BF16) for i in range(2)]
    Bpw = [ova("Bpw%d" % i, [128, 4, 128], BF16) for i in range(2)]
    Yb = [ova("Yb%d" % i, [128, 4, 128], BF16) for i in range(2)]
    kcs = ovat("kcs", ov_offs["Lp0"][0], [128, 2, 512], BF16)
    vcs = ovat("vcs", ov_offs["Bpw0"][0], [128, 4, 384], BF16)
    union("kcs", ["Lp0", "Lp1"])
    union("vcs", ["Bpw0", "Bpw1", "Yb0"])
    pT1 = ovat("pT", ov_offs["Yb1"][0], [128, 512], BF16)
    union("Yb1", ["pT"])
    ktil = ova("ktil", [128, 4, 64], BF16)
    Y32 = ova("Y32", [128, 4, 128])
    kf32 = ovat("kf32", ov_offs["Y32"][0], [128, 2, 128])
    vf32 = ovat("vf32", ov_offs["Y32"][0] + 256, [128, 256])
    union("Y32", ["kf32", "vf32"])
    tmpu = ova("tmpu", [128, 256])
    spd = ova("spd", [128, 128])
    Gpbf = ova("Gpbf", [128, 256], BF16)
    unc = ova("unc", [128, 4, 256], BF16)
    Gs = ova("Gs", [128, 128])
    Dm = ova("Dm", [128, 128])
    e1 = ova("e1", [128, 128])
    e2 = ova("e2", [128, 128])
    S32 = [ova("S32_%d" % i, [64, 4, 64], parts=64) for i in range(2)]
    Sbf = [ova("Sbf_%d" % i, [64, 4, 64], BF16, parts=64) for i in range(2)]
    qd = ova("qd", [128, 128])
    kd = ova("kd", [128, 128])
    gkl = ova("gkl", [16, 128], BF16, parts=16)
    egl = ova("egl", [128, 4])
    qt = ova("qt", [128, 128], BF16)
    kt = ova("kt", [128, 128], BF16)
    vD = ova("vD", [128, 256], BF16)
    G32 = [ova("G32_%d" % i, [128, 256]) for i in range(2)]
    dbgt = ova("dbgt", [128, 8, 128]) if DBG else None
    lntP = ova("lntP", [128, 128])
    mixer_words = ov_off[0]
    for nm_, k_ in (("hT", 8), ("mixT", 8), ("cacc", 6), ("cext", 6), ("unc", 4)):
        for i_ in range(k_):
            P.ov_res.add("%s%d" % (nm_, i_))
    for i_ in range(5):
        P.ov_res.add("vb1_%d" % i_)
    ov_off[0] = 0
    wup = [ova("wup%d" % i, [128, 8, 512], BF16) for i in range(2)]
    wdn = [ova("wdn%d" % i, [128, 4, D], BF16) for i in range(2)]
    actb = [ova("actb%d" % i, [128, 4, 512], BF16) for i in range(2)]
    relu_t = [ova("relu%d" % i, [128, 512], BF16) for i in range(2)]
    yout = [ova("yout%d" % i, [128, 128]) for i in range(2)]
    mlp_words = ov_off[0]
    for i_ in range(2):
        for k_ in range(4):
            P.ov_res.add("wup%d_%d" % (i_, k_))
            P.ov_res.add("wdn%d_%d" % (i_, k_))
    print("overlay words: mixer", mixer_words, "mlp", mlp_words, "of", OVW)

    banks = [stack.enter_context(nc.psum_tensor("pb%d" % i, [128, 512], F32)) for i in range(8)]
    bk_i = [0]

    bank_sess = {}

    held = set()
    cur_pool = [None]
    pool_i = {}

    def bank(hold=False):
        pool = cur_pool[0]
        while True:
            if pool is None:
                i = bk_i[0] % 8
                bk_i[0] += 1
            else:
                k_ = pool_i.get(pool, 0)
                pool_i[pool] = k_ + 1
                i = pool[k_ % len(pool)]
            if ("pb%d" % i) not in held:
                break
        if hold:
            held.add("pb%d" % i)
        bank_sess["pb%d" % i] = set()
        return banks[i], "pb%d" % i

    def mm(out, lhsT, rhs, rd, wr, start=True, stop=True):
        st = bank_sess[wr[0]]
        base = out.base_partition()
        quads = set(range(base // 32, (base + out.shape[0] + 31) // 32))
        newq = quads - st
        if newq:
            assert newq == quads, ("mixed psum quadrants", wr, base, out.shape)
            s_ = True
            st |= quads
        else:
            s_ = False
        P.add("pe", lambda e: e.matmul(out, lhsT, rhs, start=s_, stop=True, skip_group_check=True), rd, wr)

    def act(out, in_, func, rd, wr, bias=None, scale=None):
        kw = {}
        if bias is not None:
            kw["bias"] = bias
        if scale is not None:
            kw["scale"] = scale
        P.add("act", lambda e: e.activation(out, in_, func, **kw), rd, wr)

    def tt(out, in0, in1, op, rd, wr, eng="dve"):
        P.add(eng, lambda e: e.tensor_tensor(out=out, in0=in0, in1=in1, op=op), rd, wr)

    def ts(out, in0, s1, op0, rd, wr, s2=None, op1=None, eng="dve"):
        if op1 is None:
            P.add(eng, lambda e: e.tensor_scalar(out=out, in0=in0, scalar1=s1, scalar2=None, op0=op0), rd, wr)
        else:
            P.add(eng, lambda e: e.tensor_scalar(out=out, in0=in0, scalar1=s1, scalar2=s2, op0=op0, op1=op1), rd, wr)

    def stt(out, in0, scalar, in1, op0, op1, rd, wr):
        P.add("dve", lambda e: e.scalar_tensor_tensor(out=out, in0=in0, scalar=scalar, in1=in1, op0=op0, op1=op1), rd, wr)

    def cpy(out, in_, rd, wr, eng="dve"):
        P.add(eng, lambda e: e.tensor_copy(out, in_), rd, wr)

    def mset(ap, val, wr, eng="dve"):
        P.add(eng, lambda e: e.memset(ap, val), (), wr)

    def dma(out, in_, rd, wr, eng="sp"):
        P.add(eng, lambda e: e.dma_start(out=out, in_=in_), rd, wr, dma=True)

    def scan(out, d0, d1, rd, wr):
        P.add("dve", lambda e: e.tensor_tensor_scan(out=out, data0=d0, data1=d1, initial=0.0, op0=ALU.mult, op1=ALU.add), rd, wr)

    def recip(out, in_, rd, wr):
        P.add("dve", lambda e: e.reciprocal(out, in_), rd, wr)

    IDENT, ONESN, BLK1, BLK64, ONES = 0, 1, 2, 3, 8
    VOFF = [0, 64, 192, 256]
    VCOL = [0, 128, 192, 320]

    def ck(n):
        if STOP is not None and STOP == n:
            raise _Stop()

    try:
        for kc in range(8):
            dma(xT[:, kc, :], xT_d[kc * 128:(kc + 1) * 128, :], (), ["x%d" % kc])
        XR = ["x%d" % k for k in range(8)]
        dma(gvec[:], gvec_d, (), ["gvec"])
        dma(pcol[:], pcol_d, (), ["pcol"])
        ts(pcol[:, :, 5], pcol[:, :, 4], -1.0, ALU.mult, ["pcol"], ["pcol"])
        dma(convw[:], convw_d, (), ["convw"])
        dma(hrow[:], hrow_d, (), ["hrow"])
        dma(cstf[:], cst_d[:, 4:13, :], (), ["cstf"])
        dma(cm[:], cm_d, (), ["cm"])
        dma(invc[:], invc_d, (), ["invc"])
        dma(cstb[:], cst_d[:, 0:4, :], (), ["cstb"], eng="pool")
        dma(wgk[:], wgk_d, (), ["wgk"], eng="pool")

        def load_w_in(l):
            for kc in range(8):
                dma(w_in[:, kc, :], w_in_d[l, kc * 128:(kc + 1) * 128, :], (), ["w_in"], eng="pool")

        def load_w_out(l):
            for kc in range(8):
                dma(w_out[:, kc, :], w_out_d[l, kc * 128:(kc + 1) * 128, :], (), ["w_out"], eng="pool")

        load_w_in(0)
        load_w_out(0)

        def rms_stats(cols, n, xres, lbuf=None, lres="lnt"):
            if lbuf is None:
                lbuf = lnt
            bkt, bkr = bank()
            for kc in range(8):
                s = sq[kc % 2]
                act(s[:, :n], xT[:, kc, cols], AF.Square, [xres[kc]], ["sq%d" % (kc % 2)])
                mm(bkt[:, :n], cstb[:, ONESN, :], s[:, :n], ["sq%d" % (kc % 2), "cstb"], [bkr], start=(kc == 0), stop=(kc == 7))
            act(lbuf[:, :n], bkt[:, :n], AF.Ln, [bkr], [lres], bias=EPS)
            act(rstd[:, :n], lbuf[:, :n], AF.Exp, [lres], ["rstd"], scale=-0.5)

        TIL = [(0, 512), (512, 512), (1024, 512), (1536, 512), (2048, 128)]

        for l in range(NL):
            dma(bp[:], bp_d[l], (), ["bp"], eng="pool")
            dma(bsc[:], bsc_d[l], (), ["bsc"], eng="pool")
            dma(bsn[:], bsn_d[l], (), ["bsn"], eng="pool")
            dma(poolw[:], poolw_d[l].rearrange("k p c -> p k c"), (), ["poolw"], eng="pool")
            act(negA[:], hrow[:, l, 0:4], AF.Exp, ["hrow"], ["negA"])
            ts(negA[:], negA[:], -1.0, ALU.mult, ["negA"], ["negA"])
            mset(vb1[:], 1.0, ["vb1_%d" % s_ for s_ in range(5)])
            mset(S32[0][:], 0.0, ["S32_0"])
            mset(Sbf[0][:], 0.0, ["Sbf_0"])
            mset(G32[0][:], 0.0, ["G32_0"])

            def h2_block(bb2):
                cols2 = slice(bb2 * 128, bb2 * 128 + 128)
                rms_stats(cols2, 128, XR, lntP, "lntP")
                for kc in range(8):
                    stt(h2T[:, kc, cols2], xT[:, kc, cols2], gvec[:, 32 + l * 8 + kc:32 + l * 8 + kc + 1], rstd[:, :128],
                        ALU.mult, ALU.mult, [XR[kc], "rstd", "gvec"], ["w_in"])

            for b in range(NBLK):
                smp = (b == 16)
                ty = 1 if smp else 0
                nch = 4 if smp else 2
                cs = 32 if smp else 64
                cols = slice(b * 128, (b + 1) * 128)
                TRI, SAME, MSLN, RST = 0 + ty, 2 + ty, 4 + ty, 6 + ty
                xres = XR
                CX = ["cext%d" % g_ for g_ in range(6)]

                if smp:
                    uxs = uext[:, :, 0:188].rearrange("p g (s t) -> p g s t", s=4)
                    cxs = cext[:, :, 0:140].rearrange("p g (s t) -> p g s t", s=4)
                slot = b % 5
                need_kv = smp or b >= 12

                def proj(c0, M):
                    bkt, bkr = bank()
                    for kc in range(8):
                        mm(bkt[0:M, 0:128], w_in[:, kc, c0:c0 + M], hT[:, kc, :], ["w_in", "hT%d" % kc], [bkr],
                           start=(kc == 0), stop=(kc == 7))
                    return bkt, bkr

                def front_early(bq):
                    smq = (bq == 16)
                    colq = slice(bq * 128, (bq + 1) * 128)
                    slq = bq % 5
                    nkv = smq or bq >= 12
                    rms_stats(colq, 128, XR, lntP, "lntP")
                    for kc in range(8):
                        stt(hT[:, kc, :], xT[:, kc, colq], gvec[:, l * 8 + kc:l * 8 + kc + 1], rstd[:, :128],
                            ALU.mult, ALU.mult, [XR[kc], "rstd", "gvec"], ["hT%d" % kc])
                    if smq:
                        for g in range(2):
                            dma(uxs[:, g, :, 0:15], cpool_d[l, :, g * 128:(g + 1) * 128, :].rearrange("s p t -> p s t"), (), ["uext"])
                        for g in range(6):
                            dma(cxs[:, g, :, 0:3], sconv_d[l, :, g * 128:(g + 1) * 128, :].rearrange("s p t -> p s t"), (), ["cext%d" % g])
                    elif bq == 0:
                        mset(uext[:, :, 0:15], 0.0, ["uext"])
                        mset(cext[:, :, 0:3], 0.0, CX)
                    else:
                        cpy(uext[:, :, 0:15], uext[:, :, 128:143], ["uext"], ["uext"])
                        cpy(cext[:, :, 0:3], cext[:, :, 128:131], CX, CX)
                    for g in range(2):
                        bkt, bkr = proj(g * 128, 128)
                        if smq:
                            act(uxs[:, g, :, 15:47], bkt[:, 0:128].rearrange("p (s t) -> p s t", s=4), AF.Copy, [bkr], ["uext"])
                        else:
                            act(uext[:, g, 15:143], bkt[:, 0:128], AF.Copy, [bkr], ["uext"])
                    for g in range(2):
                        bkt, bkr = proj(256 + g * 128, 128)
                        act(qb[:, g, :], bkt[:, 0:128], AF.Copy, [bkr], ["qb"], scale=0.125)
                    for g in range(2):
                        bkt, bkr = proj(512 + g * 128, 128)
                        act(kbT[:, g, slq * 128:(slq + 1) * 128], bkt[:, 0:128], AF.Copy, [bkr], ["kbT%d" % slq])
                    for g in range(6):
                        bkt, bkr = proj(1024 + g * 128, 128)
                        if smq:
                            cpy(cxs[:, g, :, 3:35], bkt[:, 0:128].rearrange("p (s t) -> p s t", s=4), [bkr], ["cext%d" % g])
                        else:
                            cpy(cext[:, g, 3:131], bkt[:, 0:128], [bkr], ["cext%d" % g])
                    bkt, bkr = proj(2056, 128)
                    act(qd[:], bkt[:, 0:128], AF.Copy, [bkr], ["qd"], scale=float(32 ** -0.5))
                    bkt, bkr = proj(2184, 128)
                    cpy(kd[:], bkt[:, 0:128], [bkr], ["kd"])
                    bkt, bkr = proj(2824, 16)
                    cpy(gkl[:, :], bkt[0:16, 0:128], [bkr], ["gkl"])
                    for g in range(6):
                        for j in range(4):
                            if smq:
                                src = cxs[:, g, :, j:j + 32]
                                dst = cacc[:, g, :].rearrange("p (s t) -> p s t", s=4)
                            else:
                                src = cext[:, g, j:j + 128]
                                dst = cacc[:, g, :]
                            if j == 0:
                                ts(dst, src, convw[:, l, g, 0:1], ALU.mult, ["cext%d" % g, "convw"], ["cacc%d" % g])
                            else:
                                stt(dst, src, convw[:, l, g, j:j + 1], dst, ALU.mult, ALU.add, ["cext%d" % g, "convw", "cacc%d" % g], ["cacc%d" % g])

                def front_late(bq):
                    slq = bq % 5
                    nkv = (bq == 16) or bq >= 12
                    if nkv:
                        for g in range(2):
                            bkt, bkr = proj(512 + g * 128, 128)
                            act(kf32[:, g, :], bkt[:, 0:128], AF.Copy, [bkr], ["kf32"])
                    for g in range(2):
                        bkt, bkr = proj(1792 + g * 128, 128)
                        act(zs[:, g, :], bkt[:, 0:128], AF.Silu, [bkr], ["zs"])
                    for g in range(2):
                        bkt, bkr = proj(2568 + g * 128, 128)
                        act(gs[:, g, :], bkt[:, 0:128], AF.Silu, [bkr], ["gs"])
                    bkt, bkr = bank()
                    bk2, bk2r = bank()
                    for kc in range(8):
                        mm(bkt[:, 0:256], hT[:, kc, :], w_in[:, kc, 768:1024], ["w_in", "hT%d" % kc], [bkr], start=(kc == 0), stop=(kc == 7))
                        mm(bkt[:, 256:512], hT[:, kc, :], w_in[:, kc, 2312:2568], ["w_in", "hT%d" % kc], [bkr], start=(kc == 0), stop=(kc == 7))
                        mm(bk2[:, 0:8], hT[:, kc, :], w_in[:, kc, 2048:2056], ["w_in", "hT%d" % kc], [bk2r], start=(kc == 0), stop=(kc == 7))
                    for h in range(4):
                        cpy(vb1[:, slq, VCOL[h]:VCOL[h] + 64], bkt[:, 64 * h:64 * h + 64], [bkr], ["vb1_%d" % slq])
                    if nkv:
                        act(vf32[:], bkt[:, 0:256], AF.Copy, [bkr], ["vf32"])
                    act(vD[:], bkt[:, 256:512], AF.Copy, [bkr], ["vD"])
                    cpy(ab[:], bk2[:, 0:8], [bk2r], ["ab"])

                def w_out_part(k0, k1):
                    for m in range(8):
                        bkt, bkr = bank()
                        for kc in range(k0, k1):
                            mm(bkt[:, 0:128], w_out[:, kc, m * 128:(m + 1) * 128], mixT[:, kc, :], ["w_out", "mixT%d" % kc], [bkr],
                               start=(kc == k0), stop=(kc == k1 - 1))
                        tt(xT[:, m, cols], xT[:, m, cols], bkt[:, 0:128], ALU.add, ["x%d" % m, bkr], ["x%d" % m])

                prefetched = (1 <= b <= 15)
                if not prefetched:
                    front_early(b)
                do_fab = (1 <= b <= 15)
                listF, listO, listA, listB = [], [], [], []
                P.capture = listF
                cur_pool[0] = (0, 1, 2) if do_fab else None
                front_late(b)
                P.capture = listO

                ck(100 * b + 2)
                if need_kv:
                    c0 = 512 if smp else (b - 12) * 128
                    for g in range(2):
                        dma(k_o[l, g * 128:(g + 1) * 128, c0:c0 + 128], kf32[:, g, :], ["kf32"], ())
                    dma(v_o[l, c0:c0 + 128, :], vf32[:], ["vf32"], ())
                if smp:
                    for g in range(2):
                        dma(pool_o[l, 1:5, g * 128:(g + 1) * 128, :].rearrange("s p t -> p s t"), uxs[:, g, :, 32:47], ["uext"], ())
                    for g in range(6):
                        dma(conv_o[l, 1:5, g * 128:(g + 1) * 128, :].rearrange("s p t -> p s t"), cxs[:, g, :, 32:35], ["cext%d" % g], ())
                elif b == 15:
                    for g in range(2):
                        dma(pool_o[l, 0, g * 128:(g + 1) * 128, :], uext[:, g, 128:143], ["uext"], ())
                    for g in range(6):
                        dma(conv_o[l, 0, g * 128:(g + 1) * 128, :], cext[:, g, 128:131], ["cext%d" % g], ())

                P.capture = listA
                cur_pool[0] = (3,) if do_fab else None
                if smp:
                    def V(t, a, bb):
                        return t[:, :, 0:188].rearrange("p g (s t) -> p g s t", s=4)[:, :, :, a:bb]
                    L_ = 47
                else:
                    def V(t, a, bb):
                        return t[:, :, a:bb]
                    L_ = 143
                wins = [(uext, pwa, 1), (pwa, pwb, 2), (pwb, pwa, 4), (pwa, pwb, 8)]
                for gi, (src, dst, sh) in enumerate(wins):
                    lo = 2 * sh - 1
                    dres = "pwa" if dst is pwa else "pwb"
                    tt(V(dst, lo, L_), V(src, lo, L_), V(src, lo - sh, L_ - sh), ALU.add,
                       ["uext", "pwa", "pwb"], [dres])
                    kc, p0 = gi // 2, 64 * (gi % 2)
                    win = 2 * sh
                    if smp:
                        o_ = pooled[p0:p0 + 64, kc, :].rearrange("p (s t) -> p s t", s=4)
                        i0 = V(dst, 15, 47)[p0:p0 + 64, kc]
                        i1 = V(uext, 15, 47)[p0:p0 + 64, kc]
                    else:
                        o_ = pooled[p0:p0 + 64, kc, :]
                        i0 = dst[p0:p0 + 64, kc, 15:143]
                        i1 = uext[p0:p0 + 64, kc, 15:143]
                    if b == 0:
                        tt(tS[p0:p0 + 64, 0:128], i0, invc[p0:p0 + 64, kc, :], ALU.mult, ["invc", dres], ["tS"])
                        tt(o_, tS[p0:p0 + 64, 0:128], i1, ALU.subtract, ["uext", "tS"], ["pooled"])
                    else:
                        stt(o_, i0, 1.0 / win, i1, ALU.mult, ALU.subtract, ["uext", dres], ["pooled"])
                for kc in range(2):
                    bkt, bkr = bank()
                    mm(bkt[:, 0:128], poolw[:, kc, :], pooled[:, kc, :], ["poolw", "pooled"], [bkr])
                    ts(mixT[:, kc, :], bkt[:, 0:128], pcol[:, l, kc:kc + 1], ALU.mult, [bkr, "pcol"], ["mixT%d" % kc])

                P.capture = listB
                cur_pool[0] = (4, 5, 6, 7) if do_fab else None
                ob, obr = bank(hold=True)
                if not smp:
                    rlist = [r for r in range(5) if b - 4 + r >= 0]
                    for ri, r in enumerate(rlist):
                        ks = (b - 4 + r) % 5
                        scp = [bank(), bank()]
                        for h in range(4):
                            p0, g = 64 * (h % 2), h // 2
                            sc, scr = scp[h % 2]
                            mm(sc[:, 128 * g:128 * g + 128], kbT[p0:p0 + 64, g, ks * 128:(ks + 1) * 128], qb[p0:p0 + 64, g, :],
                               ["kbT%d" % ks, "qb"], [scr])
                            mm(sc[:, 128 * g:128 * g + 128], cstb[:, IDENT, :], bp[:, r, 128 * h:128 * h + 128], ["cstb", "bp"], [scr])
                        i2 = ri % 2
                        for par in range(2):
                            sc, scr = scp[par]
                            act(pT1[:].rearrange("p (h q) -> p h q", h=4)[:, par::2, :], sc[:, 0:256].rearrange("p (h q) -> p h q", h=2),
                                AF.Exp, [scr], ["pT"])
                        for h in range(4):
                            mm(ob[:, 128 * h:128 * h + 128], vb1[:, ks, VOFF[h]:VOFF[h] + 128], pT1[:, 128 * h:128 * h + 128],
                               ["vb1_%d" % ks, "pT"], [obr], start=(ri == 0), stop=(ri == len(rlist) - 1))
                    ck(100 * b + 44)
                    for h in range(4):
                        po, ps_ = (0, 64) if h % 2 == 0 else (64, 0)
                        recip(rc[po:po + 64, 128 * h:128 * h + 128], ob[ps_:ps_ + 64, 128 * h:128 * h + 128], [obr], ["tS"])
                        ck(100 * b + 45)
                        tt(mixT[po:po + 64, 2 + h // 2, :], ob[po:po + 64, 128 * h:128 * h + 128], rc[po:po + 64, 128 * h:128 * h + 128],
                           ALU.mult, [obr, "tS"], ["mixT%d" % (2 + h // 2)])
                else:
                    for s in range(4):
                        dma(kcs[:], ck_d[l, s].rearrange("(g p) t -> p g t", p=128), (), ["kcs"], eng="pool")
                        mset(vcs[:], 1.0, ["vcs"])
                        for h in range(4):
                            dma(vcs[:, :, VCOL[h]:VCOL[h] + 64], cv_d[l, s, h].rearrange("(r p) d -> p r d", p=128), (), ["vcs"], eng="pool")
                        for r in range(5):
                            scp = [bank(), bank()]
                            for h in range(4):
                                p0, g = 64 * (h % 2), h // 2
                                sc, scr = scp[h % 2]
                                if r < 4:
                                    lhsT = kcs[p0:p0 + 64, g, r * 128:(r + 1) * 128]
                                    rdk = "kcs"
                                else:
                                    lhsT = kbT[p0:p0 + 64, g, slot * 128:(slot + 1) * 128]
                                    rdk = "kbT%d" % slot
                                mm(sc[:, 32 * g:32 * g + 32], lhsT, qb[p0:p0 + 64, g, 32 * s:32 * s + 32], [rdk, "qb"], [scr])
                                tab = bsc[:, r, :] if r < 4 else bsn[:, s, :]
                                mm(sc[:, 32 * g:32 * g + 32], cstb[:, IDENT, :], tab[:, 32 * h:32 * h + 32], ["cstb", "bsc", "bsn"], [scr])
                            i2 = r % 2
                            for par in range(2):
                                sc, scr = scp[par]
                                act(pT1[:, 0:128].rearrange("p (h q) -> p h q", h=4)[:, par::2, :], sc[:, 0:64].rearrange("p (h q) -> p h q", h=2),
                                    AF.Exp, [scr], ["pT"])
                            for h in range(4):
                                lhsT = vcs[:, r, VOFF[h]:VOFF[h] + 128] if r < 4 else vb1[:, slot, VOFF[h]:VOFF[h] + 128]
                                mm(ob[:, 128 * s + 32 * h:128 * s + 32 * h + 32], lhsT, pT1[:, 32 * h:32 * h + 32],
                                   ["vcs", "vb1_%d" % slot, "pT"], [obr], start=(r == 0), stop=(r == 4))
                    obv = ob[:].rearrange("p (s h q) -> p s h q", s=4, h=4)
                    rcv = rc[:].rearrange("p (s h q) -> p s h q", s=4, h=4)
                    for h in range(4):
                        po, ps_ = (0, 64) if h % 2 == 0 else (64, 0)
                        recip(rcv[po:po + 64, :, h, :], obv[ps_:ps_ + 64, :, h, :], [obr], ["tS"])
                        tt(mixT[po:po + 64, 2 + h // 2, :].rearrange("p (s q) -> p s q", s=4), obv[po:po + 64, :, h, :],
                           rcv[po:po + 64, :, h, :], ALU.mult, [obr, "tS"], ["mixT%d" % (2 + h // 2)])

                held.discard(obr)
                P.capture = None
                cur_pool[0] = None
                if do_fab:
                    keyed = []
                    for lst, off_, sc_ in ((listF, 0.0, 0.5), (listA, 0.0, 1.0), (listB, 0.0, 1.0)):
                        n_ = len(lst)
                        for i_, op_ in enumerate(lst):
                            keyed.append((off_ + sc_ * (i_ + 0.5) / n_, len(keyed), op_))
                    keyed.sort(key=lambda t_: (t_[0], t_[1]))
                    for _, _, op_ in keyed:
                        P.add(*op_)
                    for op_ in listO:
                        P.add(*op_)
                else:
                    for lst in (listF, listO, listA, listB):
                        for op_ in lst:
                            P.add(*op_)
                listC = []
                P.capture = listC
                cur_pool[0] = (0, 1, 2, 3)
                tyC, nchC, csC = 1, 4, 32
                TRIc, SAMEc, MSLNc = 0 + tyC, 2 + tyC, 4 + tyC
                act(vcT[:], cacc[:, 4:6, :], AF.Silu, ["cacc4", "cacc5"], ["vcT"])
                csl = cacc[:, 0:4, :]
                CQ = ["cacc0", "cacc1", "cacc2", "cacc3"]
                act(csl, csl, AF.Silu, CQ, CQ)
                act(sqc[:], csl, AF.Square, CQ, ["sqc"])
                bkt, bkr = bank()
                for g in range(4):
                    mm(bkt[:, 128 * g:128 * g + 128], cstb[:, BLK1, :], sqc[:, g, :], ["cstb", "sqc"], [bkr])
                act(lnt[:], bkt[:], AF.Ln, [bkr], ["lnt"], bias=EPS)
                act(lnt[:], lnt[:], AF.Exp, ["lnt"], ["lnt"], scale=-0.5)
                rn = lnt[:].rearrange("p (g t) -> p g t", g=4)
                stt(qnT[:], csl[:, 0:2, :], 0.125, rn[:, 0:2, :], ALU.mult, ALU.mult, ["cacc0", "cacc1", "lnt"], ["qnT"])
                tt(knT[:], csl[:, 2:4, :], rn[:, 2:4, :], ALU.mult, ["cacc2", "cacc3", "lnt"], ["knT"])
                btr, btrr = bank()
                for g in range(2):
                    mm(btr[:, 128 * g:128 * g + 128], vcT[:, g, :], cstb[:, IDENT, :], ["vcT", "cstb"], [btrr])
                    mm(btr[:, 256 + 128 * g:256 + 128 * g + 128], knT[:, g, :], cstb[:, IDENT, :], ["knT", "cstb"], [btrr])
                tt(sm[:, 0:4], ab[:, 0:4], hrow[:, l, 4:8], ALU.add, ["ab", "hrow"], ["sm"])
                act(sm[:, 4:8], sm[:, 0:4], AF.Exp, ["sm"], ["sm"])
                act(sm[:, 8:12], sm[:, 4:8], AF.Ln, ["sm"], ["sm"], bias=1.0)
                tt(sm[:, 12:16], sm[:, 8:12], negA[:], ALU.mult, ["sm", "negA"], ["sm"])
                act(sm[:, 16:20], ab[:, 4:8], AF.Exp, ["ab"], ["sm"], scale=-1.0)
                ts(sm[:, 16:20], sm[:, 16:20], 1.0, ALU.add, ["sm"], ["sm"])
                recip(sm[:, 20:24], sm[:, 16:20], ["sm"], ["sm"])
                tt(gm[:, 0:4 * nchC].rearrange("p (c h) -> p c h", c=nchC), bc(sm[:, 12:16].unsqueeze(1), [128, nchC, 4]),
                   bc(cm[:, tyC, 0:nchC].unsqueeze(2), [128, nchC, 4]), ALU.mult, ["sm", "cm"], ["gm"])
                bsm, bsmr = bank()
                mm(bsm[:, 0:4], cstf[:, TRIc, :], sm[:, 12:16], ["cstf", "sm"], [bsmr])
                mm(bsm[:, 4:8], cstf[:, SAMEc, :], sm[:, 12:16], ["cstf", "sm"], [bsmr])
                mm(bsm[0:64, 8:8 + 4 * nchC], cstf[:, ONES, 0:64], gm[:, 0:4 * nchC], ["cstf", "gm"], [bsmr])
                bgr, bgrr = bank()
                for h in range(4):
                    mm(bgr[:, 128 * h:128 * h + 128], bc(sm[:, 12 + h:13 + h], [128, 128]), cstf[:, TRIc, :], ["cstf", "sm"], [bgrr])
                cpy(sm[:, 24:32], bsm[:, 0:8], [bsmr], ["sm"])
                cpy(glr[:, 0:4 * nchC], bsm[0:64, 8:8 + 4 * nchC], [bsmr], ["glr"])
                act(eglr[:, 0:4 * nchC], glr[:, 0:4 * nchC], AF.Exp, ["glr"], ["eglr"])
                act(sm[:, 32:36], sm[:, 24:28], AF.Exp, ["sm"], ["sm"])
                tt(sm[:, 44:48], sm[:, 28:32], sm[:, 24:28], ALU.subtract, ["sm"], ["sm"])
                act(sm[:, 36:40], sm[:, 44:48], AF.Exp, ["sm"], ["sm"])
                tt(sm[:, 40:44], sm[:, 20:24], sm[:, 32:36], ALU.mult, ["sm"], ["sm"])
                act(Em[:].rearrange("p h t -> p (h t)"), bgr[:], AF.Exp, [bgrr], ["Em"])
                for h in range(4):
                    p0, g = 64 * (h % 2), h // 2
                    tt(qgT[:, h, :], qnT[p0:p0 + 64, g, :], Em[p0:p0 + 64, h, :], ALU.mult, ["qnT", "Em"], ["qgT"])
                tt(Y32[:, :, 0:64], btr[:, 0:256].rearrange("p (h d) -> p h d", h=4), bc(sm[:, 20:24].unsqueeze(2), [128, 4, 64]),
                   ALU.mult, [btrr, "sm"], ["Y32"])
                tt(Y32[:, :, 64:128], btr[:, 256:512].rearrange("p (h d) -> p h d", h=4), bc(sm[:, 40:44].unsqueeze(2), [128, 4, 64]),
                   ALU.mult, [btrr, "sm"], ["Y32"])
                cpy(Yb[0][:], Y32[:], ["Y32"], ["Yb0"])
                tt(ktil[:], btr[:, 256:512].rearrange("p (h d) -> p h d", h=4), bc(sm[:, 36:40].unsqueeze(2), [128, 4, 64]),
                   ALU.mult, [btrr, "sm"], ["ktil"])
                gram = [bank(), bank()]
                for h in range(4):
                    p0, g = 64 * (h % 2), h // 2
                    gb, gbr = gram[h % 2]
                    mm(gb[:, 128 * g:128 * g + 128], knT[p0:p0 + 64, g, :], knT[p0:p0 + 64, g, :], ["knT"], [gbr])
                    mm(gb[:, 256 + 128 * g:256 + 128 * g + 128], knT[p0:p0 + 64, g, :], qnT[p0:p0 + 64, g, :], ["knT", "qnT"], [gbr])
                for h in range(4):
                    ts(Ee[:, h, :], bgr[:, 128 * h:128 * h + 128], sm[:, 24 + h:25 + h], ALU.subtract, [bgrr, "sm"], ["Ee"])
                stt(Ee[:], Ee[:], -1.0, Ee[:], ALU.mult, ALU.max, ["Ee"], ["Ee"])
                act(Ee[:], Ee[:], AF.Exp, ["Ee"], ["Ee"], scale=-1.0)
                tt(Em[:], Ee[:], bc(cstf[:, MSLNc, :].unsqueeze(1), [128, 4, 128]), ALU.mult, ["Ee", "cstf", "qgT"], ["Em"])
                for h in range(4):
                    stt(Lp[0][:, h, :], gram[h % 2][0][:, 128 * (h // 2):128 * (h // 2) + 128], sm[:, 20 + h:21 + h], Em[:, h, :],
                        ALU.mult, ALU.mult, [gram[h % 2][1], "sm", "Em"], ["Lp0"])
                tt(Em[:], Ee[:], bc(cstf[:, TRIc, :].unsqueeze(1), [128, 4, 128]), ALU.mult, ["Ee", "cstf"], ["Em"])
                for par in range(2):
                    tt(QKm[:, par::2, :], gram[par][0][:, 256:512].rearrange("p (h t) -> p h t", h=2), Em[:, par::2, :], ALU.mult,
                       [gram[par][1], "Em"], ["QKm"])
                bb_, bbr = bank()
                for h in range(4):
                    mm(bb_[:, 128 * h:128 * h + 128], Lp[0][:, h, :], cstb[:, IDENT, :], ["Lp0", "cstb"], [bbr])
                act(Bpw[0][:].rearrange("p h t -> p (h t)"), bb_[:], AF.Copy, [bbr], ["Bpw0"])
                NLV = 5
                for k in range(NLV):
                    ci, ni = k % 2, (k + 1) % 2
                    by, byr = bank()
                    for h in range(4):
                        mm(by[:, 128 * h:128 * h + 128], Bpw[ci][:, h, :], Yb[ci][:, h, :], ["Bpw%d" % ci, "Yb%d" % ci], [byr])
                    tt(Y32[:].rearrange("p h t -> p (h t)"), Y32[:].rearrange("p h t -> p (h t)"), by[:], ALU.add, ["Y32", byr], ["Y32"])
                    cpy(Yb[ni][:], Y32[:], ["Y32"], ["Yb%d" % ni])
                    if k < NLV - 1:
                        b2, b2r = bank()
                        for h in range(4):
                            mm(b2[:, 128 * h:128 * h + 128], Lp[ci][:, h, :], Bpw[ci][:, h, :], ["Lp%d" % ci, "Bpw%d" % ci], [b2r])
                        act(Bpw[ni][:].rearrange("p h t -> p (h t)"), b2[:], AF.Copy, [b2r], ["Bpw%d" % ni])
                        if k < NLV - 2:
                            l2, l2r = bank()
                            for h in range(4):
                                mm(l2[:, 128 * h:128 * h + 128], Bpw[ci][:, h, :], Lp[ci][:, h, :], ["Lp%d" % ci, "Bpw%d" % ci], [l2r])
                            act(Lp[ni][:].rearrange("p h t -> p (h t)"), l2[:], AF.Copy, [l2r], ["Lp%d" % ni])
                yfin = NLV % 2
                Yf, Yfr = Yb[yfin], "Yb%d" % yfin
                bw, bwr = bank()
                for h in range(4):
                    mm(bw[0:64, 128 * h:128 * h + 128], Yf[:, h, 64:128], cstb[:, IDENT, :], [Yfr, "cstb"], [bwr])
                act(wT[:].rearrange("p h t -> p (h t)"), bw[0:64, :], AF.Copy, [bwr], ["wT"])
                bo, bor = bank(hold=True)
                for c in range(nchC):
                    si = (c % 2) if smp else 0
                    sr, sbr = "S32_%d" % si, "Sbf_%d" % si
                    if smp:
                        dma(S32[si][:], sdel_d[l, c], (), [sr])
                        cpy(Sbf[si][:], S32[si][:], [sr], [sbr])
                    bws, bwsr = bank()
                    for h in range(4):
                        mm(bws[:, 64 * h:64 * h + 64], wT[:, h, :], Sbf[si][:, h, :], ["wT", sbr], [bwsr])
                    tt(tmpu[:].rearrange("p (h d) -> p h d", h=4), Y32[:, :, 0:64], bws[:, 0:256].rearrange("p (h d) -> p h d", h=4),
                       ALU.subtract, ["Y32", bwsr], ["tmpu"])
                    ts(unc[:, c, :], tmpu[:], cm[:, tyC, c:c + 1], ALU.mult, ["tmpu", "cm"], ["unc%d" % c])
                    for h in range(4):
                        p0, g = 64 * (h % 2), h // 2
                        mm(bo[p0:p0 + 64, 128 * g + c * csC:128 * g + (c + 1) * csC], Sbf[si][:, h, :], qgT[:, h, c * csC:(c + 1) * csC],
                           [sbr, "qgT"], [bor], start=True, stop=False)
                    bds, bdsr = bank()
                    for h in range(4):
                        mm(bds[0:64, 64 * h:64 * h + 64], ktil[:, h, :], unc[:, c, 64 * h:64 * h + 64], ["ktil", "unc%d" % c], [bdsr])
                    tt(Sdec[:], S32[si][:], bc(eglr[:, 4 * c:4 * c + 4].unsqueeze(2), [64, 4, 64]), ALU.mult, [sr, "eglr"], ["Sdec"])
                    tt(Sbf[si][:], Sdec[:], bds[0:64, 0:256].rearrange("p (h d) -> p h d", h=4), ALU.add, ["Sdec", bdsr], [sbr])
                    tt(S32[si][:], Sdec[:], bds[0:64, 0:256].rearrange("p (h d) -> p h d", h=4), ALU.add, ["Sdec", bdsr], [sr])
                    for h in range(4):
                        p0, g = 64 * (h % 2), h // 2
                        mm(bo[p0:p0 + 64, 128 * g + c * csC:128 * g + (c + 1) * csC], unc[:, c, 64 * h:64 * h + 64],
                           QKm[:, h, c * csC:(c + 1) * csC], ["unc%d" % c, "QKm"], [bor], start=False, stop=True)
                    if smp:
                        dma(del_o[l, 1 + c], S32[si][:], [sr], ())
                if b == 15:
                    dma(del_o[l, 0], S32[0][:], ["S32_0"], ())

                def out_norm(bo_, bor_, gcol, gate, gres, kbase, sq_=None, sqr="sqo", tn_=None, tnr="tno", ln_=None, lnr="lnt"):
                    sq_ = sqo if sq_ is None else sq_
                    tn_ = tno if tn_ is None else tn_
                    ln_ = lnt[:, 0:256] if ln_ is None else ln_[:]
                    act(sq_[:], bo_[:, 0:256], AF.Square, [bor_], [sqr])
                    bn, bnr = bank()
                    for g in range(2):
                        mm(bn[:, 128 * g:128 * g + 128], cstb[:, BLK64, :], sq_[:, 128 * g:128 * g + 128], ["cstb", sqr], [bnr])
                    act(ln_, bn[:, 0:256], AF.Ln, [bnr], [lnr], bias=EPS)
                    act(ln_, ln_, AF.Exp, [lnr], [lnr], scale=-0.5)
                    stt(tn_[:], bo_[:, 0:256], pcol[:, l, gcol:gcol + 1], ln_, ALU.mult, ALU.mult, [bor_, "pcol", lnr], [tnr])
                    tt(mixT[:, kbase:kbase + 2, :], tn_[:].rearrange("p (g t) -> p g t", g=2), gate[:], ALU.mult, [tnr, gres], ["mixT%d" % kbase, "mixT%d" % (kbase + 1)])

                out_norm(bo, bor, 2, zs, "zs", 4)
                held.discard(bor)
                listD = []
                P.capture = listD
                cur_pool[0] = (4, 5)

                ck(100 * b + 6)
                bd, bdr = bank()
                mm(bd[:, 0:128], wgk[:, l, :], gkl[:, :], ["wgk", "gkl"], [bdr])
                act(e1[:], bd[:, 0:128], AF.Exp, [bdr, "pcol"], ["e1"], bias=pcol[:, l, 5:6], scale=-1.0)
                act(spd[:], e1[:], AF.Ln, ["e1"], ["spd"], bias=1.0)
                scan(Gs[:], cstf[:, RST, :], spd[:], ["cstf", "spd"], ["Gs"])
                Gv = Gs[:].rearrange("p (c t) -> p c t", c=nch)
                tt(Dm[:].rearrange("p (c t) -> p c t", c=nch), Gv, bc(Gv[:, :, cs - 1:cs], [128, nch, cs]), ALU.subtract, ["Gs"], ["Dm"])
                act(e1[:], Dm[:], AF.Exp, ["Dm"], ["e1"], scale=-1.0 / 16)
                act(e2[:], Dm[:], AF.Exp, ["Dm"], ["e2"], scale=1.0 / 16)
                act(egl[:, 0:nch], Gv[:, :, cs - 1], AF.Exp, ["Gs"], ["egl"], scale=-1.0 / 16)
                tt(qt[:], qd[:], e1[:], ALU.mult, ["qd", "e1"], ["qt"])
                tt(kt[:], kd[:], e2[:], ALU.mult, ["kd", "e2"], ["kt"])
                for h in range(4):
                    ts(qm[:, h, :], qt[:], cm[:, ty, 8 + h:9 + h], ALU.mult, ["qt", "cm"], ["qm"])
                bt2, bt2r = bank()
                mm(bt2[:, 0:128], kt[:], cstb[:, IDENT, :], ["kt", "cstb"], [bt2r])
                for c in range(nch):
                    ts(ktc[:, c, :], bt2[:, 0:128], cm[:, ty, c:c + 1], ALU.mult, [bt2r, "cm"], ["ktc"])
                bat, batr = bank()
                for h in range(4):
                    mm(bat[:, 128 * h:128 * h + 128], kt[:], qm[:, h, :], ["kt", "qm"], [batr])
                tt(attm[:], bat[:].rearrange("p (h t) -> p h t", h=4), bc(cstf[:, TRI, :].unsqueeze(1), [128, 4, 128]), ALU.mult,
                   [batr, "cstf"], ["attm"])
                bod, bodr = bank(hold=True)
                for c in range(nch):
                    si = (c % 2) if smp else 0
                    gr = "G32_%d" % si
                    if smp:
                        mset(G32[si][:], 0.0, [gr])
                        for h in range(4):
                            dma(G32[si][32 * h:32 * h + 32, 64 * h:64 * h + 64], sgla_d[l, c, h], (), [gr])
                    bx, bxr = bank()
                    mm(bx[:, 0:256], ktc[:, c, :], vD[:], ["ktc", "vD"], [bxr])
                    ts(Gp32[:], G32[si][:], egl[:, c:c + 1], ALU.mult, [gr, "egl"], ["Gp32"])
                    cpy(Gpbf[:], Gp32[:], ["Gp32"], ["Gpbf"])
                    for h in range(4):
                        p0, g = 64 * (h % 2), h // 2
                        osl = bod[p0:p0 + 64, 128 * g + c * cs:128 * g + (c + 1) * cs]
                        mm(osl, Gpbf[:, 64 * h:64 * h + 64], qm[:, h, c * cs:(c + 1) * cs], ["Gpbf", "qm"], [bodr], start=True, stop=False)
                        mm(osl, vD[:, 64 * h:64 * h + 64], attm[:, h, c * cs:(c + 1) * cs], ["vD", "attm"], [bodr], start=False, stop=True)
                    tt(G32[si][:], Gp32[:], bx[:, 0:256], ALU.add, ["Gp32", bxr], [gr])
                    if smp:
                        for h in range(4):
                            dma(gla_o[l, 1 + c, h], G32[si][32 * h:32 * h + 32, 64 * h:64 * h + 64], [gr], ())
                if b == 15:
                    for h in range(4):
                        dma(gla_o[l, 0, h], G32[0][32 * h:32 * h + 32, 64 * h:64 * h + 64], ["G32_0"], ())
                out_norm(bod, bodr, 3, gs, "gs", 6, sqoD, "sqoD", tnoD, "tnoD", lntD, "lntD")
                held.discard(bodr)
                P.capture = None
                cur_pool[0] = None
                listP = []
                if 1 <= b + 1 <= 15:
                    P.capture = listP
                    cur_pool[0] = (6, 7)
                    front_early(b + 1)
                    P.capture = None
                    cur_pool[0] = None
                elif b == 16:
                    P.capture = listP
                    cur_pool[0] = (6, 7)
                    for bb2 in range(16):
                        h2_block(bb2)
                    P.capture = None
                    cur_pool[0] = None
                P.capture = listP
                cur_pool[0] = (6, 7)
                w_out_part(0, 4)
                P.capture = None
                cur_pool[0] = None
                keyed = []
                for lst, off_, sc_ in ((listC, 0.0, 1.0), (listD, 0.0, 1.0), (listP, 0.0, 1.0)):
                    n_ = len(lst)
                    for i_, op_ in enumerate(lst):
                        keyed.append((off_ + sc_ * (i_ + 0.5) / n_, len(keyed), op_))
                keyed.sort(key=lambda t_: (t_[0], t_[1]))
                for _, _, op_ in keyed:
                    P.add(*op_)

                if DBG and l == 0 and b == DBGB:
                    dl = [("vD", vD[:], 256), ("qt", qt[:], 128), ("kt", kt[:], 128), ("qm", qm[:].rearrange("p h t -> p (h t)"), 512),
                          ("ktc", ktc[:, 0:2, :].rearrange("p h t -> p (h t)"), 256), ("attm", attm[:].rearrange("p h t -> p (h t)"), 512),
                          ("Gs", Gs[:], 128), ("e1", e1[:], 128), ("e2", e2[:], 128), ("egl", egl[:, 0:2], 2), ("G32", G32[0][:], 256),
                          ("Gp32", Gp32[:], 256), ("Gpbf", Gpbf[:], 256), ("qd", qd[:], 128), ("kd", kd[:], 128), ("gs", gs[:].rearrange("p h t -> p (h t)"), 256)]
                    dflat = dbgt[:].rearrange("p a b -> p (a b)")
                    for di_, (nm_, ap_, n_) in enumerate(dl):
                        cpy(dflat[:, 0:n_], ap_, [nm_], ["dbgt"])
                        dma(dbg2_o[di_, :, 0:n_], dflat[:, 0:n_], ["dbgt"], ())
                if DBG and l == 0 and b in (DBGB, 16):
                    cpy(dbgt[:], mixT[:], ["mixT%d" % k_ for k_ in range(8)], ["dbgt"])
                    dma(dbg_o[0 if b == DBGB else 1], dbgt[:], ["dbgt"], ())
                ck(100 * b + 7)
                w_out_part(4, 8)

            ck(5000)
            P.phase_switch()
            if l + 1 < NL:
                load_w_out(l + 1)
            h2_block(16)
            ai = 0
            for j in range(8):
                wb = j % 2
                for fm in range(4):
                    dma(wup[wb][:, :, fm * 128:(fm + 1) * 128],
                        w_up_d[l, :, j * 512 + fm * 128:j * 512 + (fm + 1) * 128].rearrange("(kc p) f -> p kc f", p=128),
                        (), ["wup%d_%d" % (wb, fm)], eng="pool")
                for fc in range(4):
                    dma(wdn[wb][:, fc, :], w_dn_d[l, j * 512 + fc * 128:j * 512 + (fc + 1) * 128, :], (), ["wdn%d_%d" % (wb, fc)], eng="pool")
                for (t0, n) in TIL:
                    cols = slice(t0, t0 + n)
                    a_ = ai % 2
                    ai += 1
                    for fm in range(4):
                        bkt, bkr = bank()
                        for kc in range(8):
                            mm(bkt[:, :n], wup[wb][:, kc, fm * 128:(fm + 1) * 128], h2T[:, kc, cols], ["wup%d_%d" % (wb, fm), "w_in"], [bkr],
                               start=(kc == 0), stop=(kc == 7))
                        r_ = fm % 2
                        act(relu_t[r_][:, :n], bkt[:, :n], AF.Relu, [bkr], ["relu%d" % r_])
                        tt(actb[a_][:, fm, :n], relu_t[r_][:, :n], relu_t[r_][:, :n], ALU.mult, ["relu%d" % r_], ["actb%d" % a_])
                    for m in range(8):
                        bkt, bkr = bank()
                        for fc in range(4):
                            mm(bkt[:, :n], wdn[wb][:, fc, m * 128:(m + 1) * 128], actb[a_][:, fc, :n], ["wdn%d_%d" % (wb, fc), "actb%d" % a_], [bkr],
                               start=(fc == 0), stop=(fc == 3))
                        tt(xT[:, m, cols], xT[:, m, cols], bkt[:, :n], ALU.add, ["x%d" % m, bkr], ["x%d" % m])
            P.phase_switch()
            if l + 1 < NL:
                load_w_in(l + 1)

        ck(6000)
        for bb2 in range(NBLK):
            cols = slice(bb2 * 128, bb2 * 128 + 128)
            rms_stats(cols, 128, XR)
            for kc in range(8):
                yo = yout[kc % 2]
                stt(yo[:], xT[:, kc, cols], gvec[:, 64 + kc:65 + kc], rstd[:, :128], ALU.mult, ALU.mult,
                    [XR[kc], "rstd", "gvec"], ["yout%d" % (kc % 2)])
                dma(yT_o[kc * 128:(kc + 1) * 128, cols], yo[:], ["yout%d" % (kc % 2)], ())

    except _Stop:
        pass

    P.emit(nc, stack)
    stack.close()
    return nc


def _consts():
    cst = np.zeros((128, 13, 128), np.float32)
    cst[:, 12] = 1.0
    i = np.arange(128)
    cst[:, 0] = np.eye(128)
    cst[:, 1] = 1.0 / 1024
    blk = (i[:, None] // 64 == i[None, :] // 64).astype(np.float32)
    cst[:, 2] = blk
    cst[:, 3] = blk / 64
    for ty, cs in ((0, 64), (1, 32)):
        same = (i[:, None] // cs == i[None, :] // cs)
        cst[:, 4 + ty] = (same & (i[:, None] <= i[None, :]))
        cst[:, 6 + ty] = same
        cst[:, 8 + ty] = -(same & (i[:, None] > i[None, :])).astype(np.float32)
        cst[:, 10 + ty] = np.broadcast_to((i % cs != 0).astype(np.float32)[None, :], (128, 128))
    cm = np.zeros((128, 2, 16), np.float32)
    for ty, cs in ((0, 64), (1, 32)):
        for c in range(128 // cs):
            cm[:, ty, c] = (i // cs == c)
            cm[:, ty, 4 + c] = -cm[:, ty, c]
        for h in range(4):
            cm[:, ty, 8 + h] = (i // 32 == h)
    invc = np.zeros((128, 2, 128), np.float32)
    t = np.arange(128)
    for g, win in enumerate((2, 4, 8, 16)):
        kc, p0 = g // 2, 64 * (g % 2)
        invc[p0:p0 + 64, kc, :] = 1.0 / np.minimum(win, t + 1)[None, :]
    return cst, cm, invc


def _colT(v):
    return v.reshape(v.shape[:-1] + (8, 128))


def kernel(x_prompt, x_sample, cache_pool, cache_attn_k, cache_attn_v, state_conv, state_delta,
           state_gla, attn_norm_g, w_in, pool_w, pool_scale, rel_bias, conv_w, a_log, dt_bias,
           delta_norm_g, gla_w_gk, gla_b_gk, gla_norm_g, w_out, mlp_norm_g, w_up, w_down,
           final_norm_g, _NL=DEPTH, _DBG=False, _CORES=NCORES, _STOP=None):
    f = np.float32
    cst, cm, invc = _consts()
    gvec = np.zeros((128, 72), f)
    gvec[:, 0:32] = attn_norm_g.reshape(4, 8, 128).transpose(2, 0, 1).reshape(128, 32)
    gvec[:, 32:64] = mlp_norm_g.reshape(4, 8, 128).transpose(2, 0, 1).reshape(128, 32)
    gvec[:, 64:72] = final_norm_g.reshape(8, 128).T
    pcol = np.zeros((128, 4, 8), f)
    pcol[:, :, 0:2] = pool_scale.reshape(4, 2, 128).transpose(2, 0, 1)
    pcol[:, :, 2] = np.concatenate([delta_norm_g, delta_norm_g], 1).T
    pcol[:, :, 3] = np.concatenate([gla_norm_g, gla_norm_g], 1).T
    pcol[:, :, 4] = gla_b_gk.T
    convw = np.ascontiguousarray(conv_w.reshape(4, 4, 6, 128).transpose(3, 0, 2, 1)).astype(f)
    hrow = np.zeros((128, 4, 8), f)
    hrow[:, :, 0:4] = a_log[None]
    hrow[:, :, 4:8] = dt_bias[None]
    poolw = np.zeros((4, 2, 128, 128), f)
    for g in range(4):
        kc, p0 = g // 2, 64 * (g % 2)
        poolw[:, kc, p0:p0 + 64, p0:p0 + 64] = pool_w[:, g]
    wgk = np.ascontiguousarray(gla_w_gk.transpose(1, 0, 2)).astype(f)
    NEG = f(-100.0)
    k = np.arange(128)[:, None]
    q = np.arange(128)[None, :]
    bp = np.zeros((4, 128, 5, 4, 128), f)
    for r in range(5):
        idx = np.clip((r - 4) * 128 + k - q, -256, 256) + 256
        tab = rel_bias[:, :, idx]
        tab = tab.transpose(0, 2, 1, 3).copy()
        if r == 0:
            tab[:, 0:64, :, 64:128] = NEG
        if r == 4:
            tab[:, 64:128, :, 0:64] = NEG
        bp[:, :, r] = tab
    bp = bp.reshape(4, 128, 5, 512)
    q32 = np.arange(32)[None, :]
    bsc = np.zeros((4, 128, 4, 4, 32), f)
    for r in range(4):
        idx = np.clip(r * 128 + k - 512 - q32, -256, 256) + 256
        bsc[:, :, r] = rel_bias[:, :, idx].transpose(0, 2, 1, 3)
    bsc = bsc.reshape(4, 128, 4, 128)
    bsn = np.full((4, 128, 4, 4, 32), NEG, f)
    kk = np.arange(32)[:, None]
    idx = np.clip(kk - q32, -256, 256) + 256
    tabn = rel_bias[:, :, idx].transpose(0, 2, 1, 3)
    for s in range(4):
        bsn[:, 32 * s:32 * s + 32, s] = tabn
    bsn = bsn.reshape(4, 128, 4, 128)

    shared = dict(w_in=np.ascontiguousarray(w_in, f), w_out=np.ascontiguousarray(w_out, f),
                  w_up=np.ascontiguousarray(w_up, f), w_down=np.ascontiguousarray(w_down, f),
                  gvec=gvec, pcol=pcol, convw=convw, hrow=hrow, poolw=poolw, wgk=wgk, bp=bp, bsc=bsc, bsn=bsn,
                  cst=cst, cm=cm, invc=invc)
    in_maps = []
    for c in range(NCORES):
        sl = slice(4 * c, 4 * c + 4)
        xs = np.concatenate([x_prompt[c], x_sample[sl].reshape(128, D)], 0)
        m = dict(shared)
        m["xT"] = np.ascontiguousarray(xs.T, f)
        m["cpool"] = np.ascontiguousarray(cache_pool[:, sl].transpose(0, 1, 3, 2), f)
        m["ck"] = np.ascontiguousarray(cache_attn_k[:, sl].transpose(0, 1, 2, 4, 3).reshape(4, 4, 256, 512), f)
        m["cv"] = np.ascontiguousarray(cache_attn_v[:, sl], f)
        m["sconv"] = np.ascontiguousarray(state_conv[:, sl].transpose(0, 1, 3, 2), f)
        m["sdel"] = np.ascontiguousarray(state_delta[:, sl].transpose(0, 1, 3, 2, 4), f)
        m["sgla"] = np.ascontiguousarray(state_gla[:, sl], f)
        in_maps.append(m)

    nc = build(_NL, _DBG, _STOP)
    res = run_bass_kernel_spmd(nc, in_maps[:_CORES], core_ids=list(range(_CORES))).results
    res = list(res) + [res[0]] * (NCORES - _CORES)
    global _last_res
    _last_res = res

    y_prompt = np.zeros((8, 2048, D), f)
    y_sample = np.zeros((32, 32, D), f)
    pool_p = np.zeros((4, 8, 15, 256), f); pool_s = np.zeros((4, 32, 15, 256), f)
    k_p = np.zeros((4, 8, 4, 512, 64), f); v_p = np.zeros((4, 8, 4, 512, 64), f)
    k_s = np.zeros((4, 32, 4, 32, 64), f); v_s = np.zeros((4, 32, 4, 32, 64), f)
    conv_p = np.zeros((4, 8, 3, 768), f); conv_s = np.zeros((4, 32, 3, 768), f)
    delta_p = np.zeros((4, 8, 4, 64, 64), f); delta_s = np.zeros((4, 32, 4, 64, 64), f)
    gla_p = np.zeros((4, 8, 4, 32, 64), f); gla_s = np.zeros((4, 32, 4, 32, 64), f)
    for c in range(NCORES):
        r = res[c]
        sl = slice(4 * c, 4 * c + 4)
        yt = r["yT"].T
        y_prompt[c] = yt[:2048]
        y_sample[sl] = yt[2048:].reshape(4, 32, D)
        po = r["pool_o"].transpose(0, 1, 3, 2)
        pool_p[:, c] = po[:, 0]; pool_s[:, sl] = po[:, 1:]
        ko = r["k_o"]
        k_p[:, c] = ko[:, :, :512].reshape(4, 4, 64, 512).transpose(0, 1, 3, 2)
        k_s[:, sl] = ko[:, :, 512:].reshape(4, 4, 64, 4, 32).transpose(0, 3, 1, 4, 2)
        vo = r["v_o"]
        v_p[:, c] = vo[:, :512].reshape(4, 512, 4, 64).transpose(0, 2, 1, 3)
        v_s[:, sl] = vo[:, 512:].reshape(4, 4, 32, 4, 64).transpose(0, 1, 3, 2, 4)
        co = r["conv_o"].transpose(0, 1, 3, 2)
        conv_p[:, c] = co[:, 0]; conv_s[:, sl] = co[:, 1:]
        do = r["del_o"].transpose(0, 1, 3, 2, 4)
        delta_p[:, c] = do[:, 0]; delta_s[:, sl] = do[:, 1:]
        go = r["gla_o"]
        gla_p[:, c] = go[:, 0]; gla_s[:, sl] = go[:, 1:]
    return (y_prompt, y_sample, pool_p, k_p, v_p, conv_p, delta_p, gla_p,
            pool_s, k_s, v_s, conv_s, delta_s, gla_s)
```
